# Optimizing a Trainium2 kernel written in Bass

```python
import jax, jax.numpy as jnp
from jax import lax
import numpy as np

D_MODEL = 1024
BATCH = 8
SEQ = 8192
DEPTH = 1
DEC_BATCH = 16
DEC_SEQ = 16
PAST_LEN = 1024

CHUNK = 64
WINDOW = 128
WINDOW_CHUNKS = WINDOW // CHUNK
N_HEADS = 8
N_KV_HEADS = 2
HEAD_DIM = 128
GROUP = N_HEADS // N_KV_HEADS
ATTN_SCALE = HEAD_DIM ** -0.5
D_RNN = D_MODEL
N_RNN_BLOCKS = 8
RNN_BLOCK = D_RNN // N_RNN_BLOCKS
RNN_CONV = 4
RGLRU_C = 8.0
D_FF = ((8 * D_MODEL // 3 + 127) // 128) * 128
FFN_CONV = 3
EPS = 1e-6
Q_W = N_HEADS * HEAD_DIM
KV_W = N_KV_HEADS * HEAD_DIM
IN_W = D_RNN + Q_W + 2 * KV_W + 2 * D_MODEL
SPLITS = [D_RNN, D_RNN + Q_W, D_RNN + Q_W + KV_W, D_RNN + Q_W + 2 * KV_W]

kernel_name = 'hawk_swa_sink_convffn_stream_step'


def rms_norm(x, g):
    xf = x.astype(jnp.float32)
    r = lax.rsqrt(jnp.mean(xf * xf, axis=-1, keepdims=True) + EPS)
    return (xf * r * g.astype(jnp.float32)).astype(x.dtype)


def alibi_slopes():
    return jnp.asarray(np.array([2.0 ** (-8.0 * (h + 1) / N_HEADS) for h in range(N_HEADS)], dtype=np.float32))


def causal_dwconv(x, prev, w, b):
    width = w.shape[0]
    T = x.shape[1]
    xf = jnp.concatenate([prev.astype(x.dtype), x], axis=1)
    y = xf[:, 0:T] * w[0]
    for j in range(1, width):
        y = y + xf[:, j:j + T] * w[j]
    y = y + b
    return y, xf[:, xf.shape[1] - (width - 1):]


def linear_combine(c1, c2):
    a1, b1 = c1
    a2, b2 = c2
    return a1 * a2, a2 * b1 + b2


def rglru(u, h_prev, w_a, b_a, w_x, b_x, lam):
    B, T, _ = u.shape
    ub = u.reshape(B, T, N_RNN_BLOCKS, RNN_BLOCK)
    r = jax.nn.sigmoid((jnp.einsum('btnc,ncd->btnd', ub, w_a).reshape(B, T, D_RNN) + b_a).astype(jnp.float32))
    i = jax.nn.sigmoid((jnp.einsum('btnc,ncd->btnd', ub, w_x).reshape(B, T, D_RNN) + b_x).astype(jnp.float32))
    log_a = -RGLRU_C * r * jax.nn.softplus(-lam.astype(jnp.float32))
    a = jnp.exp(log_a)
    b = jnp.sqrt(-jnp.expm1(2.0 * log_a)) * (i * u.astype(jnp.float32))
    b = b.at[:, 0].add(a[:, 0] * h_prev.astype(jnp.float32))
    _, h = lax.associative_scan(linear_combine, (a, b), axis=1)
    return h.astype(u.dtype), h[:, -1].astype(h_prev.dtype)


def sink_alibi_attention(q, k, v, q_pos, k_pos, k_valid, sinks, slopes):
    B, N, Lq = q.shape[:3]
    qg = q.reshape(B, N, Lq, N_KV_HEADS, GROUP, HEAD_DIM)
    s = jnp.einsum('bnqkgd,bnskd->bnkgqs', qg, k).astype(jnp.float32) * ATTN_SCALE
    dist = jnp.abs(q_pos[:, :, None] - k_pos[:, None, :]).astype(jnp.float32)
    m = slopes.reshape(N_KV_HEADS, GROUP)
    s = s - m[None, None, :, :, None, None] * dist[None, :, None, None, :, :]
    s = jnp.where(k_valid[None, :, None, None, None, :], s, -jnp.inf)
    sink = sinks.astype(jnp.float32).reshape(N_KV_HEADS, GROUP)[None, None, :, :, None, None]
    mx = jnp.maximum(jnp.max(s, axis=-1, keepdims=True), sink)
    p = jnp.exp(s - mx)
    denom = jnp.sum(p, axis=-1, keepdims=True) + jnp.exp(sink - mx)
    p = (p / denom).astype(v.dtype)
    o = jnp.einsum('bnkgqs,bnskd->bnqkgd', p, v)
    return o.reshape(B, N, Lq, N_HEADS * HEAD_DIM)


def banded_window_attention(q, k, v, sinks, slopes):
    B, T = q.shape[:2]
    n_c = T // CHUNK
    pad = WINDOW_CHUNKS * CHUNK
    span = (WINDOW_CHUNKS + 1) * CHUNK

    def blocks(t):
        tp = jnp.pad(t, ((0, 0), (pad, 0), (0, 0), (0, 0))).reshape(B, n_c + WINDOW_CHUNKS, CHUNK, N_KV_HEADS, HEAD_DIM)
        return jnp.concatenate([tp[:, j:j + n_c] for j in range(WINDOW_CHUNKS + 1)], axis=2)

    qb = q.reshape(B, n_c, CHUNK, N_HEADS, HEAD_DIM)
    q_pos = jnp.arange(T, dtype=jnp.int32).reshape(n_c, CHUNK)
    k_pos = jnp.arange(n_c, dtype=jnp.int32)[:, None] * CHUNK - pad + jnp.arange(span, dtype=jnp.int32)[None, :]
    o = sink_alibi_attention(qb, blocks(k), blocks(v), q_pos, k_pos, k_pos >= 0, sinks, slopes)
    return o.reshape(B, T, N_HEADS * HEAD_DIM)


def cached_window_attention(q, k, v, k_cache, v_cache, sinks, slopes):
    S = q.shape[1]
    n_past = k_cache.shape[1]
    k_all = jnp.concatenate([k_cache.astype(k.dtype), k], axis=1)
    v_all = jnp.concatenate([v_cache.astype(v.dtype), v], axis=1)
    q_pos = (PAST_LEN + jnp.arange(S, dtype=jnp.int32))[None]
    k_pos = (PAST_LEN - n_past + jnp.arange(n_past + S, dtype=jnp.int32))[None]
    valid = jnp.ones(k_pos.shape, dtype=bool)
    o = sink_alibi_attention(q[:, None], k_all[:, None], v_all[:, None], q_pos, k_pos, valid, sinks, slopes)
    L = k_all.shape[1]
    return o[:, 0], k_all[:, L - n_past:], v_all[:, L - n_past:]


def hybrid_layer(x, p, slopes, rnn_conv_prev, rnn_h_prev, ffn_conv_prev, k_cache, v_cache):
    B, T, _ = x.shape
    h = rms_norm(x, p['norm_mix_g'])
    u, q, k, v, gl = jnp.split(h @ p['w_in'], SPLITS, axis=-1)
    u, new_rnn_conv = causal_dwconv(u, rnn_conv_prev, p['rnn_conv_w'], p['rnn_conv_b'])
    y_rnn, new_h = rglru(u, rnn_h_prev, p['rnn_gate_a_w'], p['rnn_gate_a_b'], p['rnn_gate_x_w'], p['rnn_gate_x_b'], p['rnn_lambda'])
    q = rms_norm(q.reshape(B, T, N_HEADS, HEAD_DIM), p['q_norm_g'])
    k = rms_norm(k.reshape(B, T, N_KV_HEADS, HEAD_DIM), p['k_norm_g'])
    v = v.reshape(B, T, N_KV_HEADS, HEAD_DIM)
    if k_cache is None:
        y_attn = banded_window_attention(q, k, v, p['attn_sinks'], slopes)
        new_k, new_v = k[:, T - WINDOW:], v[:, T - WINDOW:]
    else:
        y_attn, new_k, new_v = cached_window_attention(q, k, v, k_cache, v_cache, p['attn_sinks'], slopes)
    g_rnn, g_attn = jnp.split(jax.nn.sigmoid(gl + p['b_gate']), 2, axis=-1)
    mixed = g_rnn * (y_rnn @ p['w_rnn_proj']) + g_attn * (y_attn @ p['w_attn_proj'])
    x = x + mixed @ p['w_out']
    h2 = rms_norm(x, p['norm_ffn_g'])
    a, b = jnp.split(h2 @ p['w_up'], 2, axis=-1)
    a, new_ffn_conv = causal_dwconv(a, ffn_conv_prev, p['ffn_conv_w'], p['ffn_conv_b'])
    x = x + (jax.nn.gelu(a, approximate=False) * b) @ p['w_down']
    return x, (new_rnn_conv, new_h, new_k, new_v, new_ffn_conv)


def setup_inputs(seed: int = 0) -> dict:
    key = jax.random.key(seed)
    ks = iter(jax.random.split(key, 40))

    def nrm(shape, scale):
        return jax.random.normal(next(ks), shape, jnp.float32) * scale

    a0 = jax.random.uniform(next(ks), (DEPTH, D_RNN), jnp.float32, minval=0.9, maxval=0.999)
    return {
        'x_prompt': nrm((BATCH, SEQ, D_MODEL), 1.0),
        'x_sample': nrm((DEC_BATCH, DEC_SEQ, D_MODEL), 1.0),
        'state_rnn_conv': nrm((DEPTH, DEC_BATCH, RNN_CONV - 1, D_RNN), 1.0),
        'state_rnn_h': nrm((DEPTH, DEC_BATCH, D_RNN), 0.5),
        'cache_attn_k': nrm((DEPTH, DEC_BATCH, min(WINDOW, PAST_LEN), N_KV_HEADS, HEAD_DIM), 1.0),
        'cache_attn_v': nrm((DEPTH, DEC_BATCH, min(WINDOW, PAST_LEN), N_KV_HEADS, HEAD_DIM), 1.0),
        'state_ffn_conv': nrm((DEPTH, DEC_BATCH, FFN_CONV - 1, D_FF), 1.0),
        'norm_mix_g': 1.0 + nrm((DEPTH, D_MODEL), 0.02),
        'w_in': nrm((DEPTH, D_MODEL, IN_W), D_MODEL ** -0.5),
        'b_gate': nrm((DEPTH, 2 * D_MODEL), 0.02),
        'rnn_conv_w': nrm((DEPTH, RNN_CONV, D_RNN), RNN_CONV ** -0.5),
        'rnn_conv_b': nrm((DEPTH, D_RNN), 0.02),
        'rnn_gate_a_w': nrm((DEPTH, N_RNN_BLOCKS, RNN_BLOCK, RNN_BLOCK), RNN_BLOCK ** -0.5),
        'rnn_gate_a_b': nrm((DEPTH, D_RNN), 0.02),
        'rnn_gate_x_w': nrm((DEPTH, N_RNN_BLOCKS, RNN_BLOCK, RNN_BLOCK), RNN_BLOCK ** -0.5),
        'rnn_gate_x_b': nrm((DEPTH, D_RNN), 0.02),
        'rnn_lambda': jnp.log(a0) - jnp.log1p(-a0),
        'q_norm_g': 1.0 + nrm((DEPTH, HEAD_DIM), 0.02),
        'k_norm_g': 1.0 + nrm((DEPTH, HEAD_DIM), 0.02),
        'attn_sinks': nrm((DEPTH, N_HEADS), 0.5),
        'w_rnn_proj': nrm((DEPTH, D_RNN, D_MODEL), D_RNN ** -0.5),
        'w_attn_proj': nrm((DEPTH, Q_W, D_MODEL), Q_W ** -0.5),
        'w_out': nrm((DEPTH, D_MODEL, D_MODEL), D_MODEL ** -0.5),
        'norm_ffn_g': 1.0 + nrm((DEPTH, D_MODEL), 0.02),
        'w_up': nrm((DEPTH, D_MODEL, 2 * D_FF), D_MODEL ** -0.5),
        'ffn_conv_w': nrm((DEPTH, FFN_CONV, D_FF), FFN_CONV ** -0.5),
        'ffn_conv_b': nrm((DEPTH, D_FF), 0.02),
        'w_down': nrm((DEPTH, D_FF, D_MODEL), D_FF ** -0.5),
    }


def reference(x_prompt, x_sample, state_rnn_conv, state_rnn_h, cache_attn_k, cache_attn_v, state_ffn_conv,
              norm_mix_g, w_in, b_gate, rnn_conv_w, rnn_conv_b, rnn_gate_a_w, rnn_gate_a_b, rnn_gate_x_w,
              rnn_gate_x_b, rnn_lambda, q_norm_g, k_norm_g, attn_sinks, w_rnn_proj, w_attn_proj, w_out,
              norm_ffn_g, w_up, ffn_conv_w, ffn_conv_b, w_down):
    B = x_prompt.shape[0]
    dt = x_prompt.dtype
    slopes = alibi_slopes()
    xp, xs = x_prompt, x_sample
    st_p = ([], [], [], [], [])
    st_s = ([], [], [], [], [])
    for l in range(DEPTH):
        p = {
            'norm_mix_g': norm_mix_g[l], 'w_in': w_in[l], 'b_gate': b_gate[l],
            'rnn_conv_w': rnn_conv_w[l], 'rnn_conv_b': rnn_conv_b[l],
            'rnn_gate_a_w': rnn_gate_a_w[l], 'rnn_gate_a_b': rnn_gate_a_b[l],
            'rnn_gate_x_w': rnn_gate_x_w[l], 'rnn_gate_x_b': rnn_gate_x_b[l],
            'rnn_lambda': rnn_lambda[l], 'q_norm_g': q_norm_g[l], 'k_norm_g': k_norm_g[l],
            'attn_sinks': attn_sinks[l], 'w_rnn_proj': w_rnn_proj[l], 'w_attn_proj': w_attn_proj[l],
            'w_out': w_out[l], 'norm_ffn_g': norm_ffn_g[l], 'w_up': w_up[l],
            'ffn_conv_w': ffn_conv_w[l], 'ffn_conv_b': ffn_conv_b[l], 'w_down': w_down[l],
        }
        xp, sp = hybrid_layer(
            xp, p, slopes,
            jnp.zeros((B, RNN_CONV - 1, D_RNN), dt),
            jnp.zeros((B, D_RNN), dt),
            jnp.zeros((B, FFN_CONV - 1, D_FF), dt),
            None, None)
        xs, ss = hybrid_layer(
            xs, p, slopes, state_rnn_conv[l], state_rnn_h[l], state_ffn_conv[l],
            cache_attn_k[l], cache_attn_v[l])
        for j in range(5):
            st_p[j].append(sp[j])
            st_s[j].append(ss[j])
    return (xp, xs,
            jnp.stack(st_p[0]), jnp.stack(st_s[0]),
            jnp.stack(st_p[1]), jnp.stack(st_s[1]),
            jnp.stack(st_p[2]), jnp.stack(st_s[2]),
            jnp.stack(st_p[3]), jnp.stack(st_s[3]),
            jnp.stack(st_p[4]), jnp.stack(st_s[4]))
```

```python
import contextlib
import numpy as np
import concourse.bass as bass
import concourse.mybir as mybir
from concourse.bass_utils import run_bass_kernel_spmd
from concourse.alu_op_type import AluOpType as ALU

F32 = mybir.dt.float32
BF16 = mybir.dt.bfloat16
AF = mybir.ActivationFunctionType


class _Op:
    __slots__ = ("idx", "eng", "fn", "deps", "is_dma", "dsem", "dma_val", "needs_sig", "sigval", "waits")


class _DSem:
    def __init__(self, sem, name, serial=False):
        self.sem = sem
        self.name = name
        self.count = 0
        self.serial = serial
        self.last = None


class Prog:
    def __init__(self, nc):
        self.nc = nc
        self.ops = []
        self.last_writer = {}
        self.readers = {}
        self.stack = contextlib.ExitStack()
        self.eng_sems = {}
        for e in ("pe", "act", "dve", "pool"):
            self.eng_sems[e] = self.stack.enter_context(nc.semaphore("s_" + e))
        self.dsems = []
        self.final = []

    def sbuf(self, name, shape, dtype):
        return self.stack.enter_context(self.nc.sbuf_tensor(name, list(shape), dtype))

    def psum(self, name, shape, dtype):
        return self.stack.enter_context(self.nc.psum_tensor(name, list(shape), dtype))

    def dsem(self, name, serial=False):
        d = _DSem(self.stack.enter_context(self.nc.semaphore("d_" + name)), name, serial)
        self.dsems.append(d)
        return d

    def _mk(self, eng, fn, reads, writes):
        op = _Op()
        op.idx = len(self.ops)
        op.eng = eng
        op.fn = fn
        op.is_dma = False
        op.dsem = None
        op.dma_val = 0
        op.needs_sig = False
        op.sigval = 0
        deps = set()
        for r in reads:
            if r in self.last_writer:
                deps.add(self.last_writer[r])
        for w in writes:
            if w in self.last_writer:
                deps.add(self.last_writer[w])
            deps |= self.readers.get(w, set())
        op.deps = deps
        for r in reads:
            self.readers.setdefault(r, set()).add(op.idx)
        for w in writes:
            self.last_writer[w] = op.idx
            self.readers[w] = set()
        self.ops.append(op)
        return op

    def add(self, eng, fn, reads=(), writes=()):
        return self._mk(eng, fn, reads, writes)

    def fence(self):
        last = {}
        for op in self.ops:
            if op.fn is None:
                continue
            if op.is_dma:
                last[("d", id(op.dsem))] = op.idx
            else:
                last[("e", op.eng)] = op.idx
        deps = set(last.values())
        for eng in ("sp", "act", "dve", "pool", "pe"):
            op = self._mk(eng, None, (), ())
            op.deps = set(deps)

    def dma(self, fn, dsem, reads=(), writes=(), queue="sp"):
        op = self._mk(queue, fn, reads, writes)
        op.is_dma = True
        op.dsem = dsem
        if dsem.serial and dsem.last is not None:
            op.deps.add(dsem.last)
        dsem.last = op.idx
        dsem.count += 1
        op.dma_val = 16 * dsem.count
        return op

    def finish(self, final_dsems=None):
        ops = self.ops
        for op in ops:
            for d in op.deps:
                dop = ops[d]
                if dop.is_dma:
                    continue
                if dop.eng == "pe" and op.eng == "pe" and not op.is_dma:
                    continue
                dop.needs_sig = True
        cnt = {}
        for op in ops:
            if (not op.is_dma) and op.needs_sig:
                cnt[op.eng] = cnt.get(op.eng, 0) + 1
                op.sigval = cnt[op.eng]
        known = {}
        streams = {}
        for op in ops:
            kn = known.setdefault(op.eng, {})
            waits = {}
            for d in op.deps:
                dop = ops[d]
                if dop.is_dma:
                    ch, val = ("d", id(dop.dsem)), dop.dma_val
                    sem = dop.dsem.sem
                else:
                    if dop.eng == "pe" and op.eng == "pe" and not op.is_dma:
                        continue
                    ch, val = ("e", dop.eng), dop.sigval
                    sem = self.eng_sems[dop.eng]
                if kn.get(ch, 0) >= val:
                    continue
                if ch not in waits or waits[ch][1] < val:
                    waits[ch] = (sem, val)
            for ch, (sem, val) in waits.items():
                kn[ch] = val
            op.waits = list(waits.values())
            streams.setdefault(op.eng, []).append(op)
        if final_dsems is None:
            final_dsems = self.dsems
        finals = [(d.sem, 16 * d.count) for d in final_dsems if d.count > 0]
        eng_sems = self.eng_sems
        self.n_ops = {k: len(v) for k, v in streams.items()}

        def emit(name, e, tail=False):
            for op in streams.get(name, []):
                for sem, val in op.waits:
                    e.wait_ge(sem, val)
                if op.fn is None:
                    continue
                ins = op.fn(e)
                if op.is_dma:
                    ins.then_inc(op.dsem.sem, 16)
                elif op.needs_sig:
                    ins.then_inc(eng_sems[name], 1)
            if tail:
                for sem, val in finals:
                    e.wait_ge(sem, val)

        with self.nc.Block() as block:
            @block.sync
            def _(e):
                emit("sp", e, tail=True)

            @block.scalar
            def _(e):
                emit("act", e)

            @block.vector
            def _(e):
                emit("dve", e)

            @block.gpsimd
            def _(e):
                emit("pool", e)

            @block.tensor
            def _(e):
                emit("pe", e)
        self.stack.close()


D = 1024
NH = 8
NKV = 2
HD = 128
DFF = 2816
NFB = DFF // 128
INW = 4608
EPS = 1e-6
ATTN_SCALE = HD ** -0.5
NSLOT = 64
NRING = 6
LN_HALF = float(np.log(0.5))
NEG = -30000.0

P_GM, P_GF, P_CW, P_CB, P_BA, P_BX, P_LAM, P_BG, P_FW, P_FB, P_GQ, P_GK, P_SINK = (
    0, 8, 16, 48, 56, 64, 72, 80, 96, 162, 184, 185, 186)
NPRM = 194
ST_HU, ST_H, ST_FH = 0, 72, 96
NST = 228

SL_U, SL_Q, SL_K, SL_V, SL_M, SL_O, SL_UP, SL_DN = 0, 4, 8, 9, 10, 26, 30, 52


class _Stop(Exception):
    pass


_DBG_STOP = None


def _chk(level):
    if _DBG_STOP is not None and level >= _DBG_STOP:
        raise _Stop()


def build_program(T, n_cores_hint=8):
    assert T % 512 == 0
    NT = T // 512
    nc = bass.Bass("TRN2", target_bir_lowering=False)

    def din(name, shape, dt=F32):
        return nc.dram_tensor(name, list(shape), dt, kind="ExternalInput").ap()

    def dout(name, shape, dt=F32):
        return nc.dram_tensor(name, list(shape), dt, kind="ExternalOutput").ap()

    xp = din("xp", [T, D])
    xs = din("xs", [32, D])
    prm_d = din("prm", [128, NPRM])
    sst_d = din("sst", [128, 152])
    ck_d = din("ck", [2, 128, 256])
    cv_d = din("cv", [2, 128, 256])
    w_in_d = din("w_in", [D, INW])
    ga_d = din("gate_a", [8, 128, 128])
    gx_d = din("gate_x", [8, 128, 128])
    wrp_d = din("w_rnn_proj", [D, D])
    wap_d = din("w_attn_proj", [D, D])
    wout_d = din("w_out", [D, D])
    wup_d = din("w_up", [D, 2 * DFF])
    wdn_d = din("w_down", [DFF, D])

    yp = dout("yp", [T, D])
    ys = dout("ys", [32, D])
    st_o = dout("st_o", [128, NST])
    kp_o = dout("kp", [128, 256])
    vp_o = dout("vp", [128, 256])
    ks_o = dout("ks", [2, 128, 256])
    vs_o = dout("vs", [2, 128, 256])

    wsc = nc.dram_tensor("wsc", [NSLOT, 128, 2048], BF16).ap()

    P = Prog(nc)
    xt = P.sbuf("xt", [128, 2, 4, D], F32)
    ss = P.sbuf("ss", [128, 8], F32)
    rs = P.sbuf("rs", [128, 8], F32)
    xn = P.sbuf("xn", [128, 2, D], BF16)
    hTa = P.sbuf("hTa", [128, 8, 512], BF16)
    hTb = P.sbuf("hTb", [128, 8, 512], BF16)
    tl = P.sbuf("tl", [128, 19, 512], F32)
    tlb = P.sbuf("tlb", [128, 6, 512], BF16)
    yr = P.sbuf("yr", [128, 8, 512], BF16)
    ya = P.sbuf("ya", [128, 8, 512], BF16)
    mixed = P.sbuf("mixed", [128, 8, 512], BF16)
    qT = P.sbuf("qT", [128, 8, 512], BF16)
    kT = P.sbuf("kT", [128, 2, 640], BF16)
    kTc = P.sbuf("kTc", [128, 2, 2, 128], BF16)
    vb = P.sbuf("vb", [128, 5, 2, 128], BF16)
    vc = P.sbuf("vc", [128, 2, 2, 128], BF16)
    btab = P.sbuf("btab", [128, 2, 8, 128], BF16)
    pT = P.sbuf("pT", [128, 4, 512], BF16)
    yat = P.sbuf("yat", [128, 2, D], BF16)
    actb = P.sbuf("actb", [128, NFB, 512], BF16)
    ring = P.sbuf("ring", [128, NRING, 2048], BF16)
    prm = P.sbuf("prm_s", [128, NPRM], F32)
    dv = P.sbuf("dv", [128, 64], F32)
    st = P.sbuf("st", [128, NST], F32)
    gwa = P.sbuf("gwa", [128, 8, 128], BF16)
    gwx = P.sbuf("gwx", [128, 8, 128], BF16)
    identf = P.sbuf("identf", [128, 128], F32)
    identb = P.sbuf("identb", [128, 128], BF16)
    onesb = P.sbuf("onesb", [128, 128], BF16)
    kvo = P.sbuf("kvo", [128, 2, 256], F32)
    kvs = P.sbuf("kvs", [128, 2, 2, 256], F32)
    smal = P.sbuf("smal", [128, 16], F32)
    act32 = actb[:, 0:16, :].rearrange("p a b -> p (a b)").bitcast(F32).rearrange("p (a b) -> p a b", a=8)
    NSTG = 4
    stg_f = [actb[:, 4 * i:4 * i + 4, :].rearrange("p a b -> p (a b)").bitcast(F32) for i in range(NSTG)]
    stg_b = [yr[:, 2 * i:2 * i + 2, :].rearrange("p a b -> p (a b)") for i in range(NSTG)]
    stg_f2 = [actb[:, 4 * i:4 * i + 4, :].rearrange("p a b -> p (a b)").bitcast(F32) for i in range(3)]
    stg_b2 = [actb[:, 12 + 2 * i:14 + 2 * i, :].rearrange("p a b -> p (a b)") for i in range(3)]
    gstg = tl[:, 2:4, :].rearrange("p a b -> p (a b)")

    psA = [P.psum(f"psA{i}", [128, 512], F32) for i in range(4)]
    psB = [P.psum(f"psB{i}", [128, 512], F32) for i in range(2)]
    psC = P.psum("psC", [128, 1024], F32)
    banks8 = [(psA[0], ("psA", 0)), (psA[1], ("psA", 1)), (psA[2], ("psA", 2)), (psA[3], ("psA", 3)),
              (psC[:, 0:512], ("psC", 0)), (psC[:, 512:1024], ("psC", 1)), (psB[0], ("psB", 0)), (psB[1], ("psB", 1))]
    cnt = {"A": 0, "B": 0, "tf": 0, "tb": 0, "pT": 0, "ring": 0, "xn": 0, "yat": 0}

    def bankA():
        i = cnt["A"] % 4
        cnt["A"] += 1
        return psA[i], ("psA", i)

    def bankB():
        i = cnt["B"] % 2
        cnt["B"] += 1
        return psB[i], ("psB", i)

    def tmpf():
        i = cnt["tf"] % 19
        cnt["tf"] += 1
        return tl[:, i, :], ("tl", i)

    def tmpb():
        i = cnt["tb"] % 6
        cnt["tb"] += 1
        return tlb[:, i, :], ("tlb", i)

    d_const = P.dsem("const", serial=True)
    d_x = [P.dsem("x0"), P.dsem("x1")]
    d_ring = [P.dsem(f"ring{i}") for i in range(NRING)]
    d_stg = [P.dsem(f"stg{i}") for i in range(7)]
    d_wst = [P.dsem(f"wst{i}") for i in range(7)]
    d_y = [P.dsem("y0"), P.dsem("y1")]
    d_fin = P.dsem("fin")
    d_cp = P.dsem("cp")

    def prmc(c0, n=1):
        return prm[:, c0:c0 + n]

    P.dma(lambda e: e.dma_start(out=prm[:, :], in_=prm_d[:, :]), d_const, writes=["prm"])
    P.dma(lambda e: e.dma_start(out=st[:, ST_HU + 24:ST_HU + 72], in_=sst_d[:, 0:48]), d_const, writes=["st_hu"])
    P.dma(lambda e: e.dma_start(out=st[:, ST_H + 8:ST_H + 24], in_=sst_d[:, 48:64]), d_const, writes=["st_h"])
    P.dma(lambda e: e.dma_start(out=st[:, ST_FH + 44:ST_FH + 132], in_=sst_d[:, 64:152]), d_const, writes=["st_fh"])
    P.add("dve", lambda e: e.memset(st[:, ST_HU:ST_HU + 24], 0.0), writes=["st_hu"])
    P.add("dve", lambda e: e.memset(st[:, ST_H:ST_H + 8], 0.0), writes=["st_h"])
    P.add("dve", lambda e: e.memset(st[:, ST_FH:ST_FH + 44], 0.0), writes=["st_fh"])
    P.add("pool", lambda e: e.iota(identf[:, :], pattern=[[1, 128]], base=0, channel_multiplier=-1,
                                    allow_small_or_imprecise_dtypes=True), writes=["identf"])
    P.add("dve", lambda e: e.tensor_scalar(out=identf[:, :], in0=identf[:, :], scalar1=0.0, scalar2=None, op0=ALU.is_equal),
          reads=["identf"], writes=["identf"])
    P.add("dve", lambda e: e.tensor_copy(out=identb[:, :], in_=identf[:, :]), reads=["identf"], writes=["identb"])
    P.add("dve", lambda e: e.memset(onesb[:, :], 1.0), writes=["onesb"])
    btf = act32[:, 0:4, :].rearrange("p a b -> p (a b)").rearrange("p (k h q) -> p k h q", k=2, h=8)
    btf2 = act32[:, 4:8, :].rearrange("p a b -> p (a b)").rearrange("p (k h q) -> p k h q", k=2, h=8)
    P.add("pool", lambda e: e.iota(btf, pattern=[[-128, 2], [0, 8], [1, 128]], base=128, channel_multiplier=-1,
                                    allow_small_or_imprecise_dtypes=True), writes=["btf"])
    P.add("dve", lambda e: e.tensor_scalar(out=btf2, in0=btf, scalar1=-1.0, scalar2=None, op0=ALU.mult),
          reads=["btf"], writes=["btf2"])
    P.add("dve", lambda e: e.tensor_tensor(out=btf, in0=btf, in1=btf2, op=ALU.max), reads=["btf", "btf2"], writes=["btf"])
    for h in range(8):
        P.add("dve", lambda e, h=h: e.tensor_scalar(out=btf[:, :, h, :], in0=btf[:, :, h, :], scalar1=-(2.0 ** -(h + 1)),
                                                    scalar2=None, op0=ALU.mult), reads=["btf"], writes=["btf"])
    P.add("dve", lambda e: e.memset(btf[64:128, 1, :, 0:64], NEG), reads=["btf"], writes=["btf"])
    P.add("dve", lambda e: e.memset(btf[0:64, 0, :, 64:128], NEG), reads=["btf"], writes=["btf"])
    P.add("dve", lambda e: e.tensor_copy(out=btab[:, :, :, :], in_=btf), reads=["btf"], writes=["btab"])
    DV_CH, DV_C, DV_NBA, DV_NBX, DV_NBG, DV_GQS, DV_ESK, DV_MH = 0, 8, 16, 24, 32, 48, 49, 57
    DV_CF = DV_C
    P.add("act", lambda e: e.activation(out=dv[:, DV_CH:DV_CH + 8], in_=prmc(P_LAM, 8), func=AF.Exp, scale=-1.0),
          reads=["prm"], writes=["dv_c"])
    P.add("act", lambda e: e.activation(out=dv[:, DV_CH:DV_CH + 8], in_=dv[:, DV_CH:DV_CH + 8], func=AF.Ln, bias=1.0),
          reads=["dv_c"], writes=["dv_c"])
    P.add("dve", lambda e: e.tensor_scalar(out=dv[:, DV_CF:DV_CF + 8], in0=dv[:, DV_CH:DV_CH + 8], scalar1=-8.0, scalar2=None,
                                           op0=ALU.mult), reads=["dv_c"], writes=["dv_cf"])
    P.add("dve", lambda e: e.tensor_scalar(out=dv[:, DV_CH:DV_CH + 8], in0=dv[:, DV_CH:DV_CH + 8], scalar1=-16.0, scalar2=None,
                                           op0=ALU.mult), reads=["dv_c", "dv_cf"], writes=["dv_c"])
    P.add("dve", lambda e: e.tensor_scalar(out=dv[:, DV_NBA:DV_NBA + 16], in0=prmc(P_BA, 16), scalar1=-1.0, scalar2=None,
                                           op0=ALU.mult), reads=["prm"], writes=["dv_hb"])
    P.add("dve", lambda e: e.tensor_scalar(out=dv[:, DV_NBG:DV_NBG + 16], in0=prmc(P_BG, 16), scalar1=-1.0, scalar2=None,
                                           op0=ALU.mult), reads=["prm"], writes=["dv_hbg"])
    P.add("dve", lambda e: e.tensor_scalar(out=dv[:, DV_GQS:DV_GQS + 1], in0=prmc(P_GQ, 1), scalar1=ATTN_SCALE, scalar2=None,
                                           op0=ALU.mult), reads=["prm"], writes=["dv_gqs"])
    P.add("act", lambda e: e.activation(out=dv[:, DV_ESK:DV_ESK + 8], in_=prmc(P_SINK, 8), func=AF.Exp),
          reads=["prm"], writes=["dv_esk"])
    P.add("dve", lambda e: e.memset(dv[:, DV_MH:DV_MH + 4], -0.5), writes=["dv_mh"])
    CONSTS = ["prm", "dv_c", "dv_cf", "dv_hb", "dv_hbg", "dv_gqs", "dv_esk", "dv_mh"]
    for src, dst, nm in ((ga_d, gwa, "gwa"), (gx_d, gwx, "gwx")):
        P.dma(lambda e, src=src: e.dma_start(out=gstg.rearrange("p (n d) -> p n d", n=8),
                                             in_=src.rearrange("n c d -> c n d")), d_const, writes=[("tl", 2), ("tl", 3)])
        P.add("dve", lambda e, dst=dst: e.tensor_copy(out=dst[:, :, :], in_=gstg.rearrange("p (n d) -> p n d", n=8)),
              reads=[("tl", 2), ("tl", 3)], writes=[nm])
    for s in range(2):
        kc_f = tl[:, 0, 0:256]
        vc_f = tl[:, 1, 0:256]
        P.dma(lambda e, s=s: e.dma_start(out=kc_f, in_=ck_d[s]), d_const, writes=[("tl", 0)])
        P.dma(lambda e, s=s: e.dma_start(out=vc_f, in_=cv_d[s]), d_const, writes=[("tl", 1)])
        P.add("dve", lambda e, s=s: e.tensor_copy(out=vc[:, s, :, 0:128], in_=vc_f.rearrange("p (g d) -> p g d", g=2)),
              reads=[("tl", 1)], writes=[("vc", s)])
        for g in range(2):
            bk, bkk = bankB()
            P.add("pe", lambda e, g=g, bk=bk: e.transpose(bk[:, 0:128], kc_f[:, g * 128:(g + 1) * 128], identf[:, :]),
                  reads=[("tl", 0), "identf"], writes=[bkk])
            P.add("act", lambda e, s=s, g=g, bk=bk: e.activation(out=kTc[:, s, g, :], in_=bk[:, 0:128], func=AF.Copy),
                  reads=[bkk], writes=[("kTc", s)])
        P.dma(lambda e, s=s: e.dma_start(out=ks_o[s, 0:112, :], in_=ck_d[s, 16:128, :]), d_cp)
        P.dma(lambda e, s=s: e.dma_start(out=vs_o[s, 0:112, :], in_=cv_d[s, 16:128, :]), d_cp)

    P.fence()
    pp = {"i": 0}
    _pre_ok = not (_DBG_STOP is not None and _DBG_STOP <= 1)

    def prepass_chunk(src, kc, c0, ncol, dest_fn, scale, late=False):
        i = pp["i"]
        pp["i"] += 1
        if late:
            b = NSTG + i % 3
            sf, sb = stg_f2[i % 3], stg_b2[i % 3]
            depth = 2
        else:
            b = i % NSTG
            sf, sb = stg_f[b], stg_b[b]
            depth = NSTG - 1
        P.dma(lambda e: e.dma_start(out=sf[:, 0:ncol], in_=src[kc * 128:(kc + 1) * 128, c0:c0 + ncol]), d_stg[b],
              writes=[("stgf", b)])
        eng = ("dve", "act", "pool")[i % 3]
        rd = [("stgf", b)] + CONSTS
        if eng == "dve":
            P.add("dve", lambda e: e.tensor_scalar(out=sb[:, 0:ncol], in0=sf[:, 0:ncol], scalar1=scale, scalar2=None, op0=ALU.mult),
                  reads=rd, writes=[("stgb", b)])
        elif eng == "act":
            P.add("act", lambda e: e.activation(out=sb[:, 0:ncol], in_=sf[:, 0:ncol], func=AF.Copy, scale=scale),
                  reads=rd, writes=[("stgb", b)])
        else:
            P.add("pool", lambda e: e.tensor_scalar(out=sb[:, 0:ncol], in0=sf[:, 0:ncol], scalar1=scale, scalar2=1.0,
                                                    op0=ALU.mult, op1=ALU.mult), reads=rd, writes=[("stgb", b)])
        ng = ncol // 256
        slots = [dest_fn(g) for g in range(ng)]
        kcl = slots[0][1]
        s0 = slots[0][0]
        step = (slots[1][0] - s0) if ng > 1 else 1
        dst = wsc[s0:s0 + step * (ng - 1) + 1:step].rearrange("s p c -> p s c")[:, :, kcl * 256:(kcl + 1) * 256]
        pp.setdefault("pend", []).append(
            lambda: P.dma(lambda e: e.dma_start(out=dst, in_=sb[:, 0:ncol].rearrange("p (s c) -> p s c", s=ng)), d_wst[b],
                          reads=[("stgb", b)], writes=[("wsc", sl_[0]) for sl_ in slots]))
        while len(pp["pend"]) > depth:
            pp["pend"].pop(0)()

    for kc in range(8):
        g = prmc(P_GM + kc, 1)
        prepass_chunk(w_in_d, kc, 0, 1024, lambda i, kc=kc: (SL_U + i, kc), g)
        prepass_chunk(w_in_d, kc, 1024, 1024, lambda i, kc=kc: (SL_Q + i, kc), g)
        prepass_chunk(w_in_d, kc, 2048, 512, lambda i, kc=kc: (SL_K + i, kc), g)
        prepass_chunk(w_in_d, kc, 2560, 1024, lambda i, kc=kc: (SL_M + 4 * i + 2, kc), g)
        prepass_chunk(w_in_d, kc, 3584, 1024, lambda i, kc=kc: (SL_M + 4 * i + 3, kc), g)
        prepass_chunk(wrp_d, kc, 0, 1024, lambda i, kc=kc: (SL_M + 4 * i + 0, kc), 1.0)
        prepass_chunk(wap_d, kc, 0, 1024, lambda i, kc=kc: (SL_M + 4 * i + 1, kc), 1.0)
        prepass_chunk(wout_d, kc, 0, 1024, lambda i, kc=kc: (SL_O + i, kc), 1.0)

    def prepass_ffn_gen():
        for kc in range(8):
            gf = prmc(P_GF + kc, 1)
            for half in range(2):
                for (c0, ncol, p0) in ((0, 1024, 0), (1024, 1024, 4), (2048, 768, 8)):
                    prepass_chunk(wup_d, kc, half * DFF + c0, ncol,
                                  lambda i, kc=kc, half=half, p0=p0: (SL_UP + 2 * (p0 + i) + half, kc), gf, late=True)
                    yield
        for kc in range(NFB):
            prepass_chunk(wdn_d, kc, 0, 1024, lambda i, kc=kc: (SL_DN + 3 * i + kc // 8, kc % 8), 1.0, late=True)
            yield
        while pp.get("pend"):
            pp["pend"].pop(0)()
        yield

    while pp.get("pend"):
        pp["pend"].pop(0)()
    P.fence()

    ring_held = [False] * NRING

    def load_slot(slot):
        for k in range(NRING):
            r = (cnt["ring"] + k) % NRING
            if not ring_held[r]:
                break
        else:
            raise RuntimeError("weight ring exhausted")
        cnt["ring"] = r + 1
        ring_held[r] = True
        nv = 6 * 256 if (slot >= SL_DN and (slot - SL_DN) % 3 == 2) else 2048
        P.dma(lambda e: e.dma_start(out=ring[:, r, 0:nv], in_=wsc[slot, :, 0:nv]), d_ring[r], reads=[("wsc", slot)],
              writes=[("ring", r)])
        return ring[:, r, :].rearrange("p (k c) -> p k c", k=8), ("ring", r), r

    def release(sl):
        ring_held[sl[2]] = False

    def mm_fm(out_ap, okey, wv, wkey, col0, rhs_t, rkeys, W, nk=8, extra_reads=()):
        for kc in range(nk):
            P.add("pe", lambda e, kc=kc: e.matmul(out_ap, lhsT=wv[:, kc, col0:col0 + 128], rhs=rhs_t[:, kc, 0:W],
                                                  start=(kc == 0), stop=(kc == nk - 1)),
                  reads=[wkey] + list(rkeys) + list(extra_reads), writes=[okey])

    def norm_transpose(xs_i, rows, nsub, sscol, dst, dkey):
        for j in range(nsub):
            P.add("act", lambda e, j=j: e.activation(out=xn[0:rows, j % 2, :], in_=xt[0:rows, xs_i, j, :], func=AF.Square,
                                                      accum_out=ss[0:rows, sscol + j:sscol + j + 1]),
                  reads=[("xt", xs_i, j)], writes=[("xn", j % 2), ("ss", sscol + j)])
            yield
        P.add("pool", lambda e: e.tensor_scalar(out=rs[0:rows, sscol:sscol + nsub], in0=ss[0:rows, sscol:sscol + nsub],
                                                scalar1=1.0 / D, scalar2=EPS, op0=ALU.mult, op1=ALU.add),
              reads=[("ss", sscol + j) for j in range(nsub)], writes=[("rs", sscol)])
        P.add("pool", lambda e: e.tensor_tensor(out=rs[0:rows, sscol:sscol + nsub], in0=rs[0:rows, sscol:sscol + nsub],
                                                in1=dv[0:rows, DV_MH:DV_MH + nsub], op=ALU.pow),
              reads=[("rs", sscol), "dv_mh"], writes=[("rs", sscol)])
        yield
        for j in range(nsub):
            r = cnt["xn"] % 2
            cnt["xn"] += 1
            P.add("dve", lambda e, j=j, r=r: e.tensor_scalar(out=xn[0:rows, r, :], in0=xt[0:rows, xs_i, j, :],
                                                             scalar1=rs[0:rows, sscol + j:sscol + j + 1], scalar2=None,
                                                             op0=ALU.mult),
                  reads=[("xt", xs_i, j), ("rs", sscol)], writes=[("xn", r)])
            yield
            bk, bkk = bankB()
            bkb = bk[:, :].bitcast(BF16)
            for kc in range(8):
                P.add("pe", lambda e, kc=kc, r=r, bkb=bkb: e.transpose(bkb[:, kc * 128:kc * 128 + rows],
                                                                       xn[0:rows, r, kc * 128:(kc + 1) * 128],
                                                                       identb[0:rows, 0:rows]),
                      reads=[("xn", r), "identb"], writes=[bkk])
            yield
            P.add("act", lambda e, j=j, bkb=bkb: e.activation(
                out=dst[:, :, j * 128:j * 128 + rows],
                in_=bkb.rearrange("p (k t) -> p k t", k=8)[:, :, 0:rows], func=AF.Copy),
                reads=[bkk], writes=[dkey])
            yield

    def conv_taps(ps, pskey, acc, acckey, W, segs, ntap, wcol_fn, bcol, halo_fn, halokey_fn):
        P.add("act", lambda e: e.activation(out=acc[:, 0:W], in_=ps[:, 0:W], func=AF.Identity, bias=bcol,
                                            scale=wcol_fn(ntap - 1)),
              reads=[pskey] + CONSTS, writes=[acckey])
        nh = ntap - 1
        for (c0, L, sidx) in segs:
            hal = halo_fn(sidx)
            for s in range(1, ntap):
                wj = wcol_fn(ntap - 1 - s)
                P.add("dve", lambda e, c0=c0, L=L, s=s, wj=wj: e.scalar_tensor_tensor(
                    out=acc[:, c0 + s:c0 + L], in0=ps[:, c0:c0 + L - s], scalar=wj, in1=acc[:, c0 + s:c0 + L],
                    op0=ALU.mult, op1=ALU.add), reads=[pskey, acckey] + CONSTS, writes=[acckey])
                P.add("dve", lambda e, c0=c0, s=s, wj=wj, hal=hal: e.scalar_tensor_tensor(
                    out=acc[:, c0:c0 + s], in0=hal[:, nh - s:nh], scalar=wj, in1=acc[:, c0:c0 + s],
                    op0=ALU.mult, op1=ALU.add), reads=[halokey_fn(sidx), acckey] + CONSTS, writes=[acckey])
            P.add("dve", lambda e, c0=c0, L=L, hal=hal: e.tensor_scalar(out=hal[:, 0:nh], in0=ps[:, c0 + L - nh:c0 + L],
                                                                       scalar1=1.0, scalar2=None, op0=ALU.mult),
                  reads=[pskey], writes=[halokey_fn(sidx)])

    def chain(gens):
        for g in gens:
            yield from g

    def par(gens):
        gens = list(gens)
        while gens:
            for g in list(gens):
                try:
                    next(g)
                except StopIteration:
                    gens.remove(g)
            yield

    def run(gen):
        for _ in gen:
            pass

    def speed(gen, k):
        while True:
            for _ in range(k):
                try:
                    next(gen)
                except StopIteration:
                    return
            yield

    def tm_proj(xs_i, rows, nsub, slot_ids, nkc, lhs_t, lkeys):
        for cg in range(4):
            sl = [load_slot(s) for s in slot_ids(cg)]
            for j in range(nsub):
                for kc in range(nkc):
                    wv, wkey, _r = sl[kc // 8]
                    P.add("pe", lambda e, j=j, kc=kc, wv=wv: e.matmul(
                        psC[0:rows, j * 256:(j + 1) * 256], lhsT=lhs_t[:, kc, j * 128:j * 128 + rows], rhs=wv[:, kc % 8, :],
                        start=(kc == 0), stop=(kc == nkc - 1)), reads=[wkey] + list(lkeys), writes=[("psC", j // 2)])
                yield
            for s_ in sl:
                release(s_)
            for j in range(nsub):
                P.add("dve", lambda e, j=j, cg=cg: e.tensor_tensor(
                    out=xt[0:rows, xs_i, j, cg * 256:(cg + 1) * 256], in0=psC[0:rows, j * 256:(j + 1) * 256],
                    in1=xt[0:rows, xs_i, j, cg * 256:(cg + 1) * 256], op=ALU.add),
                    reads=[("psC", j // 2), ("xt", xs_i, j)], writes=[("xt", xs_i, j)])
            yield

    HT = ["hTa"]

    def stageA_gen(ti, xs_i, W, rows, nsub, segs, first_tile, is_sample):
        yield from norm_transpose(xs_i, rows, nsub, 0, hTa, "hTa")

    def front_gen(ti, xs_i, W, rows, nsub, segs, first_tile, is_sample, gate=None, wd_gate=None):
        slot_cache = {}
        slot_uses = {}
        nvj = nsub if not is_sample else len(segs)
        for k in range(4):
            slot_uses[SL_U + k] = 2
            slot_uses[SL_Q + k] = 2
        slot_uses[SL_K] = 2
        slot_uses[SL_V] = nvj

        def use_slot(sid):
            if sid not in slot_cache:
                slot_cache[sid] = load_slot(sid)
            return slot_cache[sid][0], slot_cache[sid][1]

        def done_slot(sid):
            slot_uses[sid] -= 1
            if slot_uses[sid] == 0:
                release(slot_cache[sid])

        done_head = {}
        done_tail = {}

        def rnn_head(n):
            lane = n % 2
            sset = n % 4
            bU, bUk = banks8[lane]
            bG, bGk = banks8[lane]
            acc, acck = tl[:, 4 * sset + 0, :], ("tl", 4 * sset + 0)
            t1, t1k = tl[:, 4 * sset + 1, :], ("tl", 4 * sset + 1)
            t2, t2k = tl[:, 4 * sset + 2, :], ("tl", 4 * sset + 2)
            ta, tak = tl[:, 4 * sset + 3, :], ("tl", 4 * sset + 3)
            ucb, ucbk = tlb[:, sset, :], ("tlb", sset)
            while n >= 4 and not done_tail.get(n - 4):
                yield
            wv, wkey = use_slot(SL_U + n // 2)
            mm_fm(bU[:, 0:W], bUk, wv, wkey, (n % 2) * 128, hTa, HT, W)
            done_slot(SL_U + n // 2)
            yield
            conv_taps(bU, bUk, acc, acck, W, segs, 4, lambda j, n=n: prmc(P_CW + j * 8 + n, 1), prmc(P_CB + n, 1),
                      lambda sidx, n=n: st[:, ST_HU + sidx * 24 + n * 3:ST_HU + sidx * 24 + n * 3 + 3],
                      lambda sidx, n=n: ("st_hu", sidx, n))
            yield
            P.add("dve", lambda e: e.tensor_copy(out=ucb[:, 0:W], in_=acc[:, 0:W]), reads=[acck], writes=[ucbk])
            yield
            P.add("pe", lambda e: e.matmul(bG[:, 0:W], lhsT=gwa[:, n, :], rhs=ucb[:, 0:W], start=True, stop=True),
                  reads=["gwa", ucbk], writes=[bGk])
            yield
            P.add("act", lambda e: e.activation(out=t1[:, 0:W], in_=bG[:, 0:W], func=AF.Exp, scale=-1.0,
                                                bias=dv[:, DV_NBA + n:DV_NBA + n + 1]), reads=[bGk] + CONSTS, writes=[t1k])
            yield
            P.add("pe", lambda e: e.matmul(bG[:, 0:W], lhsT=gwx[:, n, :], rhs=ucb[:, 0:W], start=True, stop=True),
                  reads=["gwx", ucbk], writes=[bGk])
            yield
            P.add("act", lambda e: e.activation(out=t2[:, 0:W], in_=bG[:, 0:W], func=AF.Exp, scale=-1.0,
                                                bias=dv[:, DV_NBX + n:DV_NBX + n + 1]), reads=[bGk] + CONSTS, writes=[t2k])
            yield
            done_head[n] = True

        def rnn_tail(n):
            lane = n % 2
            sset = n % 4
            bU, bUk = banks8[lane]
            bG, bGk = banks8[lane]
            acc, acck = tl[:, 4 * sset + 0, :], ("tl", 4 * sset + 0)
            t1, t1k = tl[:, 4 * sset + 1, :], ("tl", 4 * sset + 1)
            t2, t2k = tl[:, 4 * sset + 2, :], ("tl", 4 * sset + 2)
            ta, tak = tl[:, 4 * sset + 3, :], ("tl", 4 * sset + 3)
            ucb, ucbk = tlb[:, sset, :], ("tlb", sset)
            while not done_head.get(n):
                yield
            for tt, ttk in ((t1, t1k), (t2, t2k)):
                P.add("act", lambda e, tt=tt: e.activation(out=tt[:, 0:W], in_=tt[:, 0:W], func=AF.Ln, bias=1.0),
                      reads=[ttk], writes=[ttk])
            yield
            for tt, ttk in ((t1, t1k), (t2, t2k)):
                P.add("act", lambda e, tt=tt: e.activation(out=tt[:, 0:W], in_=tt[:, 0:W], func=AF.Exp, scale=-1.0),
                      reads=[ttk], writes=[ttk])
            yield
            P.add("act", lambda e: e.activation(out=ta[:, 0:W], in_=t1[:, 0:W], func=AF.Exp, scale=dv[:, DV_C + n:DV_C + n + 1]),
                  reads=[t1k] + CONSTS, writes=[tak])
            P.add("dve", lambda e: e.tensor_tensor(out=t2[:, 0:W], in0=t2[:, 0:W], in1=acc[:, 0:W], op=ALU.mult),
                  reads=[t2k, acck], writes=[t2k])
            yield
            P.add("pool", lambda e: e.tensor_tensor(out=t1[:, 0:W], in0=ta[:, 0:W], in1=ta[:, 0:W], op=ALU.mult),
                  reads=[tak], writes=[t1k])
            yield
            P.add("act", lambda e: e.activation(out=t1[:, 0:W], in_=t1[:, 0:W], func=AF.Ln, bias=1.0, scale=-1.0),
                  reads=[t1k], writes=[t1k])
            yield
            P.add("act", lambda e: e.activation(out=t1[:, 0:W], in_=t1[:, 0:W], func=AF.Exp, scale=0.5), reads=[t1k], writes=[t1k])
            yield
            P.add("dve", lambda e: e.tensor_tensor(out=t2[:, 0:W], in0=t2[:, 0:W], in1=t1[:, 0:W], op=ALU.mult),
                  reads=[t1k, t2k], writes=[t2k])
            yield
            for (c0, L, sidx) in segs:
                hcol = st[:, ST_H + sidx * 8 + n:ST_H + sidx * 8 + n + 1]
                P.add("dve", lambda e, c0=c0, L=L, hcol=hcol: e.tensor_tensor_scan(
                    out=acc[:, c0:c0 + L], data0=ta[:, c0:c0 + L], data1=t2[:, c0:c0 + L], initial=hcol,
                    op0=ALU.mult, op1=ALU.add), reads=[tak, t2k, ("st_h", sidx, n)], writes=[acck])
                P.add("dve", lambda e, c0=c0, L=L, hcol=hcol: e.tensor_copy(out=hcol, in_=acc[:, c0 + L - 1:c0 + L]),
                      reads=[acck], writes=[("st_h", sidx, n)])
            yield
            P.add("pool", lambda e: e.tensor_copy(out=yr[:, n, 0:W], in_=acc[:, 0:W]), reads=[acck], writes=[("yr", n)])
            yield
            done_tail[n] = True

        qk_cnt = {"i": 0}

        def qk_gate():
            while gate is not None and not gate["go"]:
                yield

        def qk_unit(hb, ql):
            qps, qpk = banks8[2 + ql]
            sps, spk = banks8[6 + ql]
            sq, sqk = tlb[:, 4 + ql, :], ("tlb", 4 + ql)
            lt, ltk = tl[:, 16 + ql, :], ("tl", 16 + ql)
            if hb < 8:
                sid = SL_Q + hb // 2
                wv, wkey = use_slot(sid)
                col0 = (hb % 2) * 128
            else:
                sid = SL_K
                wv, wkey = use_slot(sid)
                col0 = (hb - 8) * 128
            mm_fm(qps[:, 0:W], qpk, wv, wkey, col0, hTa, HT, W)
            done_slot(sid)
            yield
            P.add("act", lambda e: e.activation(out=sq[:, 0:W], in_=qps[:, 0:W], func=AF.Square), reads=[qpk], writes=[sqk])
            yield
            P.add("pe", lambda e: e.matmul(sps[:, 0:W], lhsT=onesb[:, :], rhs=sq[:, 0:W], start=True, stop=True),
                  reads=[sqk, "onesb"], writes=[spk])
            yield
            P.add("act", lambda e: e.activation(out=lt[:, 0:W], in_=sps[:, 0:W], func=AF.Ln, bias=EPS, scale=1.0 / HD),
                  reads=[spk], writes=[ltk])
            yield
            P.add("act", lambda e: e.activation(out=lt[:, 0:W], in_=lt[:, 0:W], func=AF.Exp, scale=-0.5), reads=[ltk], writes=[ltk])
            yield
            if hb < 8:
                P.add("dve", lambda e: e.scalar_tensor_tensor(
                    out=qT[:, hb, 0:W], in0=qps[:, 0:W], scalar=dv[:, DV_GQS:DV_GQS + 1], in1=lt[:, 0:W],
                    op0=ALU.mult, op1=ALU.mult), reads=[qpk, ltk] + CONSTS, writes=[("qT", hb)])
            else:
                g = hb - 8
                P.add("dve", lambda e: e.scalar_tensor_tensor(
                    out=kT[:, g, 128:128 + W], in0=qps[:, 0:W], scalar=prmc(P_GK, 1), in1=lt[:, 0:W],
                    op0=ALU.mult, op1=ALU.mult), reads=[qpk, ltk] + CONSTS, writes=[("kT", 1)])
                outs = []
                if is_sample:
                    for (c0, L, sidx) in segs:
                        outs.append((c0, L, kvs[0:L, sidx - 1, 0, g * 128:(g + 1) * 128], ("kvs", sidx - 1, 0)))
                elif ti == NT - 1:
                    outs.append((W - 128, 128, kvo[:, 0, g * 128:(g + 1) * 128], ("kvo", 0)))
                for (c0, L, dst, dkey) in outs:
                    kf, kfk = tl[:, 18, ql * 256:(ql + 1) * 256], ("tl18", ql)
                    P.add("dve", lambda e, c0=c0, L=L: e.scalar_tensor_tensor(
                        out=kf[:, 0:L], in0=qps[:, c0:c0 + L], scalar=prmc(P_GK, 1), in1=lt[:, c0:c0 + L],
                        op0=ALU.mult, op1=ALU.mult), reads=[qpk, ltk] + CONSTS, writes=[kfk])
                    P.add("pe", lambda e, L=L: e.transpose(sps[0:L, 0:128], kf[:, 0:L], identf[:, :]),
                          reads=[kfk, "identf"], writes=[spk])
                    P.add("act", lambda e, dst=dst, L=L: e.activation(out=dst, in_=sps[0:L, 0:128], func=AF.Copy),
                          reads=[spk], writes=[dkey])
            yield

        def v_unit(job, ql):
            (j, c0, L, dst, dkey, fout) = job
            vps, vpk = banks8[2 + ql]
            wv, wkey = use_slot(SL_V)
            for kc in range(8):
                P.add("pe", lambda e, kc=kc: e.matmul(vps[0:L, 0:256], lhsT=hTa[:, kc, c0:c0 + L], rhs=wv[:, kc, 0:256],
                                                      start=(kc == 0), stop=(kc == 7)), reads=[wkey] + HT, writes=[vpk])
            done_slot(SL_V)
            yield
            P.add("act", lambda e: e.activation(out=dst, in_=vps[0:L, 0:256].rearrange("p (g d) -> p g d", g=2), func=AF.Copy),
                  reads=[vpk], writes=[dkey])
            if fout is not None:
                P.add("act", lambda e: e.activation(out=fout[0], in_=vps[0:L, 0:256], func=AF.Copy), reads=[vpk], writes=[fout[1]])
            yield

        if not is_sample:
            vjobs = [(j, j * 128, 128, vb[:, 1 + j, :, 0:128], ("vb", 1 + j),
                      (kvo[:, 1, :], ("kvo", 1)) if (ti == NT - 1 and j == nsub - 1) else None) for j in range(nsub)]
        else:
            vjobs = [(sidx - 1, c0, L, vb[0:L, sidx, :, 0:128], ("vb", sidx), (kvs[0:L, sidx - 1, 1, :], ("kvs", sidx - 1, 1)))
                     for (c0, L, sidx) in segs]


        if not is_sample:
            vjobs = [(j, j * 128, 128, vb[:, 1 + j, :, 0:128], ("vb", 1 + j),
                      (kvo[:, 1, :], ("kvo", 1)) if (ti == NT - 1 and j == nsub - 1) else None) for j in range(nsub)]
        else:
            vjobs = [(sidx - 1, c0, L, vb[0:L, sidx, :, 0:128], ("vb", sidx), (kvs[0:L, sidx - 1, 1, :], ("kvs", sidx - 1, 1)))
                     for (c0, L, sidx) in segs]
        qk_flags = {}

        def qk_done(ql):
            qk_flags[ql] = True
            yield

        yield from par([chain([rnn_head(n) for n in (0, 2, 4, 6)]),
                        chain([rnn_head(n) for n in (1, 3, 5, 7)]),
                        chain([rnn_tail(n) for n in (0, 2, 4, 6)]),
                        chain([rnn_tail(n) for n in (1, 3, 5, 7)]),
                        speed(chain([qk_gate()] + [qk_unit(hb, 0) for hb in range(0, 10, 2)] + [v_unit(jb, 0) for jb in vjobs[0::2]]
                                    + [qk_done(0)]), 2),
                        speed(chain([qk_gate()] + [qk_unit(hb, 1) for hb in range(1, 10, 2)] + [v_unit(jb, 1) for jb in vjobs[1::2]]
                                    + [qk_done(1)]), 2),
                        attn_gen(ti, xs_i, W, rows, nsub, segs, first_tile, is_sample, flags=qk_flags, gate=wd_gate)])

    def attn_gen(ti, xs_i, W, rows, nsub, segs, first_tile, is_sample, flags=None, gate=None):
        while (flags is not None and not (flags.get(0) and flags.get(1))) or (gate is not None and not gate["go"]):
            yield
        acnt = [0]
        if not is_sample:
            ajobs = []
            for j in range(nsub):
                kbs = []
                if not (first_tile and j == 0):
                    kbs.append((0, kT[:, :, j * 128:(j + 1) * 128], ("kT", 0 if j == 0 else 1), vb[:, j, :, :], ("vb", j), 128))
                kbs.append((1, kT[:, :, (j + 1) * 128:(j + 2) * 128], ("kT", 1), vb[:, j + 1, :, :], ("vb", j + 1), 128))
                ajobs.append((j * 128, 128, kbs))
        else:
            ajobs = []
            for (c0, L, sidx) in segs:
                kbs = [(0, kTc[:, sidx - 1, :, :], ("kTc", sidx - 1), vc[:, sidx - 1, :, :], ("vc", sidx - 1), 128),
                       (1, kT[:, :, 128 + c0:128 + c0 + L], ("kT", 1), vb[0:L, sidx, :, :], ("vb", sidx), L)]
                ajobs.append((c0, L, kbs))
        for (c0, nq, kbs) in ajobs:
            pts = {}
            for g in range(2):
                for (kb, kTv, kTk, vv, vk, nk) in kbs:
                    sps, spk = banks8[2 + acnt[0] % 2]
                    acnt[0] += 1
                    so2 = sps[0:nk, 0:4 * nq]
                    so = so2.rearrange("p (h q) -> p h q", h=4)
                    P.add("pe", lambda e, so2=so2, kTv=kTv, g=g, nk=nk, c0=c0, nq=nq: e.matmul(
                        so2, lhsT=kTv[:, g, 0:nk], rhs=qT[:, 4 * g:4 * g + 4, c0:c0 + nq], start=True, stop=False),
                        reads=[kTk] + [("qT", 4 * g + i) for i in range(4)], writes=[spk])
                    P.add("pe", lambda e, so2=so2, kb=kb, g=g, nk=nk, nq=nq: e.matmul(
                        so2, lhsT=identb[0:nk, 0:nk], rhs=btab[0:nk, kb, 4 * g:4 * g + 4, 0:nq], start=False, stop=True),
                        reads=["identb", "btab"], writes=[spk])
                    pi = cnt["pT"] % 4
                    cnt["pT"] += 1
                    po = pT[0:nk, pi, 0:4 * nq].rearrange("p (h q) -> p h q", h=4)
                    P.add("act", lambda e, po=po, so=so: e.activation(out=po, in_=so, func=AF.Exp), reads=[spk],
                          writes=[("pT", pi)])
                    pts[(g, kb)] = (po, ("pT", pi), vv, vk, nk)
                    yield
            dps, dpk = bankB()
            for h in range(8):
                g = h // 4
                lst = [pts[(g, kb)] for (kb, *_r) in kbs]
                for idx, (po, pk, vv, vk, nk) in enumerate(lst):
                    P.add("pe", lambda e, po=po, vv=vv, nk=nk, h=h, g=g, idx=idx, n=len(lst), nq=nq: e.matmul(
                        psC[0:nq, h * 128:(h + 1) * 128], lhsT=po[:, h % 4, :], rhs=vv[0:nk, g, 0:128],
                        start=(idx == 0), stop=(idx == n - 1)), reads=[pk, vk], writes=[("psC", h // 4)])
                    P.add("pe", lambda e, po=po, vv=vv, nk=nk, h=h, g=g, idx=idx, n=len(lst), nq=nq, dps=dps: e.matmul(
                        dps[0:nq, h:h + 1], lhsT=po[:, h % 4, :], rhs=onesb[0:nk, 0:1],
                        start=(idx == 0), stop=(idx == n - 1)), reads=[pk, "onesb"], writes=[dpk])
            yield
            P.add("dve", lambda e, dps=dps, nq=nq: e.tensor_tensor(out=smal[0:nq, 0:8], in0=dps[0:nq, 0:8],
                                                                   in1=dv[0:nq, DV_ESK:DV_ESK + 8], op=ALU.add),
                  reads=[dpk] + CONSTS, writes=["smal"])
            P.add("dve", lambda e, nq=nq: e.reciprocal(out=smal[0:nq, 8:16], in_=smal[0:nq, 0:8]), reads=["smal"],
                  writes=["smal2"])
            yi = cnt["yat"] % 2
            cnt["yat"] += 1
            for half in range(2):
                P.add("dve", lambda e, half=half, yi=yi, nq=nq: e.tensor_tensor(
                    out=yat[0:nq, yi, half * 512:(half + 1) * 512].rearrange("p (h d) -> p h d", h=4),
                    in0=psC[0:nq, half * 512:(half + 1) * 512].rearrange("p (h d) -> p h d", h=4),
                    in1=smal[0:nq, 8 + 4 * half:12 + 4 * half].unsqueeze(2).to_broadcast([nq, 4, 128]), op=ALU.mult),
                    reads=[("psC", half), "smal2"], writes=[("yat", yi)])
            yield
            bk, bkk = bankB()
            bkb = bk[:, :].bitcast(BF16)
            for h in range(8):
                P.add("pe", lambda e, h=h, yi=yi, bkb=bkb, nq=nq: e.transpose(bkb[:, h * 128:h * 128 + nq],
                                                                               yat[0:nq, yi, h * 128:(h + 1) * 128],
                                                                               identb[0:nq, 0:nq]),
                      reads=[("yat", yi), "identb"], writes=[bkk])
            P.add("act", lambda e, bkb=bkb, c0=c0, nq=nq: e.activation(
                out=ya[:, :, c0:c0 + nq], in_=bkb.rearrange("p (h t) -> p h t", h=8)[:, :, 0:nq], func=AF.Copy),
                reads=[bkk], writes=["ya"])
            yield
        if not is_sample:
            P.add("pool", lambda e: e.tensor_copy(out=kT[:, :, 0:128], in_=kT[:, :, 512:640]), reads=[("kT", 1)], writes=[("kT", 0)])
            P.add("pool", lambda e: e.tensor_copy(out=vb[:, 0, :, 0:128], in_=vb[:, 4, :, 0:128]), reads=[("vb", 4)],
                  writes=[("vb", 0)])
        yield

    def mid(ti, xs_i, W, rows, nsub, segs, first_tile, is_sample):
        YR = [("yr", n) for n in range(8)]
        for pr in range(4):
            s_rp = load_slot(SL_M + 4 * pr + 0)
            s_ap = load_slot(SL_M + 4 * pr + 1)
            s_gr = load_slot(SL_M + 4 * pr + 2)
            s_ga = load_slot(SL_M + 4 * pr + 3)
            for sub in range(2):
                m = 2 * pr + sub
                col0 = sub * 128
                p3, p3k = bankA()
                mm_fm(p3[:, 0:W], p3k, s_gr[0], s_gr[1], col0, hTa, HT, W)
                p4, p4k = bankA()
                mm_fm(p4[:, 0:W], p4k, s_ga[0], s_ga[1], col0, hTa, HT, W)
                p1, p1k = bankA()
                mm_fm(p1[:, 0:W], p1k, s_rp[0], s_rp[1], col0, yr, YR, W)
                p2, p2k = bankA()
                mm_fm(p2[:, 0:W], p2k, s_ap[0], s_ap[1], col0, ya, ["ya"], W)
                t1, t1k = tmpf()
                t2, t2k = tmpf()
                P.add("act", lambda e, m=m, t1=t1, p3=p3: e.activation(out=t1[:, 0:W], in_=p3[:, 0:W], func=AF.Exp,
                                                                        bias=dv[:, DV_NBG + m:DV_NBG + m + 1], scale=-1.0),
                      reads=[p3k] + CONSTS, writes=[t1k])
                P.add("act", lambda e, m=m, t2=t2, p4=p4: e.activation(out=t2[:, 0:W], in_=p4[:, 0:W], func=AF.Exp,
                                                                        bias=dv[:, DV_NBG + 8 + m:DV_NBG + 9 + m], scale=-1.0),
                      reads=[p4k] + CONSTS, writes=[t2k])
                for tt, ttk in ((t1, t1k), (t2, t2k)):
                    P.add("act", lambda e, tt=tt: e.activation(out=tt[:, 0:W], in_=tt[:, 0:W], func=AF.Ln, bias=1.0),
                          reads=[ttk], writes=[ttk])
                for tt, ttk in ((t1, t1k), (t2, t2k)):
                    P.add("act", lambda e, tt=tt: e.activation(out=tt[:, 0:W], in_=tt[:, 0:W], func=AF.Exp, scale=-1.0),
                          reads=[ttk], writes=[ttk])
                m1, m1k = tmpb()
                m2, m2k = tmpb()
                P.add("dve", lambda e, m1=m1, t1=t1, p1=p1: e.tensor_tensor(
                    out=m1[:, 0:W], in0=p1[:, 0:W], in1=t1[:, 0:W], op=ALU.mult), reads=[t1k, p1k], writes=[m1k])
                P.add("dve", lambda e, m2=m2, t2=t2, p2=p2: e.tensor_tensor(
                    out=m2[:, 0:W], in0=p2[:, 0:W], in1=t2[:, 0:W], op=ALU.mult), reads=[t2k, p2k], writes=[m2k])
                P.add("dve", lambda e, m=m, m1=m1, m2=m2: e.tensor_tensor(out=mixed[:, m, 0:W], in0=m1[:, 0:W], in1=m2[:, 0:W],
                                                                          op=ALU.add), reads=[m1k, m2k], writes=[("mixed", m)])
            for sl_ in (s_rp, s_ap, s_gr, s_ga):
                release(sl_)
        MX = [("mixed", m) for m in range(8)]
        run(tm_proj(xs_i, rows, nsub, lambda cg: [SL_O + cg], 8, mixed, MX))
        run(norm_transpose(xs_i, rows, nsub, 4, hTb, "hTb"))

    def ffn_up(ti, xs_i, W, rows, nsub, segs):
        for pr in range(11):
            s_a = load_slot(SL_UP + 2 * pr)
            s_b = load_slot(SL_UP + 2 * pr + 1)
            for sub in range(2):
                kc2 = 2 * pr + sub
                col0 = sub * 128
                aps, apk = bankA()
                mm_fm(aps[:, 0:W], apk, s_a[0], s_a[1], col0, hTb, ["hTb"], W)
                bps, bpk = bankA()
                mm_fm(bps[:, 0:W], bpk, s_b[0], s_b[1], col0, hTb, ["hTb"], W)
                acc, acck = tmpf()
                conv_taps(aps, apk, acc, acck, W, segs, 3, lambda j, kc2=kc2: prmc(P_FW + j * NFB + kc2, 1),
                          prmc(P_FB + kc2, 1),
                          lambda sidx, kc2=kc2: st[:, ST_FH + sidx * 44 + kc2 * 2:ST_FH + sidx * 44 + kc2 * 2 + 2],
                          lambda sidx, kc2=kc2: ("st_fh", sidx, kc2))
                gl, glk = tmpf()
                P.add("act", lambda e, gl=gl, acc=acc: e.activation(out=gl[:, 0:W], in_=acc[:, 0:W], func=AF.Gelu),
                      reads=[acck], writes=[glk])
                P.add("dve", lambda e, kc2=kc2, gl=gl, bps=bps: e.tensor_tensor(out=actb[:, kc2, 0:W], in0=bps[:, 0:W],
                                                                                in1=gl[:, 0:W], op=ALU.mult),
                      reads=[bpk, glk], writes=[("act", kc2)])
                yield
            release(s_a)
            release(s_b)

    def wdown_gen(ti, xs_i, W, rows, nsub, segs):
        yield from tm_proj(xs_i, rows, nsub, lambda cg: [SL_DN + 3 * cg + k for k in range(3)], NFB, actb,
                           [("act", k) for k in range(NFB)])

    def load_x(ti):
        b = ti % 2
        if ti < NT:
            P.dma(lambda e: e.dma_start(out=xt[:, b, :, :], in_=xp[ti * 512:(ti + 1) * 512, :].rearrange("(j p) d -> p j d", p=128)),
                  d_x[b], writes=[("xt", b, j) for j in range(4)])
        else:
            P.dma(lambda e: e.dma_start(out=xt[0:32, b, 0, :], in_=xs[:, :]), d_x[b], writes=[("xt", b, 0)])

    def targs(ti):
        b = ti % 2
        if ti < NT:
            return (ti, b, 512, 128, 4, [(0, 512, 0)], ti == 0, False)
        return (ti, b, 32, 32, 1, [(0, 16, 1), (16, 16, 2)], False, True)

    load_x(0)
    run(par([prepass_ffn_gen(), chain([stageA_gen(*targs(0)), front_gen(*targs(0))])]))
    for ti in range(NT + 1):
        b = ti % 2
        if ti + 1 <= NT:
            load_x(ti + 1)
        ta_ = targs(ti)
        mid(*ta_)
        if ti == 0:
            P.fence()
        gens = [ffn_up(*ta_[:6])]
        if ti + 1 <= NT:
            gens.append(stageA_gen(*targs(ti + 1)))
        run(par(gens))
        gate = {"go": False}

        def wd_then_open(g=gate, a=ta_[:6]):
            yield from wdown_gen(*a)
            g["go"] = True

        gens = [wd_then_open()]
        if ti + 1 <= NT:
            gens.append(front_gen(*targs(ti + 1), gate=None, wd_gate=gate))
        run(par(gens))
        if ti < NT:
            P.dma(lambda e, ti=ti, b=b: e.dma_start(out=yp[ti * 512:(ti + 1) * 512, :].rearrange("(j p) d -> p j d", p=128),
                                                    in_=xt[:, b, :, :]), d_y[b], reads=[("xt", b, j) for j in range(4)],
                  queue="pool")
        else:
            P.dma(lambda e, b=b: e.dma_start(out=ys[:, :], in_=xt[0:32, b, 0, :]), d_y[b], reads=[("xt", b, 0)], queue="pool")
    stkeys = (["st_hu", "st_h", "st_fh"] + [("st_hu", s, n) for s in range(3) for n in range(8)]
              + [("st_h", s, n) for s in range(3) for n in range(8)] + [("st_fh", s, k) for s in range(3) for k in range(NFB)])
    P.dma(lambda e: e.dma_start(out=st_o[:, :], in_=st[:, :]), d_fin, reads=stkeys, queue="pool")
    P.dma(lambda e: e.dma_start(out=kp_o[:, :], in_=kvo[:, 0, :]), d_fin, reads=[("kvo", 0)], queue="pool")
    P.dma(lambda e: e.dma_start(out=vp_o[:, :], in_=kvo[:, 1, :]), d_fin, reads=[("kvo", 1)], queue="pool")
    for s in range(2):
        P.dma(lambda e, s=s: e.dma_start(out=ks_o[s, 112:128, :], in_=kvs[0:16, s, 0, :]), d_fin, reads=[("kvs", s, 0)], queue="pool")
        P.dma(lambda e, s=s: e.dma_start(out=vs_o[s, 112:128, :], in_=kvs[0:16, s, 1, :]), d_fin, reads=[("kvs", s, 1)], queue="pool")
    P.finish()
    return nc, P


def _fm(v, nblk):
    v = np.asarray(v, np.float32)
    lead = v.shape[:-1]
    v = v.reshape(lead + (nblk, 128))
    return np.moveaxis(v, -1, 0)


_CACHE = {}


def kernel(x_prompt, x_sample, state_rnn_conv, state_rnn_h, cache_attn_k, cache_attn_v, state_ffn_conv,
           norm_mix_g, w_in, b_gate, rnn_conv_w, rnn_conv_b, rnn_gate_a_w, rnn_gate_a_b, rnn_gate_x_w,
           rnn_gate_x_b, rnn_lambda, q_norm_g, k_norm_g, attn_sinks, w_rnn_proj, w_attn_proj, w_out,
           norm_ffn_g, w_up, ffn_conv_w, ffn_conv_b, w_down):
    f32 = lambda a: np.ascontiguousarray(np.asarray(a, np.float32))
    x_prompt = f32(x_prompt)
    B, T, _ = x_prompt.shape
    NC = 8
    assert B == NC and x_sample.shape[0] == 2 * NC and x_sample.shape[1] == 16
    prm = np.zeros((128, NPRM), np.float32)
    prm[:, P_GM:P_GM + 8] = _fm(norm_mix_g[0], 8)
    prm[:, P_GF:P_GF + 8] = _fm(norm_ffn_g[0], 8)
    prm[:, P_CW:P_CW + 32] = _fm(rnn_conv_w[0], 8).reshape(128, 32)
    prm[:, P_CB:P_CB + 8] = _fm(rnn_conv_b[0], 8)
    prm[:, P_BA:P_BA + 8] = _fm(rnn_gate_a_b[0], 8)
    prm[:, P_BX:P_BX + 8] = _fm(rnn_gate_x_b[0], 8)
    prm[:, P_LAM:P_LAM + 8] = _fm(rnn_lambda[0], 8)
    prm[:, P_BG:P_BG + 16] = _fm(b_gate[0], 16)
    prm[:, P_FW:P_FW + 66] = _fm(ffn_conv_w[0], NFB).reshape(128, 66)
    prm[:, P_FB:P_FB + 22] = _fm(ffn_conv_b[0], NFB)
    prm[:, P_GQ] = np.asarray(q_norm_g[0], np.float32)
    prm[:, P_GK] = np.asarray(k_norm_g[0], np.float32)
    prm[:, P_SINK:P_SINK + 8] = np.broadcast_to(np.asarray(attn_sinks[0], np.float32)[None, :], (128, 8))
    key = T
    if key not in _CACHE:
        _CACHE[key] = build_program(T)[0]
    nc = _CACHE[key]
    shared = {
        "prm": prm, "w_in": f32(w_in[0]), "gate_a": f32(rnn_gate_a_w[0]), "gate_x": f32(rnn_gate_x_w[0]),
        "w_rnn_proj": f32(w_rnn_proj[0]), "w_attn_proj": f32(w_attn_proj[0]), "w_out": f32(w_out[0]),
        "w_up": f32(w_up[0]), "w_down": f32(w_down[0]),
    }
    src = np.asarray(state_rnn_conv[0], np.float32)
    sh = np.asarray(state_rnn_h[0], np.float32)
    sf = np.asarray(state_ffn_conv[0], np.float32)
    ck = np.asarray(cache_attn_k[0], np.float32).reshape(2 * NC, 128, 256)
    cv = np.asarray(cache_attn_v[0], np.float32).reshape(2 * NC, 128, 256)
    xs_all = f32(x_sample)
    in_maps = []
    for c in range(NC):
        sst = np.zeros((128, 152), np.float32)
        for s in range(2):
            q = 2 * c + s
            sst[:, s * 24:(s + 1) * 24] = np.transpose(_fm(src[q], 8), (0, 2, 1)).reshape(128, 24)
            sst[:, 48 + s * 8:48 + (s + 1) * 8] = _fm(sh[q], 8)
            sst[:, 64 + s * 44:64 + (s + 1) * 44] = np.transpose(_fm(sf[q], NFB), (0, 2, 1)).reshape(128, 44)
        m = dict(shared)
        m["xp"] = x_prompt[c]
        m["xs"] = np.ascontiguousarray(xs_all[2 * c:2 * c + 2].reshape(32, D))
        m["sst"] = sst
        m["ck"] = np.ascontiguousarray(ck[2 * c:2 * c + 2])
        m["cv"] = np.ascontiguousarray(cv[2 * c:2 * c + 2])
        in_maps.append(m)
    res = run_bass_kernel_spmd(nc, in_maps, core_ids=list(range(NC)))
    R = res.results
    y_p = np.stack([R[c]["yp"] for c in range(NC)])[:, :, :]
    y_s = np.concatenate([R[c]["ys"].reshape(2, 16, D) for c in range(NC)], axis=0)
    st = np.stack([R[c]["st_o"] for c in range(NC)])

    def unfm(a):
        return np.transpose(a, (2, 1, 0)).reshape(a.shape[2], -1)

    rc_p = np.stack([unfm(st[c][:, ST_HU:ST_HU + 24].reshape(128, 8, 3)) for c in range(NC)])[None]
    rc_s = np.stack([unfm(st[c][:, ST_HU + 24 * (1 + s):ST_HU + 24 * (2 + s)].reshape(128, 8, 3))
                     for c in range(NC) for s in range(2)])[None]
    h_p = np.stack([st[c][:, ST_H:ST_H + 8].T.reshape(-1) for c in range(NC)])[None]
    h_s = np.stack([st[c][:, ST_H + 8 * (1 + s):ST_H + 8 * (2 + s)].T.reshape(-1) for c in range(NC) for s in range(2)])[None]
    f_p = np.stack([unfm(st[c][:, ST_FH:ST_FH + 44].reshape(128, NFB, 2)) for c in range(NC)])[None]
    f_s = np.stack([unfm(st[c][:, ST_FH + 44 * (1 + s):ST_FH + 44 * (2 + s)].reshape(128, NFB, 2))
                    for c in range(NC) for s in range(2)])[None]
    k_p = np.stack([R[c]["kp"].reshape(128, 2, 128) for c in range(NC)])[None]
    v_p = np.stack([R[c]["vp"].reshape(128, 2, 128) for c in range(NC)])[None]
    k_s = np.concatenate([R[c]["ks"].reshape(2, 128, 2, 128) for c in range(NC)], axis=0)[None]
    v_s = np.concatenate([R[c]["vs"].reshape(2, 128, 2, 128) for c in range(NC)], axis=0)[None]
    outs = (y_p, y_s, rc_p, rc_s, h_p, h_s, k_p, k_s, v_p, v_s, f_p, f_s)
    return tuple(np.ascontiguousarray(o, dtype=np.float32) for o in outs)
```

```python
import contextlib
import numpy as np
import concourse.bass as bass
import concourse.mybir as mybir
from concourse.bass_utils import run_bass_kernel_spmd
from concourse.alu_op_type import AluOpType as ALU

F32 = mybir.dt.float32
BF16 = mybir.dt.bfloat16
AF = mybir.ActivationFunctionType


class _Op:
    __slots__ = ("idx", "eng", "fn", "deps", "is_dma", "dsem", "dma_val", "needs_sig", "sigval", "waits")


class _DSem:
    def __init__(self, sem, name, serial=False):
        self.sem = sem
        self.name = name
        self.count = 0
        self.serial = serial
        self.last = None


class Prog:
    def __init__(self, nc):
        self.nc = nc
        self.ops = []
        self.last_writer = {}
        self.readers = {}
        self.stack = contextlib.ExitStack()
        self.eng_sems = {}
        for e in ("pe", "act", "dve", "pool"):
            self.eng_sems[e] = self.stack.enter_context(nc.semaphore("s_" + e))
        self.dsems = []
        self.final = []

    def sbuf(self, name, shape, dtype):
        return self.stack.enter_context(self.nc.sbuf_tensor(name, list(shape), dtype))

    def psum(self, name, shape, dtype):
        return self.stack.enter_context(self.nc.psum_tensor(name, list(shape), dtype))

    def dsem(self, name, serial=False):
        d = _DSem(self.stack.enter_context(self.nc.semaphore("d_" + name)), name, serial)
        self.dsems.append(d)
        return d

    def _mk(self, eng, fn, reads, writes):
        op = _Op()
        op.idx = len(self.ops)
        op.eng = eng
        op.fn = fn
        op.is_dma = False
        op.dsem = None
        op.dma_val = 0
        op.needs_sig = False
        op.sigval = 0
        deps = set()
        for r in reads:
            if r in self.last_writer:
                deps.add(self.last_writer[r])
        for w in writes:
            if w in self.last_writer:
                deps.add(self.last_writer[w])
            deps |= self.readers.get(w, set())
        op.deps = deps
        for r in reads:
            self.readers.setdefault(r, set()).add(op.idx)
        for w in writes:
            self.last_writer[w] = op.idx
            self.readers[w] = set()
        self.ops.append(op)
        return op

    def add(self, eng, fn, reads=(), writes=()):
        return self._mk(eng, fn, reads, writes)

    def fence(self):
        last = {}
        for op in self.ops:
            if op.fn is None:
                continue
            if op.is_dma:
                last[("d", id(op.dsem))] = op.idx
            else:
                last[("e", op.eng)] = op.idx
        deps = set(last.values())
        for eng in ("sp", "act", "dve", "pool", "pe"):
            op = self._mk(eng, None, (), ())
            op.deps = set(deps)

    def dma(self, fn, dsem, reads=(), writes=(), queue="sp"):
        op = self._mk(queue, fn, reads, writes)
        op.is_dma = True
        op.dsem = dsem
        if dsem.serial and dsem.last is not None:
            op.deps.add(dsem.last)
        dsem.last = op.idx
        dsem.count += 1
        op.dma_val = 16 * dsem.count
        return op

    def finish(self, final_dsems=None):
        ops = self.ops
        for op in ops:
            for d in op.deps:
                dop = ops[d]
                if dop.is_dma:
                    continue
                if dop.eng == "pe" and op.eng == "pe" and not op.is_dma:
                    continue
                dop.needs_sig = True
        cnt = {}
        for op in ops:
            if (not op.is_dma) and op.needs_sig:
                cnt[op.eng] = cnt.get(op.eng, 0) + 1
                op.sigval = cnt[op.eng]
        known = {}
        streams = {}
        for op in ops:
            kn = known.setdefault(op.eng, {})
            waits = {}
            for d in op.deps:
                dop = ops[d]
                if dop.is_dma:
                    ch, val = ("d", id(dop.dsem)), dop.dma_val
                    sem = dop.dsem.sem
                else:
                    if dop.eng == "pe" and op.eng == "pe" and not op.is_dma:
                        continue
                    ch, val = ("e", dop.eng), dop.sigval
                    sem = self.eng_sems[dop.eng]
                if kn.get(ch, 0) >= val:
                    continue
                if ch not in waits or waits[ch][1] < val:
                    waits[ch] = (sem, val)
            for ch, (sem, val) in waits.items():
                kn[ch] = val
            op.waits = list(waits.values())
            streams.setdefault(op.eng, []).append(op)
        if final_dsems is None:
            final_dsems = self.dsems
        finals = [(d.sem, 16 * d.count) for d in final_dsems if d.count > 0]
        eng_sems = self.eng_sems
        self.n_ops = {k: len(v) for k, v in streams.items()}

        def emit(name, e, tail=False):
            for op in streams.get(name, []):
                for sem, val in op.waits:
                    e.wait_ge(sem, val)
                if op.fn is None:
                    continue
                ins = op.fn(e)
                if op.is_dma:
                    ins.then_inc(op.dsem.sem, 16)
                elif op.needs_sig:
                    ins.then_inc(eng_sems[name], 1)
            if tail:
                for sem, val in finals:
                    e.wait_ge(sem, val)

        with self.nc.Block() as block:
            @block.sync
            def _(e):
                emit("sp", e, tail=True)

            @block.scalar
            def _(e):
                emit("act", e)

            @block.vector
            def _(e):
                emit("dve", e)

            @block.gpsimd
            def _(e):
                emit("pool", e)

            @block.tensor
            def _(e):
                emit("pe", e)
        self.stack.close()


D = 1024
NH = 8
NKV = 2
HD = 128
DFF = 2816
NFB = DFF // 128
INW = 4608
EPS = 1e-6
ATTN_SCALE = HD ** -0.5
NSLOT = 64
NRING = 6
LN_HALF = float(np.log(0.5))
NEG = -30000.0

P_GM, P_GF, P_CW, P_CB, P_BA, P_BX, P_LAM, P_BG, P_FW, P_FB, P_GQ, P_GK, P_SINK = (
    0, 8, 16, 48, 56, 64, 72, 80, 96, 162, 184, 185, 186)
NPRM = 194
ST_HU, ST_H, ST_FH = 0, 72, 96
NST = 228

SL_U, SL_Q, SL_K, SL_V, SL_M, SL_O, SL_UP, SL_DN = 0, 4, 8, 9, 10, 26, 30, 52


class _Stop(Exception):
    pass


_DBG_STOP = None


def _chk(level):
    if _DBG_STOP is not None and level >= _DBG_STOP:
        raise _Stop()


def build_program(T, n_cores_hint=8):
    assert T % 512 == 0
    NT = T // 512
    nc = bass.Bass("TRN2", target_bir_lowering=False)

    def din(name, shape, dt=F32):
        return nc.dram_tensor(name, list(shape), dt, kind="ExternalInput").ap()

    def dout(name, shape, dt=F32):
        return nc.dram_tensor(name, list(shape), dt, kind="ExternalOutput").ap()

    xp = din("xp", [T, D])
    xs = din("xs", [32, D])
    prm_d = din("prm", [128, NPRM])
    sst_d = din("sst", [128, 152])
    ck_d = din("ck", [2, 128, 256])
    cv_d = din("cv", [2, 128, 256])
    w_in_d = din("w_in", [D, INW])
    ga_d = din("gate_a", [8, 128, 128])
    gx_d = din("gate_x", [8, 128, 128])
    wrp_d = din("w_rnn_proj", [D, D])
    wap_d = din("w_attn_proj", [D, D])
    wout_d = din("w_out", [D, D])
    wup_d = din("w_up", [D, 2 * DFF])
    wdn_d = din("w_down", [DFF, D])

    yp = dout("yp", [T, D])
    ys = dout("ys", [32, D])
    st_o = dout("st_o", [128, NST])
    kp_o = dout("kp", [128, 256])
    vp_o = dout("vp", [128, 256])
    ks_o = dout("ks", [2, 128, 256])
    vs_o = dout("vs", [2, 128, 256])

    wsc = nc.dram_tensor("wsc", [NSLOT, 128, 2048], BF16).ap()

    P = Prog(nc)
    xt = P.sbuf("xt", [128, 2, 4, D], F32)
    ss = P.sbuf("ss", [128, 8], F32)
    rs = P.sbuf("rs", [128, 8], F32)
    xn = P.sbuf("xn", [128, 2, D], BF16)
    hTa = P.sbuf("hTa", [128, 8, 512], BF16)
    hTb = P.sbuf("hTb", [128, 8, 512], BF16)
    tl = P.sbuf("tl", [128, 19, 512], F32)
    tlb = P.sbuf("tlb", [128, 6, 512], BF16)
    yr = P.sbuf("yr", [128, 8, 512], BF16)
    ya = P.sbuf("ya", [128, 8, 512], BF16)
    mixed = P.sbuf("mixed", [128, 8, 512], BF16)
    qT = P.sbuf("qT", [128, 8, 512], BF16)
    kT = P.sbuf("kT", [128, 2, 640], BF16)
    kTc = P.sbuf("kTc", [128, 2, 2, 128], BF16)
    vb = P.sbuf("vb", [128, 5, 2, 128], BF16)
    vc = P.sbuf("vc", [128, 2, 2, 128], BF16)
    btab = P.sbuf("btab", [128, 2, 8, 128], BF16)
    pT = P.sbuf("pT", [128, 4, 512], BF16)
    yat = P.sbuf("yat", [128, 2, D], BF16)
    actb = P.sbuf("actb", [128, NFB, 512], BF16)
    ring = P.sbuf("ring", [128, NRING, 2048], BF16)
    prm = P.sbuf("prm_s", [128, NPRM], F32)
    dv = P.sbuf("dv", [128, 64], F32)
    st = P.sbuf("st", [128, NST], F32)
    gwa = P.sbuf("gwa", [128, 8, 128], BF16)
    gwx = P.sbuf("gwx", [128, 8, 128], BF16)
    identf = P.sbuf("identf", [128, 128], F32)
    identb = P.sbuf("identb", [128, 128], BF16)
    onesb = P.sbuf("onesb", [128, 128], BF16)
    kvo = P.sbuf("kvo", [128, 2, 256], F32)
    kvs = P.sbuf("kvs", [128, 2, 2, 256], F32)
    smal = P.sbuf("smal", [128, 16], F32)
    act32 = actb[:, 0:16, :].rearrange("p a b -> p (a b)").bitcast(F32).rearrange("p (a b) -> p a b", a=8)
    NSTG = 4
    stg_f = [actb[:, 4 * i:4 * i + 4, :].rearrange("p a b -> p (a b)").bitcast(F32) for i in range(NSTG)]
    stg_b = [yr[:, 2 * i:2 * i + 2, :].rearrange("p a b -> p (a b)") for i in range(NSTG)]
    stg_f2 = [actb[:, 4 * i:4 * i + 4, :].rearrange("p a b -> p (a b)").bitcast(F32) for i in range(3)]
    stg_b2 = [actb[:, 12 + 2 * i:14 + 2 * i, :].rearrange("p a b -> p (a b)") for i in range(3)]
    gstg = tl[:, 2:4, :].rearrange("p a b -> p (a b)")

    psA = [P.psum(f"psA{i}", [128, 512], F32) for i in range(4)]
    psB = [P.psum(f"psB{i}", [128, 512], F32) for i in range(2)]
    psC = P.psum("psC", [128, 1024], F32)
    banks8 = [(psA[0], ("psA", 0)), (psA[1], ("psA", 1)), (psA[2], ("psA", 2)), (psA[3], ("psA", 3)),
              (psC[:, 0:512], ("psC", 0)), (psC[:, 512:1024], ("psC", 1)), (psB[0], ("psB", 0)), (psB[1], ("psB", 1))]
    cnt = {"A": 0, "B": 0, "tf": 0, "tb": 0, "pT": 0, "ring": 0, "xn": 0, "yat": 0}

    def bankA():
        i = cnt["A"] % 4
        cnt["A"] += 1
        return psA[i], ("psA", i)

    def bankB():
        i = cnt["B"] % 2
        cnt["B"] += 1
        return psB[i], ("psB", i)

    def tmpf():
        i = cnt["tf"] % 19
        cnt["tf"] += 1
        return tl[:, i, :], ("tl", i)

    def tmpb():
        i = cnt["tb"] % 6
        cnt["tb"] += 1
        return tlb[:, i, :], ("tlb", i)

    d_const = P.dsem("const", serial=True)
    d_x = [P.dsem("x0"), P.dsem("x1")]
    d_ring = [P.dsem(f"ring{i}") for i in range(NRING)]
    d_stg = [P.dsem(f"stg{i}") for i in range(7)]
    d_wst = [P.dsem(f"wst{i}") for i in range(7)]
    d_y = [P.dsem("y0"), P.dsem("y1")]
    d_fin = P.dsem("fin")
    d_cp = P.dsem("cp")

    def prmc(c0, n=1):
        return prm[:, c0:c0 + n]

    P.dma(lambda e: e.dma_start(out=prm[:, :], in_=prm_d[:, :]), d_const, writes=["prm"])
    P.dma(lambda e: e.dma_start(out=st[:, ST_HU + 24:ST_HU + 72], in_=sst_d[:, 0:48]), d_const, writes=["st_hu"])
    P.dma(lambda e: e.dma_start(out=st[:, ST_H + 8:ST_H + 24], in_=sst_d[:, 48:64]), d_const, writes=["st_h"])
    P.dma(lambda e: e.dma_start(out=st[:, ST_FH + 44:ST_FH + 132], in_=sst_d[:, 64:152]), d_const, writes=["st_fh"])
    P.add("dve", lambda e: e.memset(st[:, ST_HU:ST_HU + 24], 0.0), writes=["st_hu"])
    P.add("dve", lambda e: e.memset(st[:, ST_H:ST_H + 8], 0.0), writes=["st_h"])
    P.add("dve", lambda e: e.memset(st[:, ST_FH:ST_FH + 44], 0.0), writes=["st_fh"])
    P.add("pool", lambda e: e.iota(identf[:, :], pattern=[[1, 128]], base=0, channel_multiplier=-1,
                                    allow_small_or_imprecise_dtypes=True), writes=["identf"])
    P.add("dve", lambda e: e.tensor_scalar(out=identf[:, :], in0=identf[:, :], scalar1=0.0, scalar2=None, op0=ALU.is_equal),
          reads=["identf"], writes=["identf"])
    P.add("dve", lambda e: e.tensor_copy(out=identb[:, :], in_=identf[:, :]), reads=["identf"], writes=["identb"])
    P.add("dve", lambda e: e.memset(onesb[:, :], 1.0), writes=["onesb"])
    btf = act32[:, 0:4, :].rearrange("p a b -> p (a b)").rearrange("p (k h q) -> p k h q", k=2, h=8)
    btf2 = act32[:, 4:8, :].rearrange("p a b -> p (a b)").rearrange("p (k h q) -> p k h q", k=2, h=8)
    P.add("pool", lambda e: e.iota(btf, pattern=[[-128, 2], [0, 8], [1, 128]], base=128, channel_multiplier=-1,
                                    allow_small_or_imprecise_dtypes=True), writes=["btf"])
    P.add("dve", lambda e: e.tensor_scalar(out=btf2, in0=btf, scalar1=-1.0, scalar2=None, op0=ALU.mult),
          reads=["btf"], writes=["btf2"])
    P.add("dve", lambda e: e.tensor_tensor(out=btf, in0=btf, in1=btf2, op=ALU.max), reads=["btf", "btf2"], writes=["btf"])
    for h in range(8):
        P.add("dve", lambda e, h=h: e.tensor_scalar(out=btf[:, :, h, :], in0=btf[:, :, h, :], scalar1=-(2.0 ** -(h + 1)),
                                                    scalar2=None, op0=ALU.mult), reads=["btf"], writes=["btf"])
    P.add("dve", lambda e: e.memset(btf[64:128, 1, :, 0:64], NEG), reads=["btf"], writes=["btf"])
    P.add("dve", lambda e: e.memset(btf[0:64, 0, :, 64:128], NEG), reads=["btf"], writes=["btf"])
    P.add("dve", lambda e: e.tensor_copy(out=btab[:, :, :, :], in_=btf), reads=["btf"], writes=["btab"])
    DV_CH, DV_C, DV_NBA, DV_NBX, DV_NBG, DV_GQS, DV_ESK, DV_MH = 0, 8, 16, 24, 32, 48, 49, 57
    DV_CF = DV_C
    P.add("act", lambda e: e.activation(out=dv[:, DV_CH:DV_CH + 8], in_=prmc(P_LAM, 8), func=AF.Exp, scale=-1.0),
          reads=["prm"], writes=["dv_c"])
    P.add("act", lambda e: e.activation(out=dv[:, DV_CH:DV_CH + 8], in_=dv[:, DV_CH:DV_CH + 8], func=AF.Ln, bias=1.0),
          reads=["dv_c"], writes=["dv_c"])
    P.add("dve", lambda e: e.tensor_scalar(out=dv[:, DV_CF:DV_CF + 8], in0=dv[:, DV_CH:DV_CH + 8], scalar1=-8.0, scalar2=None,
                                           op0=ALU.mult), reads=["dv_c"], writes=["dv_cf"])
    P.add("dve", lambda e: e.tensor_scalar(out=dv[:, DV_CH:DV_CH + 8], in0=dv[:, DV_CH:DV_CH + 8], scalar1=-16.0, scalar2=None,
                                           op0=ALU.mult), reads=["dv_c", "dv_cf"], writes=["dv_c"])
    P.add("dve", lambda e: e.tensor_scalar(out=dv[:, DV_NBA:DV_NBA + 16], in0=prmc(P_BA, 16), scalar1=-1.0, scalar2=None,
                                           op0=ALU.mult), reads=["prm"], writes=["dv_hb"])
    P.add("dve", lambda e: e.tensor_scalar(out=dv[:, DV_NBG:DV_NBG + 16], in0=prmc(P_BG, 16), scalar1=-1.0, scalar2=None,
                                           op0=ALU.mult), reads=["prm"], writes=["dv_hbg"])
    P.add("dve", lambda e: e.tensor_scalar(out=dv[:, DV_GQS:DV_GQS + 1], in0=prmc(P_GQ, 1), scalar1=ATTN_SCALE, scalar2=None,
                                           op0=ALU.mult), reads=["prm"], writes=["dv_gqs"])
    P.add("act", lambda e: e.activation(out=dv[:, DV_ESK:DV_ESK + 8], in_=prmc(P_SINK, 8), func=AF.Exp),
          reads=["prm"], writes=["dv_esk"])
    P.add("dve", lambda e: e.memset(dv[:, DV_MH:DV_MH + 4], -0.5), writes=["dv_mh"])
    CONSTS = ["prm", "dv_c", "dv_cf", "dv_hb", "dv_hbg", "dv_gqs", "dv_esk", "dv_mh"]
    for src, dst, nm in ((ga_d, gwa, "gwa"), (gx_d, gwx, "gwx")):
        P.dma(lambda e, src=src: e.dma_start(out=gstg.rearrange("p (n d) -> p n d", n=8),
                                             in_=src.rearrange("n c d -> c n d")), d_const, writes=[("tl", 2), ("tl", 3)])
        P.add("dve", lambda e, dst=dst: e.tensor_copy(out=dst[:, :, :], in_=gstg.rearrange("p (n d) -> p n d", n=8)),
              reads=[("tl", 2), ("tl", 3)], writes=[nm])
    for s in range(2):
        kc_f = tl[:, 0, 0:256]
        vc_f = tl[:, 1, 0:256]
        P.dma(lambda e, s=s: e.dma_start(out=kc_f, in_=ck_d[s]), d_const, writes=[("tl", 0)])
        P.dma(lambda e, s=s: e.dma_start(out=vc_f, in_=cv_d[s]), d_const, writes=[("tl", 1)])
        P.add("dve", lambda e, s=s: e.tensor_copy(out=vc[:, s, :, 0:128], in_=vc_f.rearrange("p (g d) -> p g d", g=2)),
              reads=[("tl", 1)], writes=[("vc", s)])
        for g in range(2):
            bk, bkk = bankB()
            P.add("pe", lambda e, g=g, bk=bk: e.transpose(bk[:, 0:128], kc_f[:, g * 128:(g + 1) * 128], identf[:, :]),
                  reads=[("tl", 0), "identf"], writes=[bkk])
            P.add("act", lambda e, s=s, g=g, bk=bk: e.activation(out=kTc[:, s, g, :], in_=bk[:, 0:128], func=AF.Copy),
                  reads=[bkk], writes=[("kTc", s)])
        P.dma(lambda e, s=s: e.dma_start(out=ks_o[s, 0:112, :], in_=ck_d[s, 16:128, :]), d_cp)
        P.dma(lambda e, s=s: e.dma_start(out=vs_o[s, 0:112, :], in_=cv_d[s, 16:128, :]), d_cp)

    P.fence()
    pp = {"i": 0}
    _pre_ok = not (_DBG_STOP is not None and _DBG_STOP <= 1)

    def prepass_chunk(src, kc, c0, ncol, dest_fn, scale, late=False):
        i = pp["i"]
        pp["i"] += 1
        if late:
            b = NSTG + i % 3
            sf, sb = stg_f2[i % 3], stg_b2[i % 3]
            depth = 2
        else:
            b = i % NSTG
            sf, sb = stg_f[b], stg_b[b]
            depth = NSTG - 1
        P.dma(lambda e: e.dma_start(out=sf[:, 0:ncol], in_=src[kc * 128:(kc + 1) * 128, c0:c0 + ncol]), d_stg[b],
              writes=[("stgf", b)])
        eng = ("dve", "act", "pool")[i % 3]
        rd = [("stgf", b)] + CONSTS
        if eng == "dve":
            P.add("dve", lambda e: e.tensor_scalar(out=sb[:, 0:ncol], in0=sf[:, 0:ncol], scalar1=scale, scalar2=None, op0=ALU.mult),
                  reads=rd, writes=[("stgb", b)])
        elif eng == "act":
            P.add("act", lambda e: e.activation(out=sb[:, 0:ncol], in_=sf[:, 0:ncol], func=AF.Copy, scale=scale),
                  reads=rd, writes=[("stgb", b)])
        else:
            P.add("pool", lambda e: e.tensor_scalar(out=sb[:, 0:ncol], in0=sf[:, 0:ncol], scalar1=scale, scalar2=1.0,
                                                    op0=ALU.mult, op1=ALU.mult), reads=rd, writes=[("stgb", b)])
        ng = ncol // 256
        slots = [dest_fn(g) for g in range(ng)]
        kcl = slots[0][1]
        s0 = slots[0][0]
        step = (slots[1][0] - s0) if ng > 1 else 1
        dst = wsc[s0:s0 + step * (ng - 1) + 1:step].rearrange("s p c -> p s c")[:, :, kcl * 256:(kcl + 1) * 256]
        pp.setdefault("pend", []).append(
            lambda: P.dma(lambda e: e.dma_start(out=dst, in_=sb[:, 0:ncol].rearrange("p (s c) -> p s c", s=ng)), d_wst[b],
                          reads=[("stgb", b)], writes=[("wsc", sl_[0]) for sl_ in slots]))
        while len(pp["pend"]) > depth:
            pp["pend"].pop(0)()

    for kc in range(8):
        g = prmc(P_GM + kc, 1)
        prepass_chunk(w_in_d, kc, 0, 1024, lambda i, kc=kc: (SL_U + i, kc), g)
        prepass_chunk(w_in_d, kc, 1024, 1024, lambda i, kc=kc: (SL_Q + i, kc), g)
        prepass_chunk(w_in_d, kc, 2048, 512, lambda i, kc=kc: (SL_K + i, kc), g)
        prepass_chunk(w_in_d, kc, 2560, 1024, lambda i, kc=kc: (SL_M + 4 * i + 2, kc), g)
        prepass_chunk(w_in_d, kc, 3584, 1024, lambda i, kc=kc: (SL_M + 4 * i + 3, kc), g)
        prepass_chunk(wrp_d, kc, 0, 1024, lambda i, kc=kc: (SL_M + 4 * i + 0, kc), 1.0)
        prepass_chunk(wap_d, kc, 0, 1024, lambda i, kc=kc: (SL_M + 4 * i + 1, kc), 1.0)
        prepass_chunk(wout_d, kc, 0, 1024, lambda i, kc=kc: (SL_O + i, kc), 1.0)

    def prepass_ffn_gen():
        for kc in range(8):
            gf = prmc(P_GF + kc, 1)
            for half in range(2):
                for (c0, ncol, p0) in ((0, 1024, 0), (1024, 1024, 4), (2048, 768, 8)):
                    prepass_chunk(wup_d, kc, half * DFF + c0, ncol,
                                  lambda i, kc=kc, half=half, p0=p0: (SL_UP + 2 * (p0 + i) + half, kc), gf, late=True)
                    yield
        for kc in range(NFB):
            prepass_chunk(wdn_d, kc, 0, 1024, lambda i, kc=kc: (SL_DN + 3 * i + kc // 8, kc % 8), 1.0, late=True)
            yield
        while pp.get("pend"):
            pp["pend"].pop(0)()
        yield

    while pp.get("pend"):
        pp["pend"].pop(0)()
    P.fence()

    ring_held = [False] * NRING

    def load_slot(slot):
        for k in range(NRING):
            r = (cnt["ring"] + k) % NRING
            if not ring_held[r]:
                break
        else:
            raise RuntimeError("weight ring exhausted")
        cnt["ring"] = r + 1
        ring_held[r] = True
        nv = 6 * 256 if (slot >= SL_DN and (slot - SL_DN) % 3 == 2) else 2048
        P.dma(lambda e: e.dma_start(out=ring[:, r, 0:nv], in_=wsc[slot, :, 0:nv]), d_ring[r], reads=[("wsc", slot)],
              writes=[("ring", r)])
        return ring[:, r, :].rearrange("p (k c) -> p k c", k=8), ("ring", r), r

    def release(sl):
        ring_held[sl[2]] = False

    def mm_fm(out_ap, okey, wv, wkey, col0, rhs_t, rkeys, W, nk=8, extra_reads=()):
        for kc in range(nk):
            P.add("pe", lambda e, kc=kc: e.matmul(out_ap, lhsT=wv[:, kc, col0:col0 + 128], rhs=rhs_t[:, kc, 0:W],
                                                  start=(kc == 0), stop=(kc == nk - 1)),
                  reads=[wkey] + list(rkeys) + list(extra_reads), writes=[okey])

    def norm_transpose(xs_i, rows, nsub, sscol, dst, dkey):
        for j in range(nsub):
            P.add("act", lambda e, j=j: e.activation(out=xn[0:rows, j % 2, :], in_=xt[0:rows, xs_i, j, :], func=AF.Square,
                                                      accum_out=ss[0:rows, sscol + j:sscol + j + 1]),
                  reads=[("xt", xs_i, j)], writes=[("xn", j % 2), ("ss", sscol + j)])
            yield
        P.add("pool", lambda e: e.tensor_scalar(out=rs[0:rows, sscol:sscol + nsub], in0=ss[0:rows, sscol:sscol + nsub],
                                                scalar1=1.0 / D, scalar2=EPS, op0=ALU.mult, op1=ALU.add),
              reads=[("ss", sscol + j) for j in range(nsub)], writes=[("rs", sscol)])
        P.add("pool", lambda e: e.tensor_tensor(out=rs[0:rows, sscol:sscol + nsub], in0=rs[0:rows, sscol:sscol + nsub],
                                                in1=dv[0:rows, DV_MH:DV_MH + nsub], op=ALU.pow),
              reads=[("rs", sscol), "dv_mh"], writes=[("rs", sscol)])
        yield
        for j in range(nsub):
            r = cnt["xn"] % 2
            cnt["xn"] += 1
            P.add("dve", lambda e, j=j, r=r: e.tensor_scalar(out=xn[0:rows, r, :], in0=xt[0:rows, xs_i, j, :],
                                                             scalar1=rs[0:rows, sscol + j:sscol + j + 1], scalar2=None,
                                                             op0=ALU.mult),
                  reads=[("xt", xs_i, j), ("rs", sscol)], writes=[("xn", r)])
            yield
            bk, bkk = bankB()
            bkb = bk[:, :].bitcast(BF16)
            for kc in range(8):
                P.add("pe", lambda e, kc=kc, r=r, bkb=bkb: e.transpose(bkb[:, kc * 128:kc * 128 + rows],
                                                                       xn[0:rows, r, kc * 128:(kc + 1) * 128],
                                                                       identb[0:rows, 0:rows]),
                      reads=[("xn", r), "identb"], writes=[bkk])
            yield
            P.add("act", lambda e, j=j, bkb=bkb: e.activation(
                out=dst[:, :, j * 128:j * 128 + rows],
                in_=bkb.rearrange("p (k t) -> p k t", k=8)[:, :, 0:rows], func=AF.Copy),
                reads=[bkk], writes=[dkey])
            yield

    def conv_taps(ps, pskey, acc, acckey, W, segs, ntap, wcol_fn, bcol, halo_fn, halokey_fn):
        P.add("act", lambda e: e.activation(out=acc[:, 0:W], in_=ps[:, 0:W], func=AF.Identity, bias=bcol,
                                            scale=wcol_fn(ntap - 1)),
              reads=[pskey] + CONSTS, writes=[acckey])
        nh = ntap - 1
        for (c0, L, sidx) in segs:
            hal = halo_fn(sidx)
            for s in range(1, ntap):
                wj = wcol_fn(ntap - 1 - s)
                P.add("dve", lambda e, c0=c0, L=L, s=s, wj=wj: e.scalar_tensor_tensor(
                    out=acc[:, c0 + s:c0 + L], in0=ps[:, c0:c0 + L - s], scalar=wj, in1=acc[:, c0 + s:c0 + L],
                    op0=ALU.mult, op1=ALU.add), reads=[pskey, acckey] + CONSTS, writes=[acckey])
                P.add("dve", lambda e, c0=c0, s=s, wj=wj, hal=hal: e.scalar_tensor_tensor(
                    out=acc[:, c0:c0 + s], in0=hal[:, nh - s:nh], scalar=wj, in1=acc[:, c0:c0 + s],
                    op0=ALU.mult, op1=ALU.add), reads=[halokey_fn(sidx), acckey] + CONSTS, writes=[acckey])
            P.add("dve", lambda e, c0=c0, L=L, hal=hal: e.tensor_scalar(out=hal[:, 0:nh], in0=ps[:, c0 + L - nh:c0 + L],
                                                                       scalar1=1.0, scalar2=None, op0=ALU.mult),
                  reads=[pskey], writes=[halokey_fn(sidx)])

    def chain(gens):
        for g in gens:
            yield from g

    def par(gens):
        gens = list(gens)
        while gens:
            for g in list(gens):
                try:
                    next(g)
                except StopIteration:
                    gens.remove(g)
            yield

    def run(gen):
        for _ in gen:
            pass

    def tm_proj(xs_i, rows, nsub, slot_ids, nkc, lhs_t, lkeys):
        for cg in range(4):
            sl = [load_slot(s) for s in slot_ids(cg)]
            for j in range(nsub):
                for kc in range(nkc):
                    wv, wkey, _r = sl[kc // 8]
                    P.add("pe", lambda e, j=j, kc=kc, wv=wv: e.matmul(
                        psC[0:rows, j * 256:(j + 1) * 256], lhsT=lhs_t[:, kc, j * 128:j * 128 + rows], rhs=wv[:, kc % 8, :],
                        start=(kc == 0), stop=(kc == nkc - 1)), reads=[wkey] + list(lkeys), writes=[("psC", j // 2)])
                yield
            for s_ in sl:
                release(s_)
            for j in range(nsub):
                P.add("dve", lambda e, j=j, cg=cg: e.tensor_tensor(
                    out=xt[0:rows, xs_i, j, cg * 256:(cg + 1) * 256], in0=psC[0:rows, j * 256:(j + 1) * 256],
                    in1=xt[0:rows, xs_i, j, cg * 256:(cg + 1) * 256], op=ALU.add),
                    reads=[("psC", j // 2), ("xt", xs_i, j)], writes=[("xt", xs_i, j)])
            yield

    HT = ["hTa"]

    def stageA_gen(ti, xs_i, W, rows, nsub, segs, first_tile, is_sample):
        yield from norm_transpose(xs_i, rows, nsub, 0, hTa, "hTa")

    def front_gen(ti, xs_i, W, rows, nsub, segs, first_tile, is_sample, gate=None, wd_gate=None):
        slot_cache = {}
        slot_uses = {}
        nvj = nsub if not is_sample else len(segs)
        for k in range(4):
            slot_uses[SL_U + k] = 2
            slot_uses[SL_Q + k] = 2
        slot_uses[SL_K] = 2
        slot_uses[SL_V] = nvj

        def use_slot(sid):
            if sid not in slot_cache:
                slot_cache[sid] = load_slot(sid)
            return slot_cache[sid][0], slot_cache[sid][1]

        def done_slot(sid):
            slot_uses[sid] -= 1
            if slot_uses[sid] == 0:
                release(slot_cache[sid])

        done_head = {}
        done_tail = {}

        def rnn_head(n):
            lane = n % 2
            sset = n % 4
            bU, bUk = banks8[lane]
            bG, bGk = banks8[lane]
            acc, acck = tl[:, 4 * sset + 0, :], ("tl", 4 * sset + 0)
            t1, t1k = tl[:, 4 * sset + 1, :], ("tl", 4 * sset + 1)
            t2, t2k = tl[:, 4 * sset + 2, :], ("tl", 4 * sset + 2)
            ta, tak = tl[:, 4 * sset + 3, :], ("tl", 4 * sset + 3)
            ucb, ucbk = tlb[:, sset, :], ("tlb", sset)
            while n >= 4 and not done_tail.get(n - 4):
                yield
            wv, wkey = use_slot(SL_U + n // 2)
            mm_fm(bU[:, 0:W], bUk, wv, wkey, (n % 2) * 128, hTa, HT, W)
            done_slot(SL_U + n // 2)
            yield
            conv_taps(bU, bUk, acc, acck, W, segs, 4, lambda j, n=n: prmc(P_CW + j * 8 + n, 1), prmc(P_CB + n, 1),
                      lambda sidx, n=n: st[:, ST_HU + sidx * 24 + n * 3:ST_HU + sidx * 24 + n * 3 + 3],
                      lambda sidx, n=n: ("st_hu", sidx, n))
            yield
            P.add("dve", lambda e: e.tensor_copy(out=ucb[:, 0:W], in_=acc[:, 0:W]), reads=[acck], writes=[ucbk])
            yield
            P.add("pe", lambda e: e.matmul(bG[:, 0:W], lhsT=gwa[:, n, :], rhs=ucb[:, 0:W], start=True, stop=True),
                  reads=["gwa", ucbk], writes=[bGk])
            yield
            P.add("act", lambda e: e.activation(out=t1[:, 0:W], in_=bG[:, 0:W], func=AF.Exp, scale=-1.0,
                                                bias=dv[:, DV_NBA + n:DV_NBA + n + 1]), reads=[bGk] + CONSTS, writes=[t1k])
            yield
            P.add("pe", lambda e: e.matmul(bG[:, 0:W], lhsT=gwx[:, n, :], rhs=ucb[:, 0:W], start=True, stop=True),
                  reads=["gwx", ucbk], writes=[bGk])
            yield
            P.add("act", lambda e: e.activation(out=t2[:, 0:W], in_=bG[:, 0:W], func=AF.Exp, scale=-1.0,
                                                bias=dv[:, DV_NBX + n:DV_NBX + n + 1]), reads=[bGk] + CONSTS, writes=[t2k])
            yield
            done_head[n] = True

        def rnn_tail(n):
            lane = n % 2
            sset = n % 4
            bU, bUk = banks8[lane]
            bG, bGk = banks8[lane]
            acc, acck = tl[:, 4 * sset + 0, :], ("tl", 4 * sset + 0)
            t1, t1k = tl[:, 4 * sset + 1, :], ("tl", 4 * sset + 1)
            t2, t2k = tl[:, 4 * sset + 2, :], ("tl", 4 * sset + 2)
            ta, tak = tl[:, 4 * sset + 3, :], ("tl", 4 * sset + 3)
            ucb, ucbk = tlb[:, sset, :], ("tlb", sset)
            while not done_head.get(n):
                yield
            for tt, ttk in ((t1, t1k), (t2, t2k)):
                P.add("act", lambda e, tt=tt: e.activation(out=tt[:, 0:W], in_=tt[:, 0:W], func=AF.Ln, bias=1.0),
                      reads=[ttk], writes=[ttk])
            yield
            for tt, ttk in ((t1, t1k), (t2, t2k)):
                P.add("act", lambda e, tt=tt: e.activation(out=tt[:, 0:W], in_=tt[:, 0:W], func=AF.Exp, scale=-1.0),
                      reads=[ttk], writes=[ttk])
            yield
            P.add("act", lambda e: e.activation(out=ta[:, 0:W], in_=t1[:, 0:W], func=AF.Exp, scale=dv[:, DV_C + n:DV_C + n + 1]),
                  reads=[t1k] + CONSTS, writes=[tak])
            P.add("dve", lambda e: e.tensor_tensor(out=t2[:, 0:W], in0=t2[:, 0:W], in1=acc[:, 0:W], op=ALU.mult),
                  reads=[t2k, acck], writes=[t2k])
            yield
            P.add("pool", lambda e: e.tensor_tensor(out=t1[:, 0:W], in0=ta[:, 0:W], in1=ta[:, 0:W], op=ALU.mult),
                  reads=[tak], writes=[t1k])
            yield
            P.add("act", lambda e: e.activation(out=t1[:, 0:W], in_=t1[:, 0:W], func=AF.Ln, bias=1.0, scale=-1.0),
                  reads=[t1k], writes=[t1k])
            yield
            P.add("act", lambda e: e.activation(out=t1[:, 0:W], in_=t1[:, 0:W], func=AF.Exp, scale=0.5), reads=[t1k], writes=[t1k])
            yield
            P.add("dve", lambda e: e.tensor_tensor(out=t2[:, 0:W], in0=t2[:, 0:W], in1=t1[:, 0:W], op=ALU.mult),
                  reads=[t1k, t2k], writes=[t2k])
            yield
            for (c0, L, sidx) in segs:
                hcol = st[:, ST_H + sidx * 8 + n:ST_H + sidx * 8 + n + 1]
                P.add("dve", lambda e, c0=c0, L=L, hcol=hcol: e.tensor_tensor_scan(
                    out=acc[:, c0:c0 + L], data0=ta[:, c0:c0 + L], data1=t2[:, c0:c0 + L], initial=hcol,
                    op0=ALU.mult, op1=ALU.add), reads=[tak, t2k, ("st_h", sidx, n)], writes=[acck])
                P.add("dve", lambda e, c0=c0, L=L, hcol=hcol: e.tensor_copy(out=hcol, in_=acc[:, c0 + L - 1:c0 + L]),
                      reads=[acck], writes=[("st_h", sidx, n)])
            yield
            P.add("pool", lambda e: e.tensor_copy(out=yr[:, n, 0:W], in_=acc[:, 0:W]), reads=[acck], writes=[("yr", n)])
            yield
            done_tail[n] = True

        qk_cnt = {"i": 0}

        def qk_gate():
            while gate is not None and not gate["go"]:
                yield

        def qk_unit(hb, ql):
            qps, qpk = banks8[2 + ql]
            sps, spk = banks8[6 + ql]
            sq, sqk = tlb[:, 4 + ql, :], ("tlb", 4 + ql)
            lt, ltk = tl[:, 16 + ql, :], ("tl", 16 + ql)
            if hb < 8:
                sid = SL_Q + hb // 2
                wv, wkey = use_slot(sid)
                col0 = (hb % 2) * 128
            else:
                sid = SL_K
                wv, wkey = use_slot(sid)
                col0 = (hb - 8) * 128
            mm_fm(qps[:, 0:W], qpk, wv, wkey, col0, hTa, HT, W)
            done_slot(sid)
            yield
            P.add("act", lambda e: e.activation(out=sq[:, 0:W], in_=qps[:, 0:W], func=AF.Square), reads=[qpk], writes=[sqk])
            yield
            P.add("pe", lambda e: e.matmul(sps[:, 0:W], lhsT=onesb[:, :], rhs=sq[:, 0:W], start=True, stop=True),
                  reads=[sqk, "onesb"], writes=[spk])
            yield
            P.add("act", lambda e: e.activation(out=lt[:, 0:W], in_=sps[:, 0:W], func=AF.Ln, bias=EPS, scale=1.0 / HD),
                  reads=[spk], writes=[ltk])
            yield
            P.add("act", lambda e: e.activation(out=lt[:, 0:W], in_=lt[:, 0:W], func=AF.Exp, scale=-0.5), reads=[ltk], writes=[ltk])
            yield
            if hb < 8:
                P.add("dve", lambda e: e.scalar_tensor_tensor(
                    out=qT[:, hb, 0:W], in0=qps[:, 0:W], scalar=dv[:, DV_GQS:DV_GQS + 1], in1=lt[:, 0:W],
                    op0=ALU.mult, op1=ALU.mult), reads=[qpk, ltk] + CONSTS, writes=[("qT", hb)])
            else:
                g = hb - 8
                P.add("dve", lambda e: e.scalar_tensor_tensor(
                    out=kT[:, g, 128:128 + W], in0=qps[:, 0:W], scalar=prmc(P_GK, 1), in1=lt[:, 0:W],
                    op0=ALU.mult, op1=ALU.mult), reads=[qpk, ltk] + CONSTS, writes=[("kT", 1)])
                outs = []
                if is_sample:
                    for (c0, L, sidx) in segs:
                        outs.append((c0, L, kvs[0:L, sidx - 1, 0, g * 128:(g + 1) * 128], ("kvs", sidx - 1, 0)))
                elif ti == NT - 1:
                    outs.append((W - 128, 128, kvo[:, 0, g * 128:(g + 1) * 128], ("kvo", 0)))
                for (c0, L, dst, dkey) in outs:
                    kf, kfk = tl[:, 18, ql * 256:(ql + 1) * 256], ("tl18", ql)
                    P.add("dve", lambda e, c0=c0, L=L: e.scalar_tensor_tensor(
                        out=kf[:, 0:L], in0=qps[:, c0:c0 + L], scalar=prmc(P_GK, 1), in1=lt[:, c0:c0 + L],
                        op0=ALU.mult, op1=ALU.mult), reads=[qpk, ltk] + CONSTS, writes=[kfk])
                    P.add("pe", lambda e, L=L: e.transpose(sps[0:L, 0:128], kf[:, 0:L], identf[:, :]),
                          reads=[kfk, "identf"], writes=[spk])
                    P.add("act", lambda e, dst=dst, L=L: e.activation(out=dst, in_=sps[0:L, 0:128], func=AF.Copy),
                          reads=[spk], writes=[dkey])
            yield

        def v_unit(job, ql):
            (j, c0, L, dst, dkey, fout) = job
            vps, vpk = banks8[2 + ql]
            wv, wkey = use_slot(SL_V)
            for kc in range(8):
                P.add("pe", lambda e, kc=kc: e.matmul(vps[0:L, 0:256], lhsT=hTa[:, kc, c0:c0 + L], rhs=wv[:, kc, 0:256],
                                                      start=(kc == 0), stop=(kc == 7)), reads=[wkey] + HT, writes=[vpk])
            done_slot(SL_V)
            yield
            P.add("act", lambda e: e.activation(out=dst, in_=vps[0:L, 0:256].rearrange("p (g d) -> p g d", g=2), func=AF.Copy),
                  reads=[vpk], writes=[dkey])
            if fout is not None:
                P.add("act", lambda e: e.activation(out=fout[0], in_=vps[0:L, 0:256], func=AF.Copy), reads=[vpk], writes=[fout[1]])
            yield

        if not is_sample:
            vjobs = [(j, j * 128, 128, vb[:, 1 + j, :, 0:128], ("vb", 1 + j),
                      (kvo[:, 1, :], ("kvo", 1)) if (ti == NT - 1 and j == nsub - 1) else None) for j in range(nsub)]
        else:
            vjobs = [(sidx - 1, c0, L, vb[0:L, sidx, :, 0:128], ("vb", sidx), (kvs[0:L, sidx - 1, 1, :], ("kvs", sidx - 1, 1)))
                     for (c0, L, sidx) in segs]


        if not is_sample:
            vjobs = [(j, j * 128, 128, vb[:, 1 + j, :, 0:128], ("vb", 1 + j),
                      (kvo[:, 1, :], ("kvo", 1)) if (ti == NT - 1 and j == nsub - 1) else None) for j in range(nsub)]
        else:
            vjobs = [(sidx - 1, c0, L, vb[0:L, sidx, :, 0:128], ("vb", sidx), (kvs[0:L, sidx - 1, 1, :], ("kvs", sidx - 1, 1)))
                     for (c0, L, sidx) in segs]
        qk_flags = {}

        def qk_done(ql):
            qk_flags[ql] = True
            yield

        yield from par([chain([rnn_head(n) for n in (0, 2, 4, 6)]),
                        chain([rnn_head(n) for n in (1, 3, 5, 7)]),
                        chain([rnn_tail(n) for n in (0, 4)]),
                        chain([rnn_tail(n) for n in (1, 5)]),
                        chain([rnn_tail(n) for n in (2, 6)]),
                        chain([rnn_tail(n) for n in (3, 7)]),
                        chain([qk_gate()] + [qk_unit(hb, 0) for hb in range(0, 10, 2)] + [v_unit(jb, 0) for jb in vjobs[0::2]]
                              + [qk_done(0)]),
                        chain([qk_gate()] + [qk_unit(hb, 1) for hb in range(1, 10, 2)] + [v_unit(jb, 1) for jb in vjobs[1::2]]
                              + [qk_done(1)]),
                        attn_gen(ti, xs_i, W, rows, nsub, segs, first_tile, is_sample, flags=qk_flags, gate=wd_gate)])

    def attn_gen(ti, xs_i, W, rows, nsub, segs, first_tile, is_sample, flags=None, gate=None):
        while (flags is not None and not (flags.get(0) and flags.get(1))) or (gate is not None and not gate["go"]):
            yield
        acnt = [0]
        if not is_sample:
            ajobs = []
            for j in range(nsub):
                kbs = []
                if not (first_tile and j == 0):
                    kbs.append((0, kT[:, :, j * 128:(j + 1) * 128], ("kT", 0 if j == 0 else 1), vb[:, j, :, :], ("vb", j), 128))
                kbs.append((1, kT[:, :, (j + 1) * 128:(j + 2) * 128], ("kT", 1), vb[:, j + 1, :, :], ("vb", j + 1), 128))
                ajobs.append((j * 128, 128, kbs))
        else:
            ajobs = []
            for (c0, L, sidx) in segs:
                kbs = [(0, kTc[:, sidx - 1, :, :], ("kTc", sidx - 1), vc[:, sidx - 1, :, :], ("vc", sidx - 1), 128),
                       (1, kT[:, :, 128 + c0:128 + c0 + L], ("kT", 1), vb[0:L, sidx, :, :], ("vb", sidx), L)]
                ajobs.append((c0, L, kbs))
        for (c0, nq, kbs) in ajobs:
            pts = {}
            for g in range(2):
                for (kb, kTv, kTk, vv, vk, nk) in kbs:
                    sps, spk = banks8[2 + acnt[0] % 2]
                    acnt[0] += 1
                    so2 = sps[0:nk, 0:4 * nq]
                    so = so2.rearrange("p (h q) -> p h q", h=4)
                    P.add("pe", lambda e, so2=so2, kTv=kTv, g=g, nk=nk, c0=c0, nq=nq: e.matmul(
                        so2, lhsT=kTv[:, g, 0:nk], rhs=qT[:, 4 * g:4 * g + 4, c0:c0 + nq], start=True, stop=False),
                        reads=[kTk] + [("qT", 4 * g + i) for i in range(4)], writes=[spk])
                    P.add("pe", lambda e, so2=so2, kb=kb, g=g, nk=nk, nq=nq: e.matmul(
                        so2, lhsT=identb[0:nk, 0:nk], rhs=btab[0:nk, kb, 4 * g:4 * g + 4, 0:nq], start=False, stop=True),
                        reads=["identb", "btab"], writes=[spk])
                    pi = cnt["pT"] % 4
                    cnt["pT"] += 1
                    po = pT[0:nk, pi, 0:4 * nq].rearrange("p (h q) -> p h q", h=4)
                    P.add("act", lambda e, po=po, so=so: e.activation(out=po, in_=so, func=AF.Exp), reads=[spk],
                          writes=[("pT", pi)])
                    pts[(g, kb)] = (po, ("pT", pi), vv, vk, nk)
                    yield
            dps, dpk = bankB()
            for h in range(8):
                g = h // 4
                lst = [pts[(g, kb)] for (kb, *_r) in kbs]
                for idx, (po, pk, vv, vk, nk) in enumerate(lst):
                    P.add("pe", lambda e, po=po, vv=vv, nk=nk, h=h, g=g, idx=idx, n=len(lst), nq=nq: e.matmul(
                        psC[0:nq, h * 128:(h + 1) * 128], lhsT=po[:, h % 4, :], rhs=vv[0:nk, g, 0:128],
                        start=(idx == 0), stop=(idx == n - 1)), reads=[pk, vk], writes=[("psC", h // 4)])
                    P.add("pe", lambda e, po=po, vv=vv, nk=nk, h=h, g=g, idx=idx, n=len(lst), nq=nq, dps=dps: e.matmul(
                        dps[0:nq, h:h + 1], lhsT=po[:, h % 4, :], rhs=onesb[0:nk, 0:1],
                        start=(idx == 0), stop=(idx == n - 1)), reads=[pk, "onesb"], writes=[dpk])
            yield
            P.add("dve", lambda e, dps=dps, nq=nq: e.tensor_tensor(out=smal[0:nq, 0:8], in0=dps[0:nq, 0:8],
                                                                   in1=dv[0:nq, DV_ESK:DV_ESK + 8], op=ALU.add),
                  reads=[dpk] + CONSTS, writes=["smal"])
            P.add("dve", lambda e, nq=nq: e.reciprocal(out=smal[0:nq, 8:16], in_=smal[0:nq, 0:8]), reads=["smal"],
                  writes=["smal2"])
            yi = cnt["yat"] % 2
            cnt["yat"] += 1
            for half in range(2):
                P.add("dve", lambda e, half=half, yi=yi, nq=nq: e.tensor_tensor(
                    out=yat[0:nq, yi, half * 512:(half + 1) * 512].rearrange("p (h d) -> p h d", h=4),
                    in0=psC[0:nq, half * 512:(half + 1) * 512].rearrange("p (h d) -> p h d", h=4),
                    in1=smal[0:nq, 8 + 4 * half:12 + 4 * half].unsqueeze(2).to_broadcast([nq, 4, 128]), op=ALU.mult),
                    reads=[("psC", half), "smal2"], writes=[("yat", yi)])
            yield
            bk, bkk = bankB()
            bkb = bk[:, :].bitcast(BF16)
            for h in range(8):
                P.add("pe", lambda e, h=h, yi=yi, bkb=bkb, nq=nq: e.transpose(bkb[:, h * 128:h * 128 + nq],
                                                                               yat[0:nq, yi, h * 128:(h + 1) * 128],
                                                                               identb[0:nq, 0:nq]),
                      reads=[("yat", yi), "identb"], writes=[bkk])
            P.add("act", lambda e, bkb=bkb, c0=c0, nq=nq: e.activation(
                out=ya[:, :, c0:c0 + nq], in_=bkb.rearrange("p (h t) -> p h t", h=8)[:, :, 0:nq], func=AF.Copy),
                reads=[bkk], writes=["ya"])
            yield
        if not is_sample:
            P.add("pool", lambda e: e.tensor_copy(out=kT[:, :, 0:128], in_=kT[:, :, 512:640]), reads=[("kT", 1)], writes=[("kT", 0)])
            P.add("pool", lambda e: e.tensor_copy(out=vb[:, 0, :, 0:128], in_=vb[:, 4, :, 0:128]), reads=[("vb", 4)],
                  writes=[("vb", 0)])
        yield

    def mid(ti, xs_i, W, rows, nsub, segs, first_tile, is_sample):
        YR = [("yr", n) for n in range(8)]
        for pr in range(4):
            s_rp = load_slot(SL_M + 4 * pr + 0)
            s_ap = load_slot(SL_M + 4 * pr + 1)
            s_gr = load_slot(SL_M + 4 * pr + 2)
            s_ga = load_slot(SL_M + 4 * pr + 3)
            for sub in range(2):
                m = 2 * pr + sub
                col0 = sub * 128
                p3, p3k = bankA()
                mm_fm(p3[:, 0:W], p3k, s_gr[0], s_gr[1], col0, hTa, HT, W)
                p4, p4k = bankA()
                mm_fm(p4[:, 0:W], p4k, s_ga[0], s_ga[1], col0, hTa, HT, W)
                p1, p1k = bankA()
                mm_fm(p1[:, 0:W], p1k, s_rp[0], s_rp[1], col0, yr, YR, W)
                p2, p2k = bankA()
                mm_fm(p2[:, 0:W], p2k, s_ap[0], s_ap[1], col0, ya, ["ya"], W)
                t1, t1k = tmpf()
                t2, t2k = tmpf()
                P.add("act", lambda e, m=m, t1=t1, p3=p3: e.activation(out=t1[:, 0:W], in_=p3[:, 0:W], func=AF.Exp,
                                                                        bias=dv[:, DV_NBG + m:DV_NBG + m + 1], scale=-1.0),
                      reads=[p3k] + CONSTS, writes=[t1k])
                P.add("act", lambda e, m=m, t2=t2, p4=p4: e.activation(out=t2[:, 0:W], in_=p4[:, 0:W], func=AF.Exp,
                                                                        bias=dv[:, DV_NBG + 8 + m:DV_NBG + 9 + m], scale=-1.0),
                      reads=[p4k] + CONSTS, writes=[t2k])
                for tt, ttk in ((t1, t1k), (t2, t2k)):
                    P.add("act", lambda e, tt=tt: e.activation(out=tt[:, 0:W], in_=tt[:, 0:W], func=AF.Ln, bias=1.0),
                          reads=[ttk], writes=[ttk])
                for tt, ttk in ((t1, t1k), (t2, t2k)):
                    P.add("act", lambda e, tt=tt: e.activation(out=tt[:, 0:W], in_=tt[:, 0:W], func=AF.Exp, scale=-1.0),
                          reads=[ttk], writes=[ttk])
                m1, m1k = tmpb()
                m2, m2k = tmpb()
                P.add("dve", lambda e, m1=m1, t1=t1, p1=p1: e.tensor_tensor(
                    out=m1[:, 0:W], in0=p1[:, 0:W], in1=t1[:, 0:W], op=ALU.mult), reads=[t1k, p1k], writes=[m1k])
                P.add("dve", lambda e, m2=m2, t2=t2, p2=p2: e.tensor_tensor(
                    out=m2[:, 0:W], in0=p2[:, 0:W], in1=t2[:, 0:W], op=ALU.mult), reads=[t2k, p2k], writes=[m2k])
                P.add("dve", lambda e, m=m, m1=m1, m2=m2: e.tensor_tensor(out=mixed[:, m, 0:W], in0=m1[:, 0:W], in1=m2[:, 0:W],
                                                                          op=ALU.add), reads=[m1k, m2k], writes=[("mixed", m)])
            for sl_ in (s_rp, s_ap, s_gr, s_ga):
                release(sl_)
        MX = [("mixed", m) for m in range(8)]
        run(tm_proj(xs_i, rows, nsub, lambda cg: [SL_O + cg], 8, mixed, MX))
        run(norm_transpose(xs_i, rows, nsub, 4, hTb, "hTb"))

    def ffn_up(ti, xs_i, W, rows, nsub, segs):
        for pr in range(11):
            s_a = load_slot(SL_UP + 2 * pr)
            s_b = load_slot(SL_UP + 2 * pr + 1)
            for sub in range(2):
                kc2 = 2 * pr + sub
                col0 = sub * 128
                aps, apk = bankA()
                mm_fm(aps[:, 0:W], apk, s_a[0], s_a[1], col0, hTb, ["hTb"], W)
                bps, bpk = bankA()
                mm_fm(bps[:, 0:W], bpk, s_b[0], s_b[1], col0, hTb, ["hTb"], W)
                acc, acck = tmpf()
                conv_taps(aps, apk, acc, acck, W, segs, 3, lambda j, kc2=kc2: prmc(P_FW + j * NFB + kc2, 1),
                          prmc(P_FB + kc2, 1),
                          lambda sidx, kc2=kc2: st[:, ST_FH + sidx * 44 + kc2 * 2:ST_FH + sidx * 44 + kc2 * 2 + 2],
                          lambda sidx, kc2=kc2: ("st_fh", sidx, kc2))
                gl, glk = tmpf()
                P.add("act", lambda e, gl=gl, acc=acc: e.activation(out=gl[:, 0:W], in_=acc[:, 0:W], func=AF.Gelu),
                      reads=[acck], writes=[glk])
                P.add("dve", lambda e, kc2=kc2, gl=gl, bps=bps: e.tensor_tensor(out=actb[:, kc2, 0:W], in0=bps[:, 0:W],
                                                                                in1=gl[:, 0:W], op=ALU.mult),
                      reads=[bpk, glk], writes=[("act", kc2)])
                yield
            release(s_a)
            release(s_b)

    def wdown_gen(ti, xs_i, W, rows, nsub, segs):
        yield from tm_proj(xs_i, rows, nsub, lambda cg: [SL_DN + 3 * cg + k for k in range(3)], NFB, actb,
                           [("act", k) for k in range(NFB)])

    def load_x(ti):
        b = ti % 2
        if ti < NT:
            P.dma(lambda e: e.dma_start(out=xt[:, b, :, :], in_=xp[ti * 512:(ti + 1) * 512, :].rearrange("(j p) d -> p j d", p=128)),
                  d_x[b], writes=[("xt", b, j) for j in range(4)])
        else:
            P.dma(lambda e: e.dma_start(out=xt[0:32, b, 0, :], in_=xs[:, :]), d_x[b], writes=[("xt", b, 0)])

    def targs(ti):
        b = ti % 2
        if ti < NT:
            return (ti, b, 512, 128, 4, [(0, 512, 0)], ti == 0, False)
        return (ti, b, 32, 32, 1, [(0, 16, 1), (16, 16, 2)], False, True)

    load_x(0)
    run(par([prepass_ffn_gen(), chain([stageA_gen(*targs(0)), front_gen(*targs(0))])]))
    for ti in range(NT + 1):
        b = ti % 2
        if ti + 1 <= NT:
            load_x(ti + 1)
        ta_ = targs(ti)
        mid(*ta_)
        if ti == 0:
            P.fence()
        gens = [ffn_up(*ta_[:6])]
        if ti + 1 <= NT:
            gens.append(stageA_gen(*targs(ti + 1)))
        run(par(gens))
        gate = {"go": False}

        def wd_then_open(g=gate, a=ta_[:6]):
            yield from wdown_gen(*a)
            g["go"] = True

        gens = [wd_then_open()]
        if ti + 1 <= NT:
            gens.append(front_gen(*targs(ti + 1), gate=None, wd_gate=gate))
        run(par(gens))
        if ti < NT:
            P.dma(lambda e, ti=ti, b=b: e.dma_start(out=yp[ti * 512:(ti + 1) * 512, :].rearrange("(j p) d -> p j d", p=128),
                                                    in_=xt[:, b, :, :]), d_y[b], reads=[("xt", b, j) for j in range(4)],
                  queue="pool")
        else:
            P.dma(lambda e, b=b: e.dma_start(out=ys[:, :], in_=xt[0:32, b, 0, :]), d_y[b], reads=[("xt", b, 0)], queue="pool")
    stkeys = (["st_hu", "st_h", "st_fh"] + [("st_hu", s, n) for s in range(3) for n in range(8)]
              + [("st_h", s, n) for s in range(3) for n in range(8)] + [("st_fh", s, k) for s in range(3) for k in range(NFB)])
    P.dma(lambda e: e.dma_start(out=st_o[:, :], in_=st[:, :]), d_fin, reads=stkeys, queue="pool")
    P.dma(lambda e: e.dma_start(out=kp_o[:, :], in_=kvo[:, 0, :]), d_fin, reads=[("kvo", 0)], queue="pool")
    P.dma(lambda e: e.dma_start(out=vp_o[:, :], in_=kvo[:, 1, :]), d_fin, reads=[("kvo", 1)], queue="pool")
    for s in range(2):
        P.dma(lambda e, s=s: e.dma_start(out=ks_o[s, 112:128, :], in_=kvs[0:16, s, 0, :]), d_fin, reads=[("kvs", s, 0)], queue="pool")
        P.dma(lambda e, s=s: e.dma_start(out=vs_o[s, 112:128, :], in_=kvs[0:16, s, 1, :]), d_fin, reads=[("kvs", s, 1)], queue="pool")
    P.finish()
    return nc, P


def _fm(v, nblk):
    v = np.asarray(v, np.float32)
    lead = v.shape[:-1]
    v = v.reshape(lead + (nblk, 128))
    return np.moveaxis(v, -1, 0)


_CACHE = {}


def kernel(x_prompt, x_sample, state_rnn_conv, state_rnn_h, cache_attn_k, cache_attn_v, state_ffn_conv,
           norm_mix_g, w_in, b_gate, rnn_conv_w, rnn_conv_b, rnn_gate_a_w, rnn_gate_a_b, rnn_gate_x_w,
           rnn_gate_x_b, rnn_lambda, q_norm_g, k_norm_g, attn_sinks, w_rnn_proj, w_attn_proj, w_out,
           norm_ffn_g, w_up, ffn_conv_w, ffn_conv_b, w_down):
    f32 = lambda a: np.ascontiguousarray(np.asarray(a, np.float32))
    x_prompt = f32(x_prompt)
    B, T, _ = x_prompt.shape
    NC = 8
    assert B == NC and x_sample.shape[0] == 2 * NC and x_sample.shape[1] == 16
    prm = np.zeros((128, NPRM), np.float32)
    prm[:, P_GM:P_GM + 8] = _fm(norm_mix_g[0], 8)
    prm[:, P_GF:P_GF + 8] = _fm(norm_ffn_g[0], 8)
    prm[:, P_CW:P_CW + 32] = _fm(rnn_conv_w[0], 8).reshape(128, 32)
    prm[:, P_CB:P_CB + 8] = _fm(rnn_conv_b[0], 8)
    prm[:, P_BA:P_BA + 8] = _fm(rnn_gate_a_b[0], 8)
    prm[:, P_BX:P_BX + 8] = _fm(rnn_gate_x_b[0], 8)
    prm[:, P_LAM:P_LAM + 8] = _fm(rnn_lambda[0], 8)
    prm[:, P_BG:P_BG + 16] = _fm(b_gate[0], 16)
    prm[:, P_FW:P_FW + 66] = _fm(ffn_conv_w[0], NFB).reshape(128, 66)
    prm[:, P_FB:P_FB + 22] = _fm(ffn_conv_b[0], NFB)
    prm[:, P_GQ] = np.asarray(q_norm_g[0], np.float32)
    prm[:, P_GK] = np.asarray(k_norm_g[0], np.float32)
    prm[:, P_SINK:P_SINK + 8] = np.broadcast_to(np.asarray(attn_sinks[0], np.float32)[None, :], (128, 8))
    key = T
    if key not in _CACHE:
        _CACHE[key] = build_program(T)[0]
    nc = _CACHE[key]
    shared = {
        "prm": prm, "w_in": f32(w_in[0]), "gate_a": f32(rnn_gate_a_w[0]), "gate_x": f32(rnn_gate_x_w[0]),
        "w_rnn_proj": f32(w_rnn_proj[0]), "w_attn_proj": f32(w_attn_proj[0]), "w_out": f32(w_out[0]),
        "w_up": f32(w_up[0]), "w_down": f32(w_down[0]),
    }
    src = np.asarray(state_rnn_conv[0], np.float32)
    sh = np.asarray(state_rnn_h[0], np.float32)
    sf = np.asarray(state_ffn_conv[0], np.float32)
    ck = np.asarray(cache_attn_k[0], np.float32).reshape(2 * NC, 128, 256)
    cv = np.asarray(cache_attn_v[0], np.float32).reshape(2 * NC, 128, 256)
    xs_all = f32(x_sample)
    in_maps = []
    for c in range(NC):
        sst = np.zeros((128, 152), np.float32)
        for s in range(2):
            q = 2 * c + s
            sst[:, s * 24:(s + 1) * 24] = np.transpose(_fm(src[q], 8), (0, 2, 1)).reshape(128, 24)
            sst[:, 48 + s * 8:48 + (s + 1) * 8] = _fm(sh[q], 8)
            sst[:, 64 + s * 44:64 + (s + 1) * 44] = np.transpose(_fm(sf[q], NFB), (0, 2, 1)).reshape(128, 44)
        m = dict(shared)
        m["xp"] = x_prompt[c]
        m["xs"] = np.ascontiguousarray(xs_all[2 * c:2 * c + 2].reshape(32, D))
        m["sst"] = sst
        m["ck"] = np.ascontiguousarray(ck[2 * c:2 * c + 2])
        m["cv"] = np.ascontiguousarray(cv[2 * c:2 * c + 2])
        in_maps.append(m)
    res = run_bass_kernel_spmd(nc, in_maps, core_ids=list(range(NC)))
    R = res.results
    y_p = np.stack([R[c]["yp"] for c in range(NC)])[:, :, :]
    y_s = np.concatenate([R[c]["ys"].reshape(2, 16, D) for c in range(NC)], axis=0)
    st = np.stack([R[c]["st_o"] for c in range(NC)])

    def unfm(a):
        return np.transpose(a, (2, 1, 0)).reshape(a.shape[2], -1)

    rc_p = np.stack([unfm(st[c][:, ST_HU:ST_HU + 24].reshape(128, 8, 3)) for c in range(NC)])[None]
    rc_s = np.stack([unfm(st[c][:, ST_HU + 24 * (1 + s):ST_HU + 24 * (2 + s)].reshape(128, 8, 3))
                     for c in range(NC) for s in range(2)])[None]
    h_p = np.stack([st[c][:, ST_H:ST_H + 8].T.reshape(-1) for c in range(NC)])[None]
    h_s = np.stack([st[c][:, ST_H + 8 * (1 + s):ST_H + 8 * (2 + s)].T.reshape(-1) for c in range(NC) for s in range(2)])[None]
    f_p = np.stack([unfm(st[c][:, ST_FH:ST_FH + 44].reshape(128, NFB, 2)) for c in range(NC)])[None]
    f_s = np.stack([unfm(st[c][:, ST_FH + 44 * (1 + s):ST_FH + 44 * (2 + s)].reshape(128, NFB, 2))
                    for c in range(NC) for s in range(2)])[None]
    k_p = np.stack([R[c]["kp"].reshape(128, 2, 128) for c in range(NC)])[None]
    v_p = np.stack([R[c]["vp"].reshape(128, 2, 128) for c in range(NC)])[None]
    k_s = np.concatenate([R[c]["ks"].reshape(2, 128, 2, 128) for c in range(NC)], axis=0)[None]
    v_s = np.concatenate([R[c]["vs"].reshape(2, 128, 2, 128) for c in range(NC)], axis=0)[None]
    outs = (y_p, y_s, rc_p, rc_s, h_p, h_s, k_p, k_s, v_p, v_s, f_p, f_s)
    return tuple(np.ascontiguousarray(o, dtype=np.float32) for o in outs)
```

```python
import contextlib
import numpy as np
import concourse.bass as bass
import concourse.mybir as mybir
from concourse.bass_utils import run_bass_kernel_spmd
from concourse.alu_op_type import AluOpType as ALU

F32 = mybir.dt.float32
BF16 = mybir.dt.bfloat16
AF = mybir.ActivationFunctionType


class _Op:
    __slots__ = ("idx", "eng", "fn", "deps", "is_dma", "dsem", "dma_val", "needs_sig", "sigval", "waits")


class _DSem:
    def __init__(self, sem, name, serial=False):
        self.sem = sem
        self.name = name
        self.count = 0
        self.serial = serial
        self.last = None


class Prog:
    def __init__(self, nc):
        self.nc = nc
        self.ops = []
        self.last_writer = {}
        self.readers = {}
        self.stack = contextlib.ExitStack()
        self.eng_sems = {}
        for e in ("pe", "act", "dve", "pool"):
            self.eng_sems[e] = self.stack.enter_context(nc.semaphore("s_" + e))
        self.dsems = []
        self.final = []

    def sbuf(self, name, shape, dtype):
        return self.stack.enter_context(self.nc.sbuf_tensor(name, list(shape), dtype))

    def psum(self, name, shape, dtype):
        return self.stack.enter_context(self.nc.psum_tensor(name, list(shape), dtype))

    def dsem(self, name, serial=False):
        d = _DSem(self.stack.enter_context(self.nc.semaphore("d_" + name)), name, serial)
        self.dsems.append(d)
        return d

    def _mk(self, eng, fn, reads, writes):
        op = _Op()
        op.idx = len(self.ops)
        op.eng = eng
        op.fn = fn
        op.is_dma = False
        op.dsem = None
        op.dma_val = 0
        op.needs_sig = False
        op.sigval = 0
        deps = set()
        for r in reads:
            if r in self.last_writer:
                deps.add(self.last_writer[r])
        for w in writes:
            if w in self.last_writer:
                deps.add(self.last_writer[w])
            deps |= self.readers.get(w, set())
        op.deps = deps
        for r in reads:
            self.readers.setdefault(r, set()).add(op.idx)
        for w in writes:
            self.last_writer[w] = op.idx
            self.readers[w] = set()
        self.ops.append(op)
        return op

    def add(self, eng, fn, reads=(), writes=()):
        return self._mk(eng, fn, reads, writes)

    def fence(self):
        last = {}
        for op in self.ops:
            if op.fn is None:
                continue
            if op.is_dma:
                last[("d", id(op.dsem))] = op.idx
            else:
                last[("e", op.eng)] = op.idx
        deps = set(last.values())
        for eng in ("sp", "act", "dve", "pool", "pe"):
            op = self._mk(eng, None, (), ())
            op.deps = set(deps)

    def dma(self, fn, dsem, reads=(), writes=(), queue="sp"):
        op = self._mk(queue, fn, reads, writes)
        op.is_dma = True
        op.dsem = dsem
        if dsem.serial and dsem.last is not None:
            op.deps.add(dsem.last)
        dsem.last = op.idx
        dsem.count += 1
        op.dma_val = 16 * dsem.count
        return op

    def finish(self, final_dsems=None):
        ops = self.ops
        for op in ops:
            for d in op.deps:
                dop = ops[d]
                if dop.is_dma:
                    continue
                if dop.eng == "pe" and op.eng == "pe" and not op.is_dma:
                    continue
                dop.needs_sig = True
        cnt = {}
        for op in ops:
            if (not op.is_dma) and op.needs_sig:
                cnt[op.eng] = cnt.get(op.eng, 0) + 1
                op.sigval = cnt[op.eng]
        known = {}
        streams = {}
        for op in ops:
            kn = known.setdefault(op.eng, {})
            waits = {}
            for d in op.deps:
                dop = ops[d]
                if dop.is_dma:
                    ch, val = ("d", id(dop.dsem)), dop.dma_val
                    sem = dop.dsem.sem
                else:
                    if dop.eng == "pe" and op.eng == "pe" and not op.is_dma:
                        continue
                    ch, val = ("e", dop.eng), dop.sigval
                    sem = self.eng_sems[dop.eng]
                if kn.get(ch, 0) >= val:
                    continue
                if ch not in waits or waits[ch][1] < val:
                    waits[ch] = (sem, val)
            for ch, (sem, val) in waits.items():
                kn[ch] = val
            op.waits = list(waits.values())
            streams.setdefault(op.eng, []).append(op)
        if final_dsems is None:
            final_dsems = self.dsems
        finals = [(d.sem, 16 * d.count) for d in final_dsems if d.count > 0]
        eng_sems = self.eng_sems
        self.n_ops = {k: len(v) for k, v in streams.items()}

        def emit(name, e, tail=False):
            for op in streams.get(name, []):
                for sem, val in op.waits:
                    e.wait_ge(sem, val)
                if op.fn is None:
                    continue
                ins = op.fn(e)
                if op.is_dma:
                    ins.then_inc(op.dsem.sem, 16)
                elif op.needs_sig:
                    ins.then_inc(eng_sems[name], 1)
            if tail:
                for sem, val in finals:
                    e.wait_ge(sem, val)

        with self.nc.Block() as block:
            @block.sync
            def _(e):
                emit("sp", e, tail=True)

            @block.scalar
            def _(e):
                emit("act", e)

            @block.vector
            def _(e):
                emit("dve", e)

            @block.gpsimd
            def _(e):
                emit("pool", e)

            @block.tensor
            def _(e):
                emit("pe", e)
        self.stack.close()


D = 1024
NH = 8
NKV = 2
HD = 128
DFF = 2816
NFB = DFF // 128
INW = 4608
EPS = 1e-6
ATTN_SCALE = HD ** -0.5
NSLOT = 64
NRING = 6
LN_HALF = float(np.log(0.5))
NEG = -30000.0

P_GM, P_GF, P_CW, P_CB, P_BA, P_BX, P_LAM, P_BG, P_FW, P_FB, P_GQ, P_GK, P_SINK = (
    0, 8, 16, 48, 56, 64, 72, 80, 96, 162, 184, 185, 186)
NPRM = 194
ST_HU, ST_H, ST_FH = 0, 72, 96
NST = 228

SL_U, SL_Q, SL_K, SL_V, SL_M, SL_O, SL_UP, SL_DN = 0, 4, 8, 9, 10, 26, 30, 52


class _Stop(Exception):
    pass


_DBG_STOP = None


def _chk(level):
    if _DBG_STOP is not None and level >= _DBG_STOP:
        raise _Stop()


def build_program(T, n_cores_hint=8):
    assert T % 512 == 0
    NT = T // 512
    nc = bass.Bass("TRN2", target_bir_lowering=False)

    def din(name, shape, dt=F32):
        return nc.dram_tensor(name, list(shape), dt, kind="ExternalInput").ap()

    def dout(name, shape, dt=F32):
        return nc.dram_tensor(name, list(shape), dt, kind="ExternalOutput").ap()

    xp = din("xp", [T, D])
    xs = din("xs", [32, D])
    prm_d = din("prm", [128, NPRM])
    sst_d = din("sst", [128, 152])
    ck_d = din("ck", [2, 128, 256])
    cv_d = din("cv", [2, 128, 256])
    w_in_d = din("w_in", [D, INW])
    ga_d = din("gate_a", [8, 128, 128])
    gx_d = din("gate_x", [8, 128, 128])
    wrp_d = din("w_rnn_proj", [D, D])
    wap_d = din("w_attn_proj", [D, D])
    wout_d = din("w_out", [D, D])
    wup_d = din("w_up", [D, 2 * DFF])
    wdn_d = din("w_down", [DFF, D])

    yp = dout("yp", [T, D])
    ys = dout("ys", [32, D])
    st_o = dout("st_o", [128, NST])
    kp_o = dout("kp", [128, 256])
    vp_o = dout("vp", [128, 256])
    ks_o = dout("ks", [2, 128, 256])
    vs_o = dout("vs", [2, 128, 256])

    wsc = nc.dram_tensor("wsc", [NSLOT, 128, 2048], BF16).ap()

    P = Prog(nc)
    xt = P.sbuf("xt", [128, 2, 4, D], F32)
    ss = P.sbuf("ss", [128, 8], F32)
    rs = P.sbuf("rs", [128, 8], F32)
    xn = P.sbuf("xn", [128, 2, D], BF16)
    hTa = P.sbuf("hTa", [128, 8, 512], BF16)
    hTb = P.sbuf("hTb", [128, 8, 512], BF16)
    tl = P.sbuf("tl", [128, 19, 512], F32)
    tlb = P.sbuf("tlb", [128, 6, 512], BF16)
    yr = P.sbuf("yr", [128, 8, 512], BF16)
    ya = P.sbuf("ya", [128, 8, 512], BF16)
    mixed = P.sbuf("mixed", [128, 8, 512], BF16)
    qT = P.sbuf("qT", [128, 8, 512], BF16)
    kT = P.sbuf("kT", [128, 2, 640], BF16)
    kTc = P.sbuf("kTc", [128, 2, 2, 128], BF16)
    vb = P.sbuf("vb", [128, 5, 2, 128], BF16)
    vc = P.sbuf("vc", [128, 2, 2, 128], BF16)
    btab = P.sbuf("btab", [128, 2, 8, 128], BF16)
    pT = P.sbuf("pT", [128, 4, 512], BF16)
    yat = P.sbuf("yat", [128, 2, D], BF16)
    actb = P.sbuf("actb", [128, NFB, 512], BF16)
    ring = P.sbuf("ring", [128, NRING, 2048], BF16)
    prm = P.sbuf("prm_s", [128, NPRM], F32)
    dv = P.sbuf("dv", [128, 64], F32)
    st = P.sbuf("st", [128, NST], F32)
    gwa = P.sbuf("gwa", [128, 8, 128], BF16)
    gwx = P.sbuf("gwx", [128, 8, 128], BF16)
    identf = P.sbuf("identf", [128, 128], F32)
    identb = P.sbuf("identb", [128, 128], BF16)
    onesb = P.sbuf("onesb", [128, 128], BF16)
    kvo = P.sbuf("kvo", [128, 2, 256], F32)
    kvs = P.sbuf("kvs", [128, 2, 2, 256], F32)
    smal = P.sbuf("smal", [128, 16], F32)
    act32 = actb[:, 0:16, :].rearrange("p a b -> p (a b)").bitcast(F32).rearrange("p (a b) -> p a b", a=8)
    NSTG = 4
    stg_f = [actb[:, 4 * i:4 * i + 4, :].rearrange("p a b -> p (a b)").bitcast(F32) for i in range(NSTG)]
    stg_b = [yr[:, 2 * i:2 * i + 2, :].rearrange("p a b -> p (a b)") for i in range(NSTG)]
    stg_f2 = [actb[:, 4 * i:4 * i + 4, :].rearrange("p a b -> p (a b)").bitcast(F32) for i in range(3)]
    stg_b2 = [actb[:, 12 + 2 * i:14 + 2 * i, :].rearrange("p a b -> p (a b)") for i in range(3)]
    gstg = tl[:, 2:4, :].rearrange("p a b -> p (a b)")

    psA = [P.psum(f"psA{i}", [128, 512], F32) for i in range(4)]
    psB = [P.psum(f"psB{i}", [128, 512], F32) for i in range(2)]
    psC = P.psum("psC", [128, 1024], F32)
    banks8 = [(psA[0], ("psA", 0)), (psA[1], ("psA", 1)), (psA[2], ("psA", 2)), (psA[3], ("psA", 3)),
              (psC[:, 0:512], ("psC", 0)), (psC[:, 512:1024], ("psC", 1)), (psB[0], ("psB", 0)), (psB[1], ("psB", 1))]
    cnt = {"A": 0, "B": 0, "tf": 0, "tb": 0, "pT": 0, "ring": 0, "xn": 0, "yat": 0}

    def bankA():
        i = cnt["A"] % 4
        cnt["A"] += 1
        return psA[i], ("psA", i)

    def bankB():
        i = cnt["B"] % 2
        cnt["B"] += 1
        return psB[i], ("psB", i)

    def tmpf():
        i = cnt["tf"] % 19
        cnt["tf"] += 1
        return tl[:, i, :], ("tl", i)

    def tmpb():
        i = cnt["tb"] % 6
        cnt["tb"] += 1
        return tlb[:, i, :], ("tlb", i)

    d_const = P.dsem("const", serial=True)
    d_x = [P.dsem("x0"), P.dsem("x1")]
    d_ring = [P.dsem(f"ring{i}") for i in range(NRING)]
    d_stg = [P.dsem(f"stg{i}") for i in range(7)]
    d_wst = [P.dsem(f"wst{i}") for i in range(7)]
    d_y = [P.dsem("y0"), P.dsem("y1")]
    d_fin = P.dsem("fin")
    d_cp = P.dsem("cp")

    def prmc(c0, n=1):
        return prm[:, c0:c0 + n]

    P.dma(lambda e: e.dma_start(out=prm[:, :], in_=prm_d[:, :]), d_const, writes=["prm"])
    P.dma(lambda e: e.dma_start(out=st[:, ST_HU + 24:ST_HU + 72], in_=sst_d[:, 0:48]), d_const, writes=["st_hu"])
    P.dma(lambda e: e.dma_start(out=st[:, ST_H + 8:ST_H + 24], in_=sst_d[:, 48:64]), d_const, writes=["st_h"])
    P.dma(lambda e: e.dma_start(out=st[:, ST_FH + 44:ST_FH + 132], in_=sst_d[:, 64:152]), d_const, writes=["st_fh"])
    P.add("dve", lambda e: e.memset(st[:, ST_HU:ST_HU + 24], 0.0), writes=["st_hu"])
    P.add("dve", lambda e: e.memset(st[:, ST_H:ST_H + 8], 0.0), writes=["st_h"])
    P.add("dve", lambda e: e.memset(st[:, ST_FH:ST_FH + 44], 0.0), writes=["st_fh"])
    P.add("pool", lambda e: e.iota(identf[:, :], pattern=[[1, 128]], base=0, channel_multiplier=-1,
                                    allow_small_or_imprecise_dtypes=True), writes=["identf"])
    P.add("dve", lambda e: e.tensor_scalar(out=identf[:, :], in0=identf[:, :], scalar1=0.0, scalar2=None, op0=ALU.is_equal),
          reads=["identf"], writes=["identf"])
    P.add("dve", lambda e: e.tensor_copy(out=identb[:, :], in_=identf[:, :]), reads=["identf"], writes=["identb"])
    P.add("dve", lambda e: e.memset(onesb[:, :], 1.0), writes=["onesb"])
    btf = act32[:, 0:4, :].rearrange("p a b -> p (a b)").rearrange("p (k h q) -> p k h q", k=2, h=8)
    btf2 = act32[:, 4:8, :].rearrange("p a b -> p (a b)").rearrange("p (k h q) -> p k h q", k=2, h=8)
    P.add("pool", lambda e: e.iota(btf, pattern=[[-128, 2], [0, 8], [1, 128]], base=128, channel_multiplier=-1,
                                    allow_small_or_imprecise_dtypes=True), writes=["btf"])
    P.add("dve", lambda e: e.tensor_scalar(out=btf2, in0=btf, scalar1=-1.0, scalar2=None, op0=ALU.mult),
          reads=["btf"], writes=["btf2"])
    P.add("dve", lambda e: e.tensor_tensor(out=btf, in0=btf, in1=btf2, op=ALU.max), reads=["btf", "btf2"], writes=["btf"])
    for h in range(8):
        P.add("dve", lambda e, h=h: e.tensor_scalar(out=btf[:, :, h, :], in0=btf[:, :, h, :], scalar1=-(2.0 ** -(h + 1)),
                                                    scalar2=None, op0=ALU.mult), reads=["btf"], writes=["btf"])
    P.add("dve", lambda e: e.memset(btf[64:128, 1, :, 0:64], NEG), reads=["btf"], writes=["btf"])
    P.add("dve", lambda e: e.memset(btf[0:64, 0, :, 64:128], NEG), reads=["btf"], writes=["btf"])
    P.add("dve", lambda e: e.tensor_copy(out=btab[:, :, :, :], in_=btf), reads=["btf"], writes=["btab"])
    DV_CH, DV_C, DV_NBA, DV_NBX, DV_NBG, DV_GQS, DV_ESK, DV_MH = 0, 8, 16, 24, 32, 48, 49, 57
    DV_CF = DV_C
    P.add("act", lambda e: e.activation(out=dv[:, DV_CH:DV_CH + 8], in_=prmc(P_LAM, 8), func=AF.Exp, scale=-1.0),
          reads=["prm"], writes=["dv_c"])
    P.add("act", lambda e: e.activation(out=dv[:, DV_CH:DV_CH + 8], in_=dv[:, DV_CH:DV_CH + 8], func=AF.Ln, bias=1.0),
          reads=["dv_c"], writes=["dv_c"])
    P.add("dve", lambda e: e.tensor_scalar(out=dv[:, DV_CF:DV_CF + 8], in0=dv[:, DV_CH:DV_CH + 8], scalar1=-8.0, scalar2=None,
                                           op0=ALU.mult), reads=["dv_c"], writes=["dv_cf"])
    P.add("dve", lambda e: e.tensor_scalar(out=dv[:, DV_CH:DV_CH + 8], in0=dv[:, DV_CH:DV_CH + 8], scalar1=-16.0, scalar2=None,
                                           op0=ALU.mult), reads=["dv_c", "dv_cf"], writes=["dv_c"])
    P.add("dve", lambda e: e.tensor_scalar(out=dv[:, DV_NBA:DV_NBA + 16], in0=prmc(P_BA, 16), scalar1=-1.0, scalar2=None,
                                           op0=ALU.mult), reads=["prm"], writes=["dv_hb"])
    P.add("dve", lambda e: e.tensor_scalar(out=dv[:, DV_NBG:DV_NBG + 16], in0=prmc(P_BG, 16), scalar1=-1.0, scalar2=None,
                                           op0=ALU.mult), reads=["prm"], writes=["dv_hbg"])
    P.add("dve", lambda e: e.tensor_scalar(out=dv[:, DV_GQS:DV_GQS + 1], in0=prmc(P_GQ, 1), scalar1=ATTN_SCALE, scalar2=None,
                                           op0=ALU.mult), reads=["prm"], writes=["dv_gqs"])
    P.add("act", lambda e: e.activation(out=dv[:, DV_ESK:DV_ESK + 8], in_=prmc(P_SINK, 8), func=AF.Exp),
          reads=["prm"], writes=["dv_esk"])
    P.add("dve", lambda e: e.memset(dv[:, DV_MH:DV_MH + 4], -0.5), writes=["dv_mh"])
    CONSTS = ["prm", "dv_c", "dv_cf", "dv_hb", "dv_hbg", "dv_gqs", "dv_esk", "dv_mh"]
    for src, dst, nm in ((ga_d, gwa, "gwa"), (gx_d, gwx, "gwx")):
        P.dma(lambda e, src=src: e.dma_start(out=gstg.rearrange("p (n d) -> p n d", n=8),
                                             in_=src.rearrange("n c d -> c n d")), d_const, writes=[("tl", 2), ("tl", 3)])
        P.add("dve", lambda e, dst=dst: e.tensor_copy(out=dst[:, :, :], in_=gstg.rearrange("p (n d) -> p n d", n=8)),
              reads=[("tl", 2), ("tl", 3)], writes=[nm])
    for s in range(2):
        kc_f = tl[:, 0, 0:256]
        vc_f = tl[:, 1, 0:256]
        P.dma(lambda e, s=s: e.dma_start(out=kc_f, in_=ck_d[s]), d_const, writes=[("tl", 0)])
        P.dma(lambda e, s=s: e.dma_start(out=vc_f, in_=cv_d[s]), d_const, writes=[("tl", 1)])
        P.add("dve", lambda e, s=s: e.tensor_copy(out=vc[:, s, :, 0:128], in_=vc_f.rearrange("p (g d) -> p g d", g=2)),
              reads=[("tl", 1)], writes=[("vc", s)])
        for g in range(2):
            bk, bkk = bankB()
            P.add("pe", lambda e, g=g, bk=bk: e.transpose(bk[:, 0:128], kc_f[:, g * 128:(g + 1) * 128], identf[:, :]),
                  reads=[("tl", 0), "identf"], writes=[bkk])
            P.add("act", lambda e, s=s, g=g, bk=bk: e.activation(out=kTc[:, s, g, :], in_=bk[:, 0:128], func=AF.Copy),
                  reads=[bkk], writes=[("kTc", s)])
        P.dma(lambda e, s=s: e.dma_start(out=ks_o[s, 0:112, :], in_=ck_d[s, 16:128, :]), d_cp)
        P.dma(lambda e, s=s: e.dma_start(out=vs_o[s, 0:112, :], in_=cv_d[s, 16:128, :]), d_cp)

    P.fence()
    pp = {"i": 0}
    _pre_ok = not (_DBG_STOP is not None and _DBG_STOP <= 1)

    def prepass_chunk(src, kc, c0, ncol, dest_fn, scale, late=False):
        i = pp["i"]
        pp["i"] += 1
        if late:
            b = NSTG + i % 3
            sf, sb = stg_f2[i % 3], stg_b2[i % 3]
            depth = 2
        else:
            b = i % NSTG
            sf, sb = stg_f[b], stg_b[b]
            depth = NSTG - 1
        P.dma(lambda e: e.dma_start(out=sf[:, 0:ncol], in_=src[kc * 128:(kc + 1) * 128, c0:c0 + ncol]), d_stg[b],
              writes=[("stgf", b)])
        eng = ("dve", "act", "pool")[i % 3]
        rd = [("stgf", b)] + CONSTS
        if eng == "dve":
            P.add("dve", lambda e: e.tensor_scalar(out=sb[:, 0:ncol], in0=sf[:, 0:ncol], scalar1=scale, scalar2=None, op0=ALU.mult),
                  reads=rd, writes=[("stgb", b)])
        elif eng == "act":
            P.add("act", lambda e: e.activation(out=sb[:, 0:ncol], in_=sf[:, 0:ncol], func=AF.Copy, scale=scale),
                  reads=rd, writes=[("stgb", b)])
        else:
            P.add("pool", lambda e: e.tensor_scalar(out=sb[:, 0:ncol], in0=sf[:, 0:ncol], scalar1=scale, scalar2=1.0,
                                                    op0=ALU.mult, op1=ALU.mult), reads=rd, writes=[("stgb", b)])
        ng = ncol // 256
        slots = [dest_fn(g) for g in range(ng)]
        kcl = slots[0][1]
        s0 = slots[0][0]
        step = (slots[1][0] - s0) if ng > 1 else 1
        dst = wsc[s0:s0 + step * (ng - 1) + 1:step].rearrange("s p c -> p s c")[:, :, kcl * 256:(kcl + 1) * 256]
        pp.setdefault("pend", []).append(
            lambda: P.dma(lambda e: e.dma_start(out=dst, in_=sb[:, 0:ncol].rearrange("p (s c) -> p s c", s=ng)), d_wst[b],
                          reads=[("stgb", b)], writes=[("wsc", sl_[0]) for sl_ in slots]))
        while len(pp["pend"]) > depth:
            pp["pend"].pop(0)()

    for kc in range(8):
        g = prmc(P_GM + kc, 1)
        prepass_chunk(w_in_d, kc, 0, 1024, lambda i, kc=kc: (SL_U + i, kc), g)
        prepass_chunk(w_in_d, kc, 1024, 1024, lambda i, kc=kc: (SL_Q + i, kc), g)
        prepass_chunk(w_in_d, kc, 2048, 512, lambda i, kc=kc: (SL_K + i, kc), g)
        prepass_chunk(w_in_d, kc, 2560, 1024, lambda i, kc=kc: (SL_M + 4 * i + 2, kc), g)
        prepass_chunk(w_in_d, kc, 3584, 1024, lambda i, kc=kc: (SL_M + 4 * i + 3, kc), g)
        prepass_chunk(wrp_d, kc, 0, 1024, lambda i, kc=kc: (SL_M + 4 * i + 0, kc), 1.0)
        prepass_chunk(wap_d, kc, 0, 1024, lambda i, kc=kc: (SL_M + 4 * i + 1, kc), 1.0)
        prepass_chunk(wout_d, kc, 0, 1024, lambda i, kc=kc: (SL_O + i, kc), 1.0)

    def prepass_ffn_gen():
        for kc in range(8):
            gf = prmc(P_GF + kc, 1)
            for half in range(2):
                for (c0, ncol, p0) in ((0, 1024, 0), (1024, 1024, 4), (2048, 768, 8)):
                    prepass_chunk(wup_d, kc, half * DFF + c0, ncol,
                                  lambda i, kc=kc, half=half, p0=p0: (SL_UP + 2 * (p0 + i) + half, kc), gf, late=True)
                    yield
        for kc in range(NFB):
            prepass_chunk(wdn_d, kc, 0, 1024, lambda i, kc=kc: (SL_DN + 3 * i + kc // 8, kc % 8), 1.0, late=True)
            yield
        while pp.get("pend"):
            pp["pend"].pop(0)()
        yield

    while pp.get("pend"):
        pp["pend"].pop(0)()
    P.fence()

    ring_held = [False] * NRING

    def load_slot(slot):
        for k in range(NRING):
            r = (cnt["ring"] + k) % NRING
            if not ring_held[r]:
                break
        else:
            raise RuntimeError("weight ring exhausted")
        cnt["ring"] = r + 1
        ring_held[r] = True
        nv = 6 * 256 if (slot >= SL_DN and (slot - SL_DN) % 3 == 2) else 2048
        P.dma(lambda e: e.dma_start(out=ring[:, r, 0:nv], in_=wsc[slot, :, 0:nv]), d_ring[r], reads=[("wsc", slot)],
              writes=[("ring", r)])
        return ring[:, r, :].rearrange("p (k c) -> p k c", k=8), ("ring", r), r

    def release(sl):
        ring_held[sl[2]] = False

    def mm_fm(out_ap, okey, wv, wkey, col0, rhs_t, rkeys, W, nk=8, extra_reads=()):
        for kc in range(nk):
            P.add("pe", lambda e, kc=kc: e.matmul(out_ap, lhsT=wv[:, kc, col0:col0 + 128], rhs=rhs_t[:, kc, 0:W],
                                                  start=(kc == 0), stop=(kc == nk - 1)),
                  reads=[wkey] + list(rkeys) + list(extra_reads), writes=[okey])

    def norm_transpose(xs_i, rows, nsub, sscol, dst, dkey):
        for j in range(nsub):
            P.add("act", lambda e, j=j: e.activation(out=xn[0:rows, j % 2, :], in_=xt[0:rows, xs_i, j, :], func=AF.Square,
                                                      accum_out=ss[0:rows, sscol + j:sscol + j + 1]),
                  reads=[("xt", xs_i, j)], writes=[("xn", j % 2), ("ss", sscol + j)])
            yield
        P.add("pool", lambda e: e.tensor_scalar(out=rs[0:rows, sscol:sscol + nsub], in0=ss[0:rows, sscol:sscol + nsub],
                                                scalar1=1.0 / D, scalar2=EPS, op0=ALU.mult, op1=ALU.add),
              reads=[("ss", sscol + j) for j in range(nsub)], writes=[("rs", sscol)])
        P.add("pool", lambda e: e.tensor_tensor(out=rs[0:rows, sscol:sscol + nsub], in0=rs[0:rows, sscol:sscol + nsub],
                                                in1=dv[0:rows, DV_MH:DV_MH + nsub], op=ALU.pow),
              reads=[("rs", sscol), "dv_mh"], writes=[("rs", sscol)])
        yield
        for j in range(nsub):
            r = cnt["xn"] % 2
            cnt["xn"] += 1
            P.add("dve", lambda e, j=j, r=r: e.tensor_scalar(out=xn[0:rows, r, :], in0=xt[0:rows, xs_i, j, :],
                                                             scalar1=rs[0:rows, sscol + j:sscol + j + 1], scalar2=None,
                                                             op0=ALU.mult),
                  reads=[("xt", xs_i, j), ("rs", sscol)], writes=[("xn", r)])
            yield
            bk, bkk = bankB()
            bkb = bk[:, :].bitcast(BF16)
            for kc in range(8):
                P.add("pe", lambda e, kc=kc, r=r, bkb=bkb: e.transpose(bkb[:, kc * 128:kc * 128 + rows],
                                                                       xn[0:rows, r, kc * 128:(kc + 1) * 128],
                                                                       identb[0:rows, 0:rows]),
                      reads=[("xn", r), "identb"], writes=[bkk])
            yield
            P.add("act", lambda e, j=j, bkb=bkb: e.activation(
                out=dst[:, :, j * 128:j * 128 + rows],
                in_=bkb.rearrange("p (k t) -> p k t", k=8)[:, :, 0:rows], func=AF.Copy),
                reads=[bkk], writes=[dkey])
            yield

    def conv_taps(ps, pskey, acc, acckey, W, segs, ntap, wcol_fn, bcol, halo_fn, halokey_fn):
        P.add("act", lambda e: e.activation(out=acc[:, 0:W], in_=ps[:, 0:W], func=AF.Identity, bias=bcol,
                                            scale=wcol_fn(ntap - 1)),
              reads=[pskey] + CONSTS, writes=[acckey])
        nh = ntap - 1
        for (c0, L, sidx) in segs:
            hal = halo_fn(sidx)
            for s in range(1, ntap):
                wj = wcol_fn(ntap - 1 - s)
                P.add("dve", lambda e, c0=c0, L=L, s=s, wj=wj: e.scalar_tensor_tensor(
                    out=acc[:, c0 + s:c0 + L], in0=ps[:, c0:c0 + L - s], scalar=wj, in1=acc[:, c0 + s:c0 + L],
                    op0=ALU.mult, op1=ALU.add), reads=[pskey, acckey] + CONSTS, writes=[acckey])
                P.add("dve", lambda e, c0=c0, s=s, wj=wj, hal=hal: e.scalar_tensor_tensor(
                    out=acc[:, c0:c0 + s], in0=hal[:, nh - s:nh], scalar=wj, in1=acc[:, c0:c0 + s],
                    op0=ALU.mult, op1=ALU.add), reads=[halokey_fn(sidx), acckey] + CONSTS, writes=[acckey])
            P.add("dve", lambda e, c0=c0, L=L, hal=hal: e.tensor_scalar(out=hal[:, 0:nh], in0=ps[:, c0 + L - nh:c0 + L],
                                                                       scalar1=1.0, scalar2=None, op0=ALU.mult),
                  reads=[pskey], writes=[halokey_fn(sidx)])

    def chain(gens):
        for g in gens:
            yield from g

    def par(gens):
        gens = list(gens)
        while gens:
            for g in list(gens):
                try:
                    next(g)
                except StopIteration:
                    gens.remove(g)
            yield

    def run(gen):
        for _ in gen:
            pass

    def tm_proj(xs_i, rows, nsub, slot_ids, nkc, lhs_t, lkeys):
        for cg in range(4):
            sl = [load_slot(s) for s in slot_ids(cg)]
            for j in range(nsub):
                for kc in range(nkc):
                    wv, wkey, _r = sl[kc // 8]
                    P.add("pe", lambda e, j=j, kc=kc, wv=wv: e.matmul(
                        psC[0:rows, j * 256:(j + 1) * 256], lhsT=lhs_t[:, kc, j * 128:j * 128 + rows], rhs=wv[:, kc % 8, :],
                        start=(kc == 0), stop=(kc == nkc - 1)), reads=[wkey] + list(lkeys), writes=[("psC", j // 2)])
                yield
            for s_ in sl:
                release(s_)
            for j in range(nsub):
                P.add("dve", lambda e, j=j, cg=cg: e.tensor_tensor(
                    out=xt[0:rows, xs_i, j, cg * 256:(cg + 1) * 256], in0=psC[0:rows, j * 256:(j + 1) * 256],
                    in1=xt[0:rows, xs_i, j, cg * 256:(cg + 1) * 256], op=ALU.add),
                    reads=[("psC", j // 2), ("xt", xs_i, j)], writes=[("xt", xs_i, j)])
            yield

    HT = ["hTa"]

    def stageA_gen(ti, xs_i, W, rows, nsub, segs, first_tile, is_sample):
        yield from norm_transpose(xs_i, rows, nsub, 0, hTa, "hTa")

    def front_gen(ti, xs_i, W, rows, nsub, segs, first_tile, is_sample, gate=None, wd_gate=None):
        slot_cache = {}
        slot_uses = {}
        nvj = nsub if not is_sample else len(segs)
        for k in range(4):
            slot_uses[SL_U + k] = 2
            slot_uses[SL_Q + k] = 2
        slot_uses[SL_K] = 2
        slot_uses[SL_V] = nvj

        def use_slot(sid):
            if sid not in slot_cache:
                slot_cache[sid] = load_slot(sid)
            return slot_cache[sid][0], slot_cache[sid][1]

        def done_slot(sid):
            slot_uses[sid] -= 1
            if slot_uses[sid] == 0:
                release(slot_cache[sid])

        done_head = {}
        done_tail = {}

        def rnn_head(n):
            lane = n % 2
            sset = n % 4
            bU, bUk = banks8[lane]
            bG, bGk = banks8[lane]
            acc, acck = tl[:, 4 * sset + 0, :], ("tl", 4 * sset + 0)
            t1, t1k = tl[:, 4 * sset + 1, :], ("tl", 4 * sset + 1)
            t2, t2k = tl[:, 4 * sset + 2, :], ("tl", 4 * sset + 2)
            ta, tak = tl[:, 4 * sset + 3, :], ("tl", 4 * sset + 3)
            ucb, ucbk = tlb[:, sset, :], ("tlb", sset)
            while n >= 4 and not done_tail.get(n - 4):
                yield
            wv, wkey = use_slot(SL_U + n // 2)
            mm_fm(bU[:, 0:W], bUk, wv, wkey, (n % 2) * 128, hTa, HT, W)
            done_slot(SL_U + n // 2)
            yield
            conv_taps(bU, bUk, acc, acck, W, segs, 4, lambda j, n=n: prmc(P_CW + j * 8 + n, 1), prmc(P_CB + n, 1),
                      lambda sidx, n=n: st[:, ST_HU + sidx * 24 + n * 3:ST_HU + sidx * 24 + n * 3 + 3],
                      lambda sidx, n=n: ("st_hu", sidx, n))
            yield
            P.add("dve", lambda e: e.tensor_copy(out=ucb[:, 0:W], in_=acc[:, 0:W]), reads=[acck], writes=[ucbk])
            yield
            P.add("pe", lambda e: e.matmul(bG[:, 0:W], lhsT=gwa[:, n, :], rhs=ucb[:, 0:W], start=True, stop=True),
                  reads=["gwa", ucbk], writes=[bGk])
            yield
            P.add("act", lambda e: e.activation(out=t1[:, 0:W], in_=bG[:, 0:W], func=AF.Exp, scale=-1.0,
                                                bias=dv[:, DV_NBA + n:DV_NBA + n + 1]), reads=[bGk] + CONSTS, writes=[t1k])
            yield
            P.add("pe", lambda e: e.matmul(bG[:, 0:W], lhsT=gwx[:, n, :], rhs=ucb[:, 0:W], start=True, stop=True),
                  reads=["gwx", ucbk], writes=[bGk])
            yield
            P.add("act", lambda e: e.activation(out=t2[:, 0:W], in_=bG[:, 0:W], func=AF.Exp, scale=-1.0,
                                                bias=dv[:, DV_NBX + n:DV_NBX + n + 1]), reads=[bGk] + CONSTS, writes=[t2k])
            yield
            done_head[n] = True

        def rnn_tail(n):
            lane = n % 2
            sset = n % 4
            bU, bUk = banks8[lane]
            bG, bGk = banks8[lane]
            acc, acck = tl[:, 4 * sset + 0, :], ("tl", 4 * sset + 0)
            t1, t1k = tl[:, 4 * sset + 1, :], ("tl", 4 * sset + 1)
            t2, t2k = tl[:, 4 * sset + 2, :], ("tl", 4 * sset + 2)
            ta, tak = tl[:, 4 * sset + 3, :], ("tl", 4 * sset + 3)
            ucb, ucbk = tlb[:, sset, :], ("tlb", sset)
            while not done_head.get(n):
                yield
            for tt, ttk in ((t1, t1k), (t2, t2k)):
                P.add("act", lambda e, tt=tt: e.activation(out=tt[:, 0:W], in_=tt[:, 0:W], func=AF.Ln, bias=1.0),
                      reads=[ttk], writes=[ttk])
            yield
            for tt, ttk in ((t1, t1k), (t2, t2k)):
                P.add("act", lambda e, tt=tt: e.activation(out=tt[:, 0:W], in_=tt[:, 0:W], func=AF.Exp, scale=-1.0),
                      reads=[ttk], writes=[ttk])
            yield
            P.add("act", lambda e: e.activation(out=ta[:, 0:W], in_=t1[:, 0:W], func=AF.Exp, scale=dv[:, DV_C + n:DV_C + n + 1]),
                  reads=[t1k] + CONSTS, writes=[tak])
            P.add("dve", lambda e: e.tensor_tensor(out=t2[:, 0:W], in0=t2[:, 0:W], in1=acc[:, 0:W], op=ALU.mult),
                  reads=[t2k, acck], writes=[t2k])
            yield
            P.add("act", lambda e: e.activation(out=t1[:, 0:W], in_=t1[:, 0:W], func=AF.Exp, scale=dv[:, DV_CH + n:DV_CH + n + 1]),
                  reads=[t1k] + CONSTS, writes=[t1k])
            yield
            P.add("act", lambda e: e.activation(out=t1[:, 0:W], in_=t1[:, 0:W], func=AF.Ln, bias=1.0, scale=-1.0),
                  reads=[t1k], writes=[t1k])
            yield
            P.add("act", lambda e: e.activation(out=t1[:, 0:W], in_=t1[:, 0:W], func=AF.Exp, scale=0.5), reads=[t1k], writes=[t1k])
            yield
            P.add("dve", lambda e: e.tensor_tensor(out=t2[:, 0:W], in0=t2[:, 0:W], in1=t1[:, 0:W], op=ALU.mult),
                  reads=[t1k, t2k], writes=[t2k])
            yield
            for (c0, L, sidx) in segs:
                hcol = st[:, ST_H + sidx * 8 + n:ST_H + sidx * 8 + n + 1]
                P.add("dve", lambda e, c0=c0, L=L, hcol=hcol: e.tensor_tensor_scan(
                    out=acc[:, c0:c0 + L], data0=ta[:, c0:c0 + L], data1=t2[:, c0:c0 + L], initial=hcol,
                    op0=ALU.mult, op1=ALU.add), reads=[tak, t2k, ("st_h", sidx, n)], writes=[acck])
                P.add("dve", lambda e, c0=c0, L=L, hcol=hcol: e.tensor_copy(out=hcol, in_=acc[:, c0 + L - 1:c0 + L]),
                      reads=[acck], writes=[("st_h", sidx, n)])
            yield
            P.add("pool", lambda e: e.tensor_copy(out=yr[:, n, 0:W], in_=acc[:, 0:W]), reads=[acck], writes=[("yr", n)])
            yield
            done_tail[n] = True

        qk_cnt = {"i": 0}

        def qk_gate():
            while gate is not None and not gate["go"]:
                yield

        def qk_unit(hb, ql):
            qps, qpk = banks8[2 + ql]
            sps, spk = banks8[6 + ql]
            sq, sqk = tlb[:, 4 + ql, :], ("tlb", 4 + ql)
            lt, ltk = tl[:, 16 + ql, :], ("tl", 16 + ql)
            if hb < 8:
                sid = SL_Q + hb // 2
                wv, wkey = use_slot(sid)
                col0 = (hb % 2) * 128
            else:
                sid = SL_K
                wv, wkey = use_slot(sid)
                col0 = (hb - 8) * 128
            mm_fm(qps[:, 0:W], qpk, wv, wkey, col0, hTa, HT, W)
            done_slot(sid)
            yield
            P.add("act", lambda e: e.activation(out=sq[:, 0:W], in_=qps[:, 0:W], func=AF.Square), reads=[qpk], writes=[sqk])
            yield
            P.add("pe", lambda e: e.matmul(sps[:, 0:W], lhsT=onesb[:, :], rhs=sq[:, 0:W], start=True, stop=True),
                  reads=[sqk, "onesb"], writes=[spk])
            yield
            P.add("act", lambda e: e.activation(out=lt[:, 0:W], in_=sps[:, 0:W], func=AF.Ln, bias=EPS, scale=1.0 / HD),
                  reads=[spk], writes=[ltk])
            yield
            P.add("act", lambda e: e.activation(out=lt[:, 0:W], in_=lt[:, 0:W], func=AF.Exp, scale=-0.5), reads=[ltk], writes=[ltk])
            yield
            if hb < 8:
                P.add("dve", lambda e: e.scalar_tensor_tensor(
                    out=qT[:, hb, 0:W], in0=qps[:, 0:W], scalar=dv[:, DV_GQS:DV_GQS + 1], in1=lt[:, 0:W],
                    op0=ALU.mult, op1=ALU.mult), reads=[qpk, ltk] + CONSTS, writes=[("qT", hb)])
            else:
                g = hb - 8
                P.add("dve", lambda e: e.scalar_tensor_tensor(
                    out=kT[:, g, 128:128 + W], in0=qps[:, 0:W], scalar=prmc(P_GK, 1), in1=lt[:, 0:W],
                    op0=ALU.mult, op1=ALU.mult), reads=[qpk, ltk] + CONSTS, writes=[("kT", 1)])
                outs = []
                if is_sample:
                    for (c0, L, sidx) in segs:
                        outs.append((c0, L, kvs[0:L, sidx - 1, 0, g * 128:(g + 1) * 128], ("kvs", sidx - 1, 0)))
                elif ti == NT - 1:
                    outs.append((W - 128, 128, kvo[:, 0, g * 128:(g + 1) * 128], ("kvo", 0)))
                for (c0, L, dst, dkey) in outs:
                    kf, kfk = tl[:, 18, ql * 256:(ql + 1) * 256], ("tl18", ql)
                    P.add("dve", lambda e, c0=c0, L=L: e.scalar_tensor_tensor(
                        out=kf[:, 0:L], in0=qps[:, c0:c0 + L], scalar=prmc(P_GK, 1), in1=lt[:, c0:c0 + L],
                        op0=ALU.mult, op1=ALU.mult), reads=[qpk, ltk] + CONSTS, writes=[kfk])
                    P.add("pe", lambda e, L=L: e.transpose(sps[0:L, 0:128], kf[:, 0:L], identf[:, :]),
                          reads=[kfk, "identf"], writes=[spk])
                    P.add("act", lambda e, dst=dst, L=L: e.activation(out=dst, in_=sps[0:L, 0:128], func=AF.Copy),
                          reads=[spk], writes=[dkey])
            yield

        def v_unit(job, ql):
            (j, c0, L, dst, dkey, fout) = job
            vps, vpk = banks8[2 + ql]
            wv, wkey = use_slot(SL_V)
            for kc in range(8):
                P.add("pe", lambda e, kc=kc: e.matmul(vps[0:L, 0:256], lhsT=hTa[:, kc, c0:c0 + L], rhs=wv[:, kc, 0:256],
                                                      start=(kc == 0), stop=(kc == 7)), reads=[wkey] + HT, writes=[vpk])
            done_slot(SL_V)
            yield
            P.add("act", lambda e: e.activation(out=dst, in_=vps[0:L, 0:256].rearrange("p (g d) -> p g d", g=2), func=AF.Copy),
                  reads=[vpk], writes=[dkey])
            if fout is not None:
                P.add("act", lambda e: e.activation(out=fout[0], in_=vps[0:L, 0:256], func=AF.Copy), reads=[vpk], writes=[fout[1]])
            yield

        if not is_sample:
            vjobs = [(j, j * 128, 128, vb[:, 1 + j, :, 0:128], ("vb", 1 + j),
                      (kvo[:, 1, :], ("kvo", 1)) if (ti == NT - 1 and j == nsub - 1) else None) for j in range(nsub)]
        else:
            vjobs = [(sidx - 1, c0, L, vb[0:L, sidx, :, 0:128], ("vb", sidx), (kvs[0:L, sidx - 1, 1, :], ("kvs", sidx - 1, 1)))
                     for (c0, L, sidx) in segs]


        if not is_sample:
            vjobs = [(j, j * 128, 128, vb[:, 1 + j, :, 0:128], ("vb", 1 + j),
                      (kvo[:, 1, :], ("kvo", 1)) if (ti == NT - 1 and j == nsub - 1) else None) for j in range(nsub)]
        else:
            vjobs = [(sidx - 1, c0, L, vb[0:L, sidx, :, 0:128], ("vb", sidx), (kvs[0:L, sidx - 1, 1, :], ("kvs", sidx - 1, 1)))
                     for (c0, L, sidx) in segs]
        qk_flags = {}

        def qk_done(ql):
            qk_flags[ql] = True
            yield

        yield from par([chain([rnn_head(n) for n in (0, 2, 4, 6)]),
                        chain([rnn_head(n) for n in (1, 3, 5, 7)]),
                        chain([rnn_tail(n) for n in (0, 4)]),
                        chain([rnn_tail(n) for n in (1, 5)]),
                        chain([rnn_tail(n) for n in (2, 6)]),
                        chain([rnn_tail(n) for n in (3, 7)]),
                        chain([qk_gate()] + [qk_unit(hb, 0) for hb in range(0, 10, 2)] + [v_unit(jb, 0) for jb in vjobs[0::2]]
                              + [qk_done(0)]),
                        chain([qk_gate()] + [qk_unit(hb, 1) for hb in range(1, 10, 2)] + [v_unit(jb, 1) for jb in vjobs[1::2]]
                              + [qk_done(1)]),
                        attn_gen(ti, xs_i, W, rows, nsub, segs, first_tile, is_sample, flags=qk_flags, gate=wd_gate)])

    def attn_gen(ti, xs_i, W, rows, nsub, segs, first_tile, is_sample, flags=None, gate=None):
        while (flags is not None and not (flags.get(0) and flags.get(1))) or (gate is not None and not gate["go"]):
            yield
        acnt = [0]
        if not is_sample:
            ajobs = []
            for j in range(nsub):
                kbs = []
                if not (first_tile and j == 0):
                    kbs.append((0, kT[:, :, j * 128:(j + 1) * 128], ("kT", 0 if j == 0 else 1), vb[:, j, :, :], ("vb", j), 128))
                kbs.append((1, kT[:, :, (j + 1) * 128:(j + 2) * 128], ("kT", 1), vb[:, j + 1, :, :], ("vb", j + 1), 128))
                ajobs.append((j * 128, 128, kbs))
        else:
            ajobs = []
            for (c0, L, sidx) in segs:
                kbs = [(0, kTc[:, sidx - 1, :, :], ("kTc", sidx - 1), vc[:, sidx - 1, :, :], ("vc", sidx - 1), 128),
                       (1, kT[:, :, 128 + c0:128 + c0 + L], ("kT", 1), vb[0:L, sidx, :, :], ("vb", sidx), L)]
                ajobs.append((c0, L, kbs))
        for (c0, nq, kbs) in ajobs:
            pts = {}
            for g in range(2):
                for (kb, kTv, kTk, vv, vk, nk) in kbs:
                    sps, spk = banks8[2 + acnt[0] % 2]
                    acnt[0] += 1
                    so2 = sps[0:nk, 0:4 * nq]
                    so = so2.rearrange("p (h q) -> p h q", h=4)
                    P.add("pe", lambda e, so2=so2, kTv=kTv, g=g, nk=nk, c0=c0, nq=nq: e.matmul(
                        so2, lhsT=kTv[:, g, 0:nk], rhs=qT[:, 4 * g:4 * g + 4, c0:c0 + nq], start=True, stop=False),
                        reads=[kTk] + [("qT", 4 * g + i) for i in range(4)], writes=[spk])
                    P.add("pe", lambda e, so2=so2, kb=kb, g=g, nk=nk, nq=nq: e.matmul(
                        so2, lhsT=identb[0:nk, 0:nk], rhs=btab[0:nk, kb, 4 * g:4 * g + 4, 0:nq], start=False, stop=True),
                        reads=["identb", "btab"], writes=[spk])
                    pi = cnt["pT"] % 4
                    cnt["pT"] += 1
                    po = pT[0:nk, pi, 0:4 * nq].rearrange("p (h q) -> p h q", h=4)
                    P.add("act", lambda e, po=po, so=so: e.activation(out=po, in_=so, func=AF.Exp), reads=[spk],
                          writes=[("pT", pi)])
                    pts[(g, kb)] = (po, ("pT", pi), vv, vk, nk)
                    yield
            dps, dpk = bankB()
            for h in range(8):
                g = h // 4
                lst = [pts[(g, kb)] for (kb, *_r) in kbs]
                for idx, (po, pk, vv, vk, nk) in enumerate(lst):
                    P.add("pe", lambda e, po=po, vv=vv, nk=nk, h=h, g=g, idx=idx, n=len(lst), nq=nq: e.matmul(
                        psC[0:nq, h * 128:(h + 1) * 128], lhsT=po[:, h % 4, :], rhs=vv[0:nk, g, 0:128],
                        start=(idx == 0), stop=(idx == n - 1)), reads=[pk, vk], writes=[("psC", h // 4)])
                    P.add("pe", lambda e, po=po, vv=vv, nk=nk, h=h, g=g, idx=idx, n=len(lst), nq=nq, dps=dps: e.matmul(
                        dps[0:nq, h:h + 1], lhsT=po[:, h % 4, :], rhs=onesb[0:nk, 0:1],
                        start=(idx == 0), stop=(idx == n - 1)), reads=[pk, "onesb"], writes=[dpk])
            yield
            P.add("dve", lambda e, dps=dps, nq=nq: e.tensor_tensor(out=smal[0:nq, 0:8], in0=dps[0:nq, 0:8],
                                                                   in1=dv[0:nq, DV_ESK:DV_ESK + 8], op=ALU.add),
                  reads=[dpk] + CONSTS, writes=["smal"])
            P.add("dve", lambda e, nq=nq: e.reciprocal(out=smal[0:nq, 8:16], in_=smal[0:nq, 0:8]), reads=["smal"],
                  writes=["smal2"])
            yi = cnt["yat"] % 2
            cnt["yat"] += 1
            for half in range(2):
                P.add("dve", lambda e, half=half, yi=yi, nq=nq: e.tensor_tensor(
                    out=yat[0:nq, yi, half * 512:(half + 1) * 512].rearrange("p (h d) -> p h d", h=4),
                    in0=psC[0:nq, half * 512:(half + 1) * 512].rearrange("p (h d) -> p h d", h=4),
                    in1=smal[0:nq, 8 + 4 * half:12 + 4 * half].unsqueeze(2).to_broadcast([nq, 4, 128]), op=ALU.mult),
                    reads=[("psC", half), "smal2"], writes=[("yat", yi)])
            yield
            bk, bkk = bankB()
            bkb = bk[:, :].bitcast(BF16)
            for h in range(8):
                P.add("pe", lambda e, h=h, yi=yi, bkb=bkb, nq=nq: e.transpose(bkb[:, h * 128:h * 128 + nq],
                                                                               yat[0:nq, yi, h * 128:(h + 1) * 128],
                                                                               identb[0:nq, 0:nq]),
                      reads=[("yat", yi), "identb"], writes=[bkk])
            P.add("act", lambda e, bkb=bkb, c0=c0, nq=nq: e.activation(
                out=ya[:, :, c0:c0 + nq], in_=bkb.rearrange("p (h t) -> p h t", h=8)[:, :, 0:nq], func=AF.Copy),
                reads=[bkk], writes=["ya"])
            yield
        if not is_sample:
            P.add("pool", lambda e: e.tensor_copy(out=kT[:, :, 0:128], in_=kT[:, :, 512:640]), reads=[("kT", 1)], writes=[("kT", 0)])
            P.add("pool", lambda e: e.tensor_copy(out=vb[:, 0, :, 0:128], in_=vb[:, 4, :, 0:128]), reads=[("vb", 4)],
                  writes=[("vb", 0)])
        yield

    def mid(ti, xs_i, W, rows, nsub, segs, first_tile, is_sample):
        YR = [("yr", n) for n in range(8)]
        for pr in range(4):
            s_rp = load_slot(SL_M + 4 * pr + 0)
            s_ap = load_slot(SL_M + 4 * pr + 1)
            s_gr = load_slot(SL_M + 4 * pr + 2)
            s_ga = load_slot(SL_M + 4 * pr + 3)
            for sub in range(2):
                m = 2 * pr + sub
                col0 = sub * 128
                p3, p3k = bankA()
                mm_fm(p3[:, 0:W], p3k, s_gr[0], s_gr[1], col0, hTa, HT, W)
                p4, p4k = bankA()
                mm_fm(p4[:, 0:W], p4k, s_ga[0], s_ga[1], col0, hTa, HT, W)
                p1, p1k = bankA()
                mm_fm(p1[:, 0:W], p1k, s_rp[0], s_rp[1], col0, yr, YR, W)
                p2, p2k = bankA()
                mm_fm(p2[:, 0:W], p2k, s_ap[0], s_ap[1], col0, ya, ["ya"], W)
                t1, t1k = tmpf()
                t2, t2k = tmpf()
                P.add("act", lambda e, m=m, t1=t1, p3=p3: e.activation(out=t1[:, 0:W], in_=p3[:, 0:W], func=AF.Exp,
                                                                        bias=dv[:, DV_NBG + m:DV_NBG + m + 1], scale=-1.0),
                      reads=[p3k] + CONSTS, writes=[t1k])
                P.add("act", lambda e, m=m, t2=t2, p4=p4: e.activation(out=t2[:, 0:W], in_=p4[:, 0:W], func=AF.Exp,
                                                                        bias=dv[:, DV_NBG + 8 + m:DV_NBG + 9 + m], scale=-1.0),
                      reads=[p4k] + CONSTS, writes=[t2k])
                for tt, ttk in ((t1, t1k), (t2, t2k)):
                    P.add("act", lambda e, tt=tt: e.activation(out=tt[:, 0:W], in_=tt[:, 0:W], func=AF.Ln, bias=1.0),
                          reads=[ttk], writes=[ttk])
                for tt, ttk in ((t1, t1k), (t2, t2k)):
                    P.add("act", lambda e, tt=tt: e.activation(out=tt[:, 0:W], in_=tt[:, 0:W], func=AF.Exp, scale=-1.0),
                          reads=[ttk], writes=[ttk])
                m1, m1k = tmpb()
                m2, m2k = tmpb()
                P.add("dve", lambda e, m1=m1, t1=t1, p1=p1: e.tensor_tensor(
                    out=m1[:, 0:W], in0=p1[:, 0:W], in1=t1[:, 0:W], op=ALU.mult), reads=[t1k, p1k], writes=[m1k])
                P.add("dve", lambda e, m2=m2, t2=t2, p2=p2: e.tensor_tensor(
                    out=m2[:, 0:W], in0=p2[:, 0:W], in1=t2[:, 0:W], op=ALU.mult), reads=[t2k, p2k], writes=[m2k])
                P.add("dve", lambda e, m=m, m1=m1, m2=m2: e.tensor_tensor(out=mixed[:, m, 0:W], in0=m1[:, 0:W], in1=m2[:, 0:W],
                                                                          op=ALU.add), reads=[m1k, m2k], writes=[("mixed", m)])
            for sl_ in (s_rp, s_ap, s_gr, s_ga):
                release(sl_)
        MX = [("mixed", m) for m in range(8)]
        run(tm_proj(xs_i, rows, nsub, lambda cg: [SL_O + cg], 8, mixed, MX))
        run(norm_transpose(xs_i, rows, nsub, 4, hTb, "hTb"))

    def ffn_up(ti, xs_i, W, rows, nsub, segs):
        for pr in range(11):
            s_a = load_slot(SL_UP + 2 * pr)
            s_b = load_slot(SL_UP + 2 * pr + 1)
            for sub in range(2):
                kc2 = 2 * pr + sub
                col0 = sub * 128
                aps, apk = bankA()
                mm_fm(aps[:, 0:W], apk, s_a[0], s_a[1], col0, hTb, ["hTb"], W)
                bps, bpk = bankA()
                mm_fm(bps[:, 0:W], bpk, s_b[0], s_b[1], col0, hTb, ["hTb"], W)
                acc, acck = tmpf()
                conv_taps(aps, apk, acc, acck, W, segs, 3, lambda j, kc2=kc2: prmc(P_FW + j * NFB + kc2, 1),
                          prmc(P_FB + kc2, 1),
                          lambda sidx, kc2=kc2: st[:, ST_FH + sidx * 44 + kc2 * 2:ST_FH + sidx * 44 + kc2 * 2 + 2],
                          lambda sidx, kc2=kc2: ("st_fh", sidx, kc2))
                gl, glk = tmpf()
                P.add("act", lambda e, gl=gl, acc=acc: e.activation(out=gl[:, 0:W], in_=acc[:, 0:W], func=AF.Gelu),
                      reads=[acck], writes=[glk])
                P.add("dve", lambda e, kc2=kc2, gl=gl, bps=bps: e.tensor_tensor(out=actb[:, kc2, 0:W], in0=bps[:, 0:W],
                                                                                in1=gl[:, 0:W], op=ALU.mult),
                      reads=[bpk, glk], writes=[("act", kc2)])
                yield
            release(s_a)
            release(s_b)

    def wdown_gen(ti, xs_i, W, rows, nsub, segs):
        yield from tm_proj(xs_i, rows, nsub, lambda cg: [SL_DN + 3 * cg + k for k in range(3)], NFB, actb,
                           [("act", k) for k in range(NFB)])

    def load_x(ti):
        b = ti % 2
        if ti < NT:
            P.dma(lambda e: e.dma_start(out=xt[:, b, :, :], in_=xp[ti * 512:(ti + 1) * 512, :].rearrange("(j p) d -> p j d", p=128)),
                  d_x[b], writes=[("xt", b, j) for j in range(4)])
        else:
            P.dma(lambda e: e.dma_start(out=xt[0:32, b, 0, :], in_=xs[:, :]), d_x[b], writes=[("xt", b, 0)])

    def targs(ti):
        b = ti % 2
        if ti < NT:
            return (ti, b, 512, 128, 4, [(0, 512, 0)], ti == 0, False)
        return (ti, b, 32, 32, 1, [(0, 16, 1), (16, 16, 2)], False, True)

    load_x(0)
    run(par([prepass_ffn_gen(), chain([stageA_gen(*targs(0)), front_gen(*targs(0))])]))
    for ti in range(NT + 1):
        b = ti % 2
        if ti + 1 <= NT:
            load_x(ti + 1)
        ta_ = targs(ti)
        mid(*ta_)
        if ti == 0:
            P.fence()
        gens = [ffn_up(*ta_[:6])]
        if ti + 1 <= NT:
            gens.append(stageA_gen(*targs(ti + 1)))
        run(par(gens))
        gate = {"go": False}

        def wd_then_open(g=gate, a=ta_[:6]):
            yield from wdown_gen(*a)
            g["go"] = True

        gens = [wd_then_open()]
        if ti + 1 <= NT:
            gens.append(front_gen(*targs(ti + 1), gate=None, wd_gate=gate))
        run(par(gens))
        if ti < NT:
            P.dma(lambda e, ti=ti, b=b: e.dma_start(out=yp[ti * 512:(ti + 1) * 512, :].rearrange("(j p) d -> p j d", p=128),
                                                    in_=xt[:, b, :, :]), d_y[b], reads=[("xt", b, j) for j in range(4)],
                  queue="pool")
        else:
            P.dma(lambda e, b=b: e.dma_start(out=ys[:, :], in_=xt[0:32, b, 0, :]), d_y[b], reads=[("xt", b, 0)], queue="pool")
    stkeys = (["st_hu", "st_h", "st_fh"] + [("st_hu", s, n) for s in range(3) for n in range(8)]
              + [("st_h", s, n) for s in range(3) for n in range(8)] + [("st_fh", s, k) for s in range(3) for k in range(NFB)])
    P.dma(lambda e: e.dma_start(out=st_o[:, :], in_=st[:, :]), d_fin, reads=stkeys, queue="pool")
    P.dma(lambda e: e.dma_start(out=kp_o[:, :], in_=kvo[:, 0, :]), d_fin, reads=[("kvo", 0)], queue="pool")
    P.dma(lambda e: e.dma_start(out=vp_o[:, :], in_=kvo[:, 1, :]), d_fin, reads=[("kvo", 1)], queue="pool")
    for s in range(2):
        P.dma(lambda e, s=s: e.dma_start(out=ks_o[s, 112:128, :], in_=kvs[0:16, s, 0, :]), d_fin, reads=[("kvs", s, 0)], queue="pool")
        P.dma(lambda e, s=s: e.dma_start(out=vs_o[s, 112:128, :], in_=kvs[0:16, s, 1, :]), d_fin, reads=[("kvs", s, 1)], queue="pool")
    P.finish()
    return nc, P


def _fm(v, nblk):
    v = np.asarray(v, np.float32)
    lead = v.shape[:-1]
    v = v.reshape(lead + (nblk, 128))
    return np.moveaxis(v, -1, 0)


_CACHE = {}


def kernel(x_prompt, x_sample, state_rnn_conv, state_rnn_h, cache_attn_k, cache_attn_v, state_ffn_conv,
           norm_mix_g, w_in, b_gate, rnn_conv_w, rnn_conv_b, rnn_gate_a_w, rnn_gate_a_b, rnn_gate_x_w,
           rnn_gate_x_b, rnn_lambda, q_norm_g, k_norm_g, attn_sinks, w_rnn_proj, w_attn_proj, w_out,
           norm_ffn_g, w_up, ffn_conv_w, ffn_conv_b, w_down):
    f32 = lambda a: np.ascontiguousarray(np.asarray(a, np.float32))
    x_prompt = f32(x_prompt)
    B, T, _ = x_prompt.shape
    NC = 8
    assert B == NC and x_sample.shape[0] == 2 * NC and x_sample.shape[1] == 16
    prm = np.zeros((128, NPRM), np.float32)
    prm[:, P_GM:P_GM + 8] = _fm(norm_mix_g[0], 8)
    prm[:, P_GF:P_GF + 8] = _fm(norm_ffn_g[0], 8)
    prm[:, P_CW:P_CW + 32] = _fm(rnn_conv_w[0], 8).reshape(128, 32)
    prm[:, P_CB:P_CB + 8] = _fm(rnn_conv_b[0], 8)
    prm[:, P_BA:P_BA + 8] = _fm(rnn_gate_a_b[0], 8)
    prm[:, P_BX:P_BX + 8] = _fm(rnn_gate_x_b[0], 8)
    prm[:, P_LAM:P_LAM + 8] = _fm(rnn_lambda[0], 8)
    prm[:, P_BG:P_BG + 16] = _fm(b_gate[0], 16)
    prm[:, P_FW:P_FW + 66] = _fm(ffn_conv_w[0], NFB).reshape(128, 66)
    prm[:, P_FB:P_FB + 22] = _fm(ffn_conv_b[0], NFB)
    prm[:, P_GQ] = np.asarray(q_norm_g[0], np.float32)
    prm[:, P_GK] = np.asarray(k_norm_g[0], np.float32)
    prm[:, P_SINK:P_SINK + 8] = np.broadcast_to(np.asarray(attn_sinks[0], np.float32)[None, :], (128, 8))
    key = T
    if key not in _CACHE:
        _CACHE[key] = build_program(T)[0]
    nc = _CACHE[key]
    shared = {
        "prm": prm, "w_in": f32(w_in[0]), "gate_a": f32(rnn_gate_a_w[0]), "gate_x": f32(rnn_gate_x_w[0]),
        "w_rnn_proj": f32(w_rnn_proj[0]), "w_attn_proj": f32(w_attn_proj[0]), "w_out": f32(w_out[0]),
        "w_up": f32(w_up[0]), "w_down": f32(w_down[0]),
    }
    src = np.asarray(state_rnn_conv[0], np.float32)
    sh = np.asarray(state_rnn_h[0], np.float32)
    sf = np.asarray(state_ffn_conv[0], np.float32)
    ck = np.asarray(cache_attn_k[0], np.float32).reshape(2 * NC, 128, 256)
    cv = np.asarray(cache_attn_v[0], np.float32).reshape(2 * NC, 128, 256)
    xs_all = f32(x_sample)
    in_maps = []
    for c in range(NC):
        sst = np.zeros((128, 152), np.float32)
        for s in range(2):
            q = 2 * c + s
            sst[:, s * 24:(s + 1) * 24] = np.transpose(_fm(src[q], 8), (0, 2, 1)).reshape(128, 24)
            sst[:, 48 + s * 8:48 + (s + 1) * 8] = _fm(sh[q], 8)
            sst[:, 64 + s * 44:64 + (s + 1) * 44] = np.transpose(_fm(sf[q], NFB), (0, 2, 1)).reshape(128, 44)
        m = dict(shared)
        m["xp"] = x_prompt[c]
        m["xs"] = np.ascontiguousarray(xs_all[2 * c:2 * c + 2].reshape(32, D))
        m["sst"] = sst
        m["ck"] = np.ascontiguousarray(ck[2 * c:2 * c + 2])
        m["cv"] = np.ascontiguousarray(cv[2 * c:2 * c + 2])
        in_maps.append(m)
    res = run_bass_kernel_spmd(nc, in_maps, core_ids=list(range(NC)))
    R = res.results
    y_p = np.stack([R[c]["yp"] for c in range(NC)])[:, :, :]
    y_s = np.concatenate([R[c]["ys"].reshape(2, 16, D) for c in range(NC)], axis=0)
    st = np.stack([R[c]["st_o"] for c in range(NC)])

    def unfm(a):
        return np.transpose(a, (2, 1, 0)).reshape(a.shape[2], -1)

    rc_p = np.stack([unfm(st[c][:, ST_HU:ST_HU + 24].reshape(128, 8, 3)) for c in range(NC)])[None]
    rc_s = np.stack([unfm(st[c][:, ST_HU + 24 * (1 + s):ST_HU + 24 * (2 + s)].reshape(128, 8, 3))
                     for c in range(NC) for s in range(2)])[None]
    h_p = np.stack([st[c][:, ST_H:ST_H + 8].T.reshape(-1) for c in range(NC)])[None]
    h_s = np.stack([st[c][:, ST_H + 8 * (1 + s):ST_H + 8 * (2 + s)].T.reshape(-1) for c in range(NC) for s in range(2)])[None]
    f_p = np.stack([unfm(st[c][:, ST_FH:ST_FH + 44].reshape(128, NFB, 2)) for c in range(NC)])[None]
    f_s = np.stack([unfm(st[c][:, ST_FH + 44 * (1 + s):ST_FH + 44 * (2 + s)].reshape(128, NFB, 2))
                    for c in range(NC) for s in range(2)])[None]
    k_p = np.stack([R[c]["kp"].reshape(128, 2, 128) for c in range(NC)])[None]
    v_p = np.stack([R[c]["vp"].reshape(128, 2, 128) for c in range(NC)])[None]
    k_s = np.concatenate([R[c]["ks"].reshape(2, 128, 2, 128) for c in range(NC)], axis=0)[None]
    v_s = np.concatenate([R[c]["vs"].reshape(2, 128, 2, 128) for c in range(NC)], axis=0)[None]
    outs = (y_p, y_s, rc_p, rc_s, h_p, h_s, k_p, k_s, v_p, v_s, f_p, f_s)
    return tuple(np.ascontiguousarray(o, dtype=np.float32) for o in outs)
```

```python
import contextlib
import numpy as np
import concourse.bass as bass
import concourse.mybir as mybir
from concourse.bass_utils import run_bass_kernel_spmd
from concourse.alu_op_type import AluOpType as ALU

F32 = mybir.dt.float32
BF16 = mybir.dt.bfloat16
AF = mybir.ActivationFunctionType


class _Op:
    __slots__ = ("idx", "eng", "fn", "deps", "is_dma", "dsem", "dma_val", "needs_sig", "sigval", "waits")


class _DSem:
    def __init__(self, sem, name, serial=False):
        self.sem = sem
        self.name = name
        self.count = 0
        self.serial = serial
        self.last = None


class Prog:
    def __init__(self, nc):
        self.nc = nc
        self.ops = []
        self.last_writer = {}
        self.readers = {}
        self.stack = contextlib.ExitStack()
        self.eng_sems = {}
        for e in ("pe", "act", "dve", "pool"):
            self.eng_sems[e] = self.stack.enter_context(nc.semaphore("s_" + e))
        self.dsems = []
        self.final = []

    def sbuf(self, name, shape, dtype):
        return self.stack.enter_context(self.nc.sbuf_tensor(name, list(shape), dtype))

    def psum(self, name, shape, dtype):
        return self.stack.enter_context(self.nc.psum_tensor(name, list(shape), dtype))

    def dsem(self, name, serial=False):
        d = _DSem(self.stack.enter_context(self.nc.semaphore("d_" + name)), name, serial)
        self.dsems.append(d)
        return d

    def _mk(self, eng, fn, reads, writes):
        op = _Op()
        op.idx = len(self.ops)
        op.eng = eng
        op.fn = fn
        op.is_dma = False
        op.dsem = None
        op.dma_val = 0
        op.needs_sig = False
        op.sigval = 0
        deps = set()
        for r in reads:
            if r in self.last_writer:
                deps.add(self.last_writer[r])
        for w in writes:
            if w in self.last_writer:
                deps.add(self.last_writer[w])
            deps |= self.readers.get(w, set())
        op.deps = deps
        for r in reads:
            self.readers.setdefault(r, set()).add(op.idx)
        for w in writes:
            self.last_writer[w] = op.idx
            self.readers[w] = set()
        self.ops.append(op)
        return op

    def add(self, eng, fn, reads=(), writes=()):
        return self._mk(eng, fn, reads, writes)

    def fence(self):
        last = {}
        for op in self.ops:
            if op.fn is None:
                continue
            if op.is_dma:
                last[("d", id(op.dsem))] = op.idx
            else:
                last[("e", op.eng)] = op.idx
        deps = set(last.values())
        for eng in ("sp", "act", "dve", "pool", "pe"):
            op = self._mk(eng, None, (), ())
            op.deps = set(deps)

    def dma(self, fn, dsem, reads=(), writes=(), queue="sp"):
        op = self._mk(queue, fn, reads, writes)
        op.is_dma = True
        op.dsem = dsem
        if dsem.serial and dsem.last is not None:
            op.deps.add(dsem.last)
        dsem.last = op.idx
        dsem.count += 1
        op.dma_val = 16 * dsem.count
        return op

    def finish(self, final_dsems=None):
        ops = self.ops
        for op in ops:
            for d in op.deps:
                dop = ops[d]
                if dop.is_dma:
                    continue
                if dop.eng == "pe" and op.eng == "pe" and not op.is_dma:
                    continue
                dop.needs_sig = True
        cnt = {}
        for op in ops:
            if (not op.is_dma) and op.needs_sig:
                cnt[op.eng] = cnt.get(op.eng, 0) + 1
                op.sigval = cnt[op.eng]
        known = {}
        streams = {}
        for op in ops:
            kn = known.setdefault(op.eng, {})
            waits = {}
            for d in op.deps:
                dop = ops[d]
                if dop.is_dma:
                    ch, val = ("d", id(dop.dsem)), dop.dma_val
                    sem = dop.dsem.sem
                else:
                    if dop.eng == "pe" and op.eng == "pe" and not op.is_dma:
                        continue
                    ch, val = ("e", dop.eng), dop.sigval
                    sem = self.eng_sems[dop.eng]
                if kn.get(ch, 0) >= val:
                    continue
                if ch not in waits or waits[ch][1] < val:
                    waits[ch] = (sem, val)
            for ch, (sem, val) in waits.items():
                kn[ch] = val
            op.waits = list(waits.values())
            streams.setdefault(op.eng, []).append(op)
        if final_dsems is None:
            final_dsems = self.dsems
        finals = [(d.sem, 16 * d.count) for d in final_dsems if d.count > 0]
        eng_sems = self.eng_sems
        self.n_ops = {k: len(v) for k, v in streams.items()}

        def emit(name, e, tail=False):
            for op in streams.get(name, []):
                for sem, val in op.waits:
                    e.wait_ge(sem, val)
                if op.fn is None:
                    continue
                ins = op.fn(e)
                if op.is_dma:
                    ins.then_inc(op.dsem.sem, 16)
                elif op.needs_sig:
                    ins.then_inc(eng_sems[name], 1)
            if tail:
                for sem, val in finals:
                    e.wait_ge(sem, val)

        with self.nc.Block() as block:
            @block.sync
            def _(e):
                emit("sp", e, tail=True)

            @block.scalar
            def _(e):
                emit("act", e)

            @block.vector
            def _(e):
                emit("dve", e)

            @block.gpsimd
            def _(e):
                emit("pool", e)

            @block.tensor
            def _(e):
                emit("pe", e)
        self.stack.close()


D = 1024
NH = 8
NKV = 2
HD = 128
DFF = 2816
NFB = DFF // 128
INW = 4608
EPS = 1e-6
ATTN_SCALE = HD ** -0.5
NSLOT = 64
NRING = 6
LN_HALF = float(np.log(0.5))
NEG = -30000.0

P_GM, P_GF, P_CW, P_CB, P_BA, P_BX, P_LAM, P_BG, P_FW, P_FB, P_GQ, P_GK, P_SINK = (
    0, 8, 16, 48, 56, 64, 72, 80, 96, 162, 184, 185, 186)
NPRM = 194
ST_HU, ST_H, ST_FH = 0, 72, 96
NST = 228

SL_U, SL_Q, SL_K, SL_V, SL_M, SL_O, SL_UP, SL_DN = 0, 4, 8, 9, 10, 26, 30, 52


class _Stop(Exception):
    pass


_DBG_STOP = None


def _chk(level):
    if _DBG_STOP is not None and level >= _DBG_STOP:
        raise _Stop()


def build_program(T, n_cores_hint=8):
    assert T % 512 == 0
    NT = T // 512
    nc = bass.Bass("TRN2", target_bir_lowering=False)

    def din(name, shape, dt=F32):
        return nc.dram_tensor(name, list(shape), dt, kind="ExternalInput").ap()

    def dout(name, shape, dt=F32):
        return nc.dram_tensor(name, list(shape), dt, kind="ExternalOutput").ap()

    xp = din("xp", [T, D])
    xs = din("xs", [32, D])
    prm_d = din("prm", [128, NPRM])
    sst_d = din("sst", [128, 152])
    ck_d = din("ck", [2, 128, 256])
    cv_d = din("cv", [2, 128, 256])
    w_in_d = din("w_in", [D, INW])
    ga_d = din("gate_a", [8, 128, 128])
    gx_d = din("gate_x", [8, 128, 128])
    wrp_d = din("w_rnn_proj", [D, D])
    wap_d = din("w_attn_proj", [D, D])
    wout_d = din("w_out", [D, D])
    wup_d = din("w_up", [D, 2 * DFF])
    wdn_d = din("w_down", [DFF, D])

    yp = dout("yp", [T, D])
    ys = dout("ys", [32, D])
    st_o = dout("st_o", [128, NST])
    kp_o = dout("kp", [128, 256])
    vp_o = dout("vp", [128, 256])
    ks_o = dout("ks", [2, 128, 256])
    vs_o = dout("vs", [2, 128, 256])

    wsc = nc.dram_tensor("wsc", [NSLOT, 128, 2048], BF16).ap()

    P = Prog(nc)
    xt = P.sbuf("xt", [128, 2, 4, D], F32)
    ss = P.sbuf("ss", [128, 8], F32)
    rs = P.sbuf("rs", [128, 8], F32)
    xn = P.sbuf("xn", [128, 2, D], BF16)
    hTa = P.sbuf("hTa", [128, 8, 512], BF16)
    hTb = P.sbuf("hTb", [128, 8, 512], BF16)
    tl = P.sbuf("tl", [128, 19, 512], F32)
    tlb = P.sbuf("tlb", [128, 6, 512], BF16)
    yr = P.sbuf("yr", [128, 8, 512], BF16)
    ya = P.sbuf("ya", [128, 8, 512], BF16)
    mixed = P.sbuf("mixed", [128, 8, 512], BF16)
    qT = P.sbuf("qT", [128, 8, 512], BF16)
    kT = P.sbuf("kT", [128, 2, 640], BF16)
    kTc = P.sbuf("kTc", [128, 2, 2, 128], BF16)
    vb = P.sbuf("vb", [128, 5, 2, 128], BF16)
    vc = P.sbuf("vc", [128, 2, 2, 128], BF16)
    btab = P.sbuf("btab", [128, 2, 8, 128], BF16)
    pT = P.sbuf("pT", [128, 4, 512], BF16)
    yat = P.sbuf("yat", [128, 2, D], BF16)
    actb = P.sbuf("actb", [128, NFB, 512], BF16)
    ring = P.sbuf("ring", [128, NRING, 2048], BF16)
    prm = P.sbuf("prm_s", [128, NPRM], F32)
    dv = P.sbuf("dv", [128, 64], F32)
    st = P.sbuf("st", [128, NST], F32)
    gwa = P.sbuf("gwa", [128, 8, 128], BF16)
    gwx = P.sbuf("gwx", [128, 8, 128], BF16)
    identf = P.sbuf("identf", [128, 128], F32)
    identb = P.sbuf("identb", [128, 128], BF16)
    onesb = P.sbuf("onesb", [128, 128], BF16)
    kvo = P.sbuf("kvo", [128, 2, 256], F32)
    kvs = P.sbuf("kvs", [128, 2, 2, 256], F32)
    smal = P.sbuf("smal", [128, 16], F32)
    act32 = actb[:, 0:16, :].rearrange("p a b -> p (a b)").bitcast(F32).rearrange("p (a b) -> p a b", a=8)
    NSTG = 4
    stg_f = [actb[:, 4 * i:4 * i + 4, :].rearrange("p a b -> p (a b)").bitcast(F32) for i in range(NSTG)]
    stg_b = [yr[:, 2 * i:2 * i + 2, :].rearrange("p a b -> p (a b)") for i in range(NSTG)]
    stg_f2 = [actb[:, 4 * i:4 * i + 4, :].rearrange("p a b -> p (a b)").bitcast(F32) for i in range(3)]
    stg_b2 = [actb[:, 12 + 2 * i:14 + 2 * i, :].rearrange("p a b -> p (a b)") for i in range(3)]
    gstg = tl[:, 2:4, :].rearrange("p a b -> p (a b)")

    psA = [P.psum(f"psA{i}", [128, 512], F32) for i in range(4)]
    psB = [P.psum(f"psB{i}", [128, 512], F32) for i in range(2)]
    psC = P.psum("psC", [128, 1024], F32)
    banks8 = [(psA[0], ("psA", 0)), (psA[1], ("psA", 1)), (psA[2], ("psA", 2)), (psA[3], ("psA", 3)),
              (psC[:, 0:512], ("psC", 0)), (psC[:, 512:1024], ("psC", 1)), (psB[0], ("psB", 0)), (psB[1], ("psB", 1))]
    cnt = {"A": 0, "B": 0, "tf": 0, "tb": 0, "pT": 0, "ring": 0, "xn": 0, "yat": 0}

    def bankA():
        i = cnt["A"] % 4
        cnt["A"] += 1
        return psA[i], ("psA", i)

    def bankB():
        i = cnt["B"] % 2
        cnt["B"] += 1
        return psB[i], ("psB", i)

    def tmpf():
        i = cnt["tf"] % 19
        cnt["tf"] += 1
        return tl[:, i, :], ("tl", i)

    def tmpb():
        i = cnt["tb"] % 6
        cnt["tb"] += 1
        return tlb[:, i, :], ("tlb", i)

    d_const = P.dsem("const", serial=True)
    d_x = [P.dsem("x0"), P.dsem("x1")]
    d_ring = [P.dsem(f"ring{i}") for i in range(NRING)]
    d_stg = [P.dsem(f"stg{i}") for i in range(7)]
    d_wst = [P.dsem(f"wst{i}") for i in range(7)]
    d_y = [P.dsem("y0"), P.dsem("y1")]
    d_fin = P.dsem("fin")
    d_cp = P.dsem("cp")

    def prmc(c0, n=1):
        return prm[:, c0:c0 + n]

    P.dma(lambda e: e.dma_start(out=prm[:, :], in_=prm_d[:, :]), d_const, writes=["prm"])
    P.dma(lambda e: e.dma_start(out=st[:, ST_HU + 24:ST_HU + 72], in_=sst_d[:, 0:48]), d_const, writes=["st_hu"])
    P.dma(lambda e: e.dma_start(out=st[:, ST_H + 8:ST_H + 24], in_=sst_d[:, 48:64]), d_const, writes=["st_h"])
    P.dma(lambda e: e.dma_start(out=st[:, ST_FH + 44:ST_FH + 132], in_=sst_d[:, 64:152]), d_const, writes=["st_fh"])
    P.add("dve", lambda e: e.memset(st[:, ST_HU:ST_HU + 24], 0.0), writes=["st_hu"])
    P.add("dve", lambda e: e.memset(st[:, ST_H:ST_H + 8], 0.0), writes=["st_h"])
    P.add("dve", lambda e: e.memset(st[:, ST_FH:ST_FH + 44], 0.0), writes=["st_fh"])
    P.add("pool", lambda e: e.iota(identf[:, :], pattern=[[1, 128]], base=0, channel_multiplier=-1,
                                    allow_small_or_imprecise_dtypes=True), writes=["identf"])
    P.add("dve", lambda e: e.tensor_scalar(out=identf[:, :], in0=identf[:, :], scalar1=0.0, scalar2=None, op0=ALU.is_equal),
          reads=["identf"], writes=["identf"])
    P.add("dve", lambda e: e.tensor_copy(out=identb[:, :], in_=identf[:, :]), reads=["identf"], writes=["identb"])
    P.add("dve", lambda e: e.memset(onesb[:, :], 1.0), writes=["onesb"])
    btf = act32[:, 0:4, :].rearrange("p a b -> p (a b)").rearrange("p (k h q) -> p k h q", k=2, h=8)
    btf2 = act32[:, 4:8, :].rearrange("p a b -> p (a b)").rearrange("p (k h q) -> p k h q", k=2, h=8)
    P.add("pool", lambda e: e.iota(btf, pattern=[[-128, 2], [0, 8], [1, 128]], base=128, channel_multiplier=-1,
                                    allow_small_or_imprecise_dtypes=True), writes=["btf"])
    P.add("dve", lambda e: e.tensor_scalar(out=btf2, in0=btf, scalar1=-1.0, scalar2=None, op0=ALU.mult),
          reads=["btf"], writes=["btf2"])
    P.add("dve", lambda e: e.tensor_tensor(out=btf, in0=btf, in1=btf2, op=ALU.max), reads=["btf", "btf2"], writes=["btf"])
    for h in range(8):
        P.add("dve", lambda e, h=h: e.tensor_scalar(out=btf[:, :, h, :], in0=btf[:, :, h, :], scalar1=-(2.0 ** -(h + 1)),
                                                    scalar2=None, op0=ALU.mult), reads=["btf"], writes=["btf"])
    P.add("dve", lambda e: e.memset(btf[64:128, 1, :, 0:64], NEG), reads=["btf"], writes=["btf"])
    P.add("dve", lambda e: e.memset(btf[0:64, 0, :, 64:128], NEG), reads=["btf"], writes=["btf"])
    P.add("dve", lambda e: e.tensor_copy(out=btab[:, :, :, :], in_=btf), reads=["btf"], writes=["btab"])
    DV_CH, DV_C, DV_NBA, DV_NBX, DV_NBG, DV_GQS, DV_ESK, DV_MH = 0, 8, 16, 24, 32, 48, 49, 57
    DV_CF = DV_C
    P.add("act", lambda e: e.activation(out=dv[:, DV_CH:DV_CH + 8], in_=prmc(P_LAM, 8), func=AF.Exp, scale=-1.0),
          reads=["prm"], writes=["dv_c"])
    P.add("act", lambda e: e.activation(out=dv[:, DV_CH:DV_CH + 8], in_=dv[:, DV_CH:DV_CH + 8], func=AF.Ln, bias=1.0),
          reads=["dv_c"], writes=["dv_c"])
    P.add("dve", lambda e: e.tensor_scalar(out=dv[:, DV_CF:DV_CF + 8], in0=dv[:, DV_CH:DV_CH + 8], scalar1=-8.0, scalar2=None,
                                           op0=ALU.mult), reads=["dv_c"], writes=["dv_cf"])
    P.add("dve", lambda e: e.tensor_scalar(out=dv[:, DV_CH:DV_CH + 8], in0=dv[:, DV_CH:DV_CH + 8], scalar1=-16.0, scalar2=None,
                                           op0=ALU.mult), reads=["dv_c", "dv_cf"], writes=["dv_c"])
    P.add("dve", lambda e: e.tensor_scalar(out=dv[:, DV_NBA:DV_NBA + 16], in0=prmc(P_BA, 16), scalar1=-1.0, scalar2=None,
                                           op0=ALU.mult), reads=["prm"], writes=["dv_hb"])
    P.add("dve", lambda e: e.tensor_scalar(out=dv[:, DV_NBG:DV_NBG + 16], in0=prmc(P_BG, 16), scalar1=-1.0, scalar2=None,
                                           op0=ALU.mult), reads=["prm"], writes=["dv_hbg"])
    P.add("dve", lambda e: e.tensor_scalar(out=dv[:, DV_GQS:DV_GQS + 1], in0=prmc(P_GQ, 1), scalar1=ATTN_SCALE, scalar2=None,
                                           op0=ALU.mult), reads=["prm"], writes=["dv_gqs"])
    P.add("act", lambda e: e.activation(out=dv[:, DV_ESK:DV_ESK + 8], in_=prmc(P_SINK, 8), func=AF.Exp),
          reads=["prm"], writes=["dv_esk"])
    P.add("dve", lambda e: e.memset(dv[:, DV_MH:DV_MH + 4], -0.5), writes=["dv_mh"])
    CONSTS = ["prm", "dv_c", "dv_cf", "dv_hb", "dv_hbg", "dv_gqs", "dv_esk", "dv_mh"]
    for src, dst, nm in ((ga_d, gwa, "gwa"), (gx_d, gwx, "gwx")):
        P.dma(lambda e, src=src: e.dma_start(out=gstg.rearrange("p (n d) -> p n d", n=8),
                                             in_=src.rearrange("n c d -> c n d")), d_const, writes=[("tl", 2), ("tl", 3)])
        P.add("dve", lambda e, dst=dst: e.tensor_copy(out=dst[:, :, :], in_=gstg.rearrange("p (n d) -> p n d", n=8)),
              reads=[("tl", 2), ("tl", 3)], writes=[nm])
    for s in range(2):
        kc_f = tl[:, 0, 0:256]
        vc_f = tl[:, 1, 0:256]
        P.dma(lambda e, s=s: e.dma_start(out=kc_f, in_=ck_d[s]), d_const, writes=[("tl", 0)])
        P.dma(lambda e, s=s: e.dma_start(out=vc_f, in_=cv_d[s]), d_const, writes=[("tl", 1)])
        P.add("dve", lambda e, s=s: e.tensor_copy(out=vc[:, s, :, 0:128], in_=vc_f.rearrange("p (g d) -> p g d", g=2)),
              reads=[("tl", 1)], writes=[("vc", s)])
        for g in range(2):
            bk, bkk = bankB()
            P.add("pe", lambda e, g=g, bk=bk: e.transpose(bk[:, 0:128], kc_f[:, g * 128:(g + 1) * 128], identf[:, :]),
                  reads=[("tl", 0), "identf"], writes=[bkk])
            P.add("act", lambda e, s=s, g=g, bk=bk: e.activation(out=kTc[:, s, g, :], in_=bk[:, 0:128], func=AF.Copy),
                  reads=[bkk], writes=[("kTc", s)])
        P.dma(lambda e, s=s: e.dma_start(out=ks_o[s, 0:112, :], in_=ck_d[s, 16:128, :]), d_cp)
        P.dma(lambda e, s=s: e.dma_start(out=vs_o[s, 0:112, :], in_=cv_d[s, 16:128, :]), d_cp)

    P.fence()
    pp = {"i": 0}
    _pre_ok = not (_DBG_STOP is not None and _DBG_STOP <= 1)

    def prepass_chunk(src, kc, c0, ncol, dest_fn, scale, late=False):
        i = pp["i"]
        pp["i"] += 1
        if late:
            b = NSTG + i % 3
            sf, sb = stg_f2[i % 3], stg_b2[i % 3]
            depth = 2
        else:
            b = i % NSTG
            sf, sb = stg_f[b], stg_b[b]
            depth = NSTG - 1
        P.dma(lambda e: e.dma_start(out=sf[:, 0:ncol], in_=src[kc * 128:(kc + 1) * 128, c0:c0 + ncol]), d_stg[b],
              writes=[("stgf", b)])
        eng = ("dve", "act", "pool")[i % 3]
        rd = [("stgf", b)] + CONSTS
        if eng == "dve":
            P.add("dve", lambda e: e.tensor_scalar(out=sb[:, 0:ncol], in0=sf[:, 0:ncol], scalar1=scale, scalar2=None, op0=ALU.mult),
                  reads=rd, writes=[("stgb", b)])
        elif eng == "act":
            P.add("act", lambda e: e.activation(out=sb[:, 0:ncol], in_=sf[:, 0:ncol], func=AF.Copy, scale=scale),
                  reads=rd, writes=[("stgb", b)])
        else:
            P.add("pool", lambda e: e.tensor_scalar(out=sb[:, 0:ncol], in0=sf[:, 0:ncol], scalar1=scale, scalar2=1.0,
                                                    op0=ALU.mult, op1=ALU.mult), reads=rd, writes=[("stgb", b)])
        ng = ncol // 256
        slots = [dest_fn(g) for g in range(ng)]
        kcl = slots[0][1]
        s0 = slots[0][0]
        step = (slots[1][0] - s0) if ng > 1 else 1
        dst = wsc[s0:s0 + step * (ng - 1) + 1:step].rearrange("s p c -> p s c")[:, :, kcl * 256:(kcl + 1) * 256]
        pp.setdefault("pend", []).append(
            lambda: P.dma(lambda e: e.dma_start(out=dst, in_=sb[:, 0:ncol].rearrange("p (s c) -> p s c", s=ng)), d_wst[b],
                          reads=[("stgb", b)], writes=[("wsc", sl_[0]) for sl_ in slots]))
        while len(pp["pend"]) > depth:
            pp["pend"].pop(0)()

    for kc in range(8):
        g = prmc(P_GM + kc, 1)
        prepass_chunk(w_in_d, kc, 0, 1024, lambda i, kc=kc: (SL_U + i, kc), g)
        prepass_chunk(w_in_d, kc, 1024, 1024, lambda i, kc=kc: (SL_Q + i, kc), g)
        prepass_chunk(w_in_d, kc, 2048, 512, lambda i, kc=kc: (SL_K + i, kc), g)
        prepass_chunk(w_in_d, kc, 2560, 1024, lambda i, kc=kc: (SL_M + 4 * i + 2, kc), g)
        prepass_chunk(w_in_d, kc, 3584, 1024, lambda i, kc=kc: (SL_M + 4 * i + 3, kc), g)
        prepass_chunk(wrp_d, kc, 0, 1024, lambda i, kc=kc: (SL_M + 4 * i + 0, kc), 1.0)
        prepass_chunk(wap_d, kc, 0, 1024, lambda i, kc=kc: (SL_M + 4 * i + 1, kc), 1.0)
        prepass_chunk(wout_d, kc, 0, 1024, lambda i, kc=kc: (SL_O + i, kc), 1.0)

    def prepass_ffn_gen():
        for kc in range(8):
            gf = prmc(P_GF + kc, 1)
            for half in range(2):
                for (c0, ncol, p0) in ((0, 1024, 0), (1024, 1024, 4), (2048, 768, 8)):
                    prepass_chunk(wup_d, kc, half * DFF + c0, ncol,
                                  lambda i, kc=kc, half=half, p0=p0: (SL_UP + 2 * (p0 + i) + half, kc), gf, late=True)
                    yield
        for kc in range(NFB):
            prepass_chunk(wdn_d, kc, 0, 1024, lambda i, kc=kc: (SL_DN + 3 * i + kc // 8, kc % 8), 1.0, late=True)
            yield
        while pp.get("pend"):
            pp["pend"].pop(0)()
        yield

    while pp.get("pend"):
        pp["pend"].pop(0)()
    P.fence()

    ring_held = [False] * NRING

    def load_slot(slot):
        for k in range(NRING):
            r = (cnt["ring"] + k) % NRING
            if not ring_held[r]:
                break
        else:
            raise RuntimeError("weight ring exhausted")
        cnt["ring"] = r + 1
        ring_held[r] = True
        nv = 6 * 256 if (slot >= SL_DN and (slot - SL_DN) % 3 == 2) else 2048
        P.dma(lambda e: e.dma_start(out=ring[:, r, 0:nv], in_=wsc[slot, :, 0:nv]), d_ring[r], reads=[("wsc", slot)],
              writes=[("ring", r)])
        return ring[:, r, :].rearrange("p (k c) -> p k c", k=8), ("ring", r), r

    def release(sl):
        ring_held[sl[2]] = False

    def mm_fm(out_ap, okey, wv, wkey, col0, rhs_t, rkeys, W, nk=8, extra_reads=()):
        for kc in range(nk):
            P.add("pe", lambda e, kc=kc: e.matmul(out_ap, lhsT=wv[:, kc, col0:col0 + 128], rhs=rhs_t[:, kc, 0:W],
                                                  start=(kc == 0), stop=(kc == nk - 1)),
                  reads=[wkey] + list(rkeys) + list(extra_reads), writes=[okey])

    def norm_transpose(xs_i, rows, nsub, sscol, dst, dkey):
        for j in range(nsub):
            P.add("act", lambda e, j=j: e.activation(out=xn[0:rows, j % 2, :], in_=xt[0:rows, xs_i, j, :], func=AF.Square,
                                                      accum_out=ss[0:rows, sscol + j:sscol + j + 1]),
                  reads=[("xt", xs_i, j)], writes=[("xn", j % 2), ("ss", sscol + j)])
            yield
        P.add("pool", lambda e: e.tensor_scalar(out=rs[0:rows, sscol:sscol + nsub], in0=ss[0:rows, sscol:sscol + nsub],
                                                scalar1=1.0 / D, scalar2=EPS, op0=ALU.mult, op1=ALU.add),
              reads=[("ss", sscol + j) for j in range(nsub)], writes=[("rs", sscol)])
        P.add("pool", lambda e: e.tensor_tensor(out=rs[0:rows, sscol:sscol + nsub], in0=rs[0:rows, sscol:sscol + nsub],
                                                in1=dv[0:rows, DV_MH:DV_MH + nsub], op=ALU.pow),
              reads=[("rs", sscol), "dv_mh"], writes=[("rs", sscol)])
        yield
        for j in range(nsub):
            r = cnt["xn"] % 2
            cnt["xn"] += 1
            P.add("dve", lambda e, j=j, r=r: e.tensor_scalar(out=xn[0:rows, r, :], in0=xt[0:rows, xs_i, j, :],
                                                             scalar1=rs[0:rows, sscol + j:sscol + j + 1], scalar2=None,
                                                             op0=ALU.mult),
                  reads=[("xt", xs_i, j), ("rs", sscol)], writes=[("xn", r)])
            yield
            bk, bkk = bankB()
            bkb = bk[:, :].bitcast(BF16)
            for kc in range(8):
                P.add("pe", lambda e, kc=kc, r=r, bkb=bkb: e.transpose(bkb[:, kc * 128:kc * 128 + rows],
                                                                       xn[0:rows, r, kc * 128:(kc + 1) * 128],
                                                                       identb[0:rows, 0:rows]),
                      reads=[("xn", r), "identb"], writes=[bkk])
            yield
            P.add("act", lambda e, j=j, bkb=bkb: e.activation(
                out=dst[:, :, j * 128:j * 128 + rows],
                in_=bkb.rearrange("p (k t) -> p k t", k=8)[:, :, 0:rows], func=AF.Copy),
                reads=[bkk], writes=[dkey])
            yield

    def conv_taps(ps, pskey, acc, acckey, W, segs, ntap, wcol_fn, bcol, halo_fn, halokey_fn):
        P.add("act", lambda e: e.activation(out=acc[:, 0:W], in_=ps[:, 0:W], func=AF.Identity, bias=bcol,
                                            scale=wcol_fn(ntap - 1)),
              reads=[pskey] + CONSTS, writes=[acckey])
        nh = ntap - 1
        for (c0, L, sidx) in segs:
            hal = halo_fn(sidx)
            for s in range(1, ntap):
                wj = wcol_fn(ntap - 1 - s)
                P.add("dve", lambda e, c0=c0, L=L, s=s, wj=wj: e.scalar_tensor_tensor(
                    out=acc[:, c0 + s:c0 + L], in0=ps[:, c0:c0 + L - s], scalar=wj, in1=acc[:, c0 + s:c0 + L],
                    op0=ALU.mult, op1=ALU.add), reads=[pskey, acckey] + CONSTS, writes=[acckey])
                P.add("dve", lambda e, c0=c0, s=s, wj=wj, hal=hal: e.scalar_tensor_tensor(
                    out=acc[:, c0:c0 + s], in0=hal[:, nh - s:nh], scalar=wj, in1=acc[:, c0:c0 + s],
                    op0=ALU.mult, op1=ALU.add), reads=[halokey_fn(sidx), acckey] + CONSTS, writes=[acckey])
            P.add("dve", lambda e, c0=c0, L=L, hal=hal: e.tensor_scalar(out=hal[:, 0:nh], in0=ps[:, c0 + L - nh:c0 + L],
                                                                       scalar1=1.0, scalar2=None, op0=ALU.mult),
                  reads=[pskey], writes=[halokey_fn(sidx)])

    def chain(gens):
        for g in gens:
            yield from g

    def par(gens):
        gens = list(gens)
        while gens:
            for g in list(gens):
                try:
                    next(g)
                except StopIteration:
                    gens.remove(g)
            yield

    def run(gen):
        for _ in gen:
            pass

    def tm_proj(xs_i, rows, nsub, slot_ids, nkc, lhs_t, lkeys):
        for cg in range(4):
            sl = [load_slot(s) for s in slot_ids(cg)]
            for j in range(nsub):
                for kc in range(nkc):
                    wv, wkey, _r = sl[kc // 8]
                    P.add("pe", lambda e, j=j, kc=kc, wv=wv: e.matmul(
                        psC[0:rows, j * 256:(j + 1) * 256], lhsT=lhs_t[:, kc, j * 128:j * 128 + rows], rhs=wv[:, kc % 8, :],
                        start=(kc == 0), stop=(kc == nkc - 1)), reads=[wkey] + list(lkeys), writes=[("psC", j // 2)])
                yield
            for s_ in sl:
                release(s_)
            for j in range(nsub):
                P.add("dve", lambda e, j=j, cg=cg: e.tensor_tensor(
                    out=xt[0:rows, xs_i, j, cg * 256:(cg + 1) * 256], in0=psC[0:rows, j * 256:(j + 1) * 256],
                    in1=xt[0:rows, xs_i, j, cg * 256:(cg + 1) * 256], op=ALU.add),
                    reads=[("psC", j // 2), ("xt", xs_i, j)], writes=[("xt", xs_i, j)])
            yield

    HT = ["hTa"]

    def stageA_gen(ti, xs_i, W, rows, nsub, segs, first_tile, is_sample):
        yield from norm_transpose(xs_i, rows, nsub, 0, hTa, "hTa")

    def front_gen(ti, xs_i, W, rows, nsub, segs, first_tile, is_sample, gate=None, wd_gate=None):
        slot_cache = {}
        slot_uses = {}
        nvj = nsub if not is_sample else len(segs)
        for k in range(4):
            slot_uses[SL_U + k] = 2
            slot_uses[SL_Q + k] = 2
        slot_uses[SL_K] = 2
        slot_uses[SL_V] = nvj

        def use_slot(sid):
            if sid not in slot_cache:
                slot_cache[sid] = load_slot(sid)
            return slot_cache[sid][0], slot_cache[sid][1]

        def done_slot(sid):
            slot_uses[sid] -= 1
            if slot_uses[sid] == 0:
                release(slot_cache[sid])

        done_head = {}
        done_tail = {}

        def rnn_head(n):
            lane = n % 2
            sset = n % 4
            bU, bUk = banks8[lane]
            bG, bGk = banks8[lane]
            acc, acck = tl[:, 4 * sset + 0, :], ("tl", 4 * sset + 0)
            t1, t1k = tl[:, 4 * sset + 1, :], ("tl", 4 * sset + 1)
            t2, t2k = tl[:, 4 * sset + 2, :], ("tl", 4 * sset + 2)
            ta, tak = tl[:, 4 * sset + 3, :], ("tl", 4 * sset + 3)
            ucb, ucbk = tlb[:, sset, :], ("tlb", sset)
            while n >= 4 and not done_tail.get(n - 4):
                yield
            wv, wkey = use_slot(SL_U + n // 2)
            mm_fm(bU[:, 0:W], bUk, wv, wkey, (n % 2) * 128, hTa, HT, W)
            done_slot(SL_U + n // 2)
            yield
            conv_taps(bU, bUk, acc, acck, W, segs, 4, lambda j, n=n: prmc(P_CW + j * 8 + n, 1), prmc(P_CB + n, 1),
                      lambda sidx, n=n: st[:, ST_HU + sidx * 24 + n * 3:ST_HU + sidx * 24 + n * 3 + 3],
                      lambda sidx, n=n: ("st_hu", sidx, n))
            yield
            P.add("dve", lambda e: e.tensor_copy(out=ucb[:, 0:W], in_=acc[:, 0:W]), reads=[acck], writes=[ucbk])
            yield
            P.add("pe", lambda e: e.matmul(bG[:, 0:W], lhsT=gwa[:, n, :], rhs=ucb[:, 0:W], start=True, stop=True),
                  reads=["gwa", ucbk], writes=[bGk])
            yield
            P.add("act", lambda e: e.activation(out=t1[:, 0:W], in_=bG[:, 0:W], func=AF.Exp, scale=-1.0,
                                                bias=dv[:, DV_NBA + n:DV_NBA + n + 1]), reads=[bGk] + CONSTS, writes=[t1k])
            yield
            P.add("pe", lambda e: e.matmul(bG[:, 0:W], lhsT=gwx[:, n, :], rhs=ucb[:, 0:W], start=True, stop=True),
                  reads=["gwx", ucbk], writes=[bGk])
            yield
            P.add("act", lambda e: e.activation(out=t2[:, 0:W], in_=bG[:, 0:W], func=AF.Exp, scale=-1.0,
                                                bias=dv[:, DV_NBX + n:DV_NBX + n + 1]), reads=[bGk] + CONSTS, writes=[t2k])
            yield
            done_head[n] = True

        def rnn_tail(n):
            lane = n % 2
            sset = n % 4
            bU, bUk = banks8[lane]
            bG, bGk = banks8[lane]
            acc, acck = tl[:, 4 * sset + 0, :], ("tl", 4 * sset + 0)
            t1, t1k = tl[:, 4 * sset + 1, :], ("tl", 4 * sset + 1)
            t2, t2k = tl[:, 4 * sset + 2, :], ("tl", 4 * sset + 2)
            ta, tak = tl[:, 4 * sset + 3, :], ("tl", 4 * sset + 3)
            ucb, ucbk = tlb[:, sset, :], ("tlb", sset)
            while not done_head.get(n):
                yield
            for tt, ttk in ((t1, t1k), (t2, t2k)):
                P.add("act", lambda e, tt=tt: e.activation(out=tt[:, 0:W], in_=tt[:, 0:W], func=AF.Ln, bias=1.0),
                      reads=[ttk], writes=[ttk])
            yield
            for tt, ttk in ((t1, t1k), (t2, t2k)):
                P.add("act", lambda e, tt=tt: e.activation(out=tt[:, 0:W], in_=tt[:, 0:W], func=AF.Exp, scale=-1.0),
                      reads=[ttk], writes=[ttk])
            yield
            P.add("act", lambda e: e.activation(out=ta[:, 0:W], in_=t1[:, 0:W], func=AF.Exp, scale=dv[:, DV_C + n:DV_C + n + 1]),
                  reads=[t1k] + CONSTS, writes=[tak])
            P.add("dve", lambda e: e.tensor_tensor(out=t2[:, 0:W], in0=t2[:, 0:W], in1=acc[:, 0:W], op=ALU.mult),
                  reads=[t2k, acck], writes=[t2k])
            yield
            P.add("act", lambda e: e.activation(out=t1[:, 0:W], in_=t1[:, 0:W], func=AF.Exp, scale=dv[:, DV_CH + n:DV_CH + n + 1]),
                  reads=[t1k] + CONSTS, writes=[t1k])
            yield
            P.add("act", lambda e: e.activation(out=t1[:, 0:W], in_=t1[:, 0:W], func=AF.Ln, bias=1.0, scale=-1.0),
                  reads=[t1k], writes=[t1k])
            yield
            P.add("act", lambda e: e.activation(out=t1[:, 0:W], in_=t1[:, 0:W], func=AF.Exp, scale=0.5), reads=[t1k], writes=[t1k])
            yield
            P.add("dve", lambda e: e.tensor_tensor(out=t2[:, 0:W], in0=t2[:, 0:W], in1=t1[:, 0:W], op=ALU.mult),
                  reads=[t1k, t2k], writes=[t2k])
            yield
            for (c0, L, sidx) in segs:
                hcol = st[:, ST_H + sidx * 8 + n:ST_H + sidx * 8 + n + 1]
                P.add("dve", lambda e, c0=c0, L=L, hcol=hcol: e.tensor_tensor_scan(
                    out=acc[:, c0:c0 + L], data0=ta[:, c0:c0 + L], data1=t2[:, c0:c0 + L], initial=hcol,
                    op0=ALU.mult, op1=ALU.add), reads=[tak, t2k, ("st_h", sidx, n)], writes=[acck])
                P.add("dve", lambda e, c0=c0, L=L, hcol=hcol: e.tensor_copy(out=hcol, in_=acc[:, c0 + L - 1:c0 + L]),
                      reads=[acck], writes=[("st_h", sidx, n)])
            yield
            P.add("dve", lambda e: e.tensor_copy(out=yr[:, n, 0:W], in_=acc[:, 0:W]), reads=[acck], writes=[("yr", n)])
            yield
            done_tail[n] = True

        qk_cnt = {"i": 0}

        def qk_gate():
            while gate is not None and not gate["go"]:
                yield

        def qk_unit(hb, ql):
            qps, qpk = banks8[2 + ql]
            sps, spk = banks8[6 + ql]
            sq, sqk = tlb[:, 4 + ql, :], ("tlb", 4 + ql)
            lt, ltk = tl[:, 16 + ql, :], ("tl", 16 + ql)
            if hb < 8:
                sid = SL_Q + hb // 2
                wv, wkey = use_slot(sid)
                col0 = (hb % 2) * 128
            else:
                sid = SL_K
                wv, wkey = use_slot(sid)
                col0 = (hb - 8) * 128
            mm_fm(qps[:, 0:W], qpk, wv, wkey, col0, hTa, HT, W)
            done_slot(sid)
            yield
            P.add("act", lambda e: e.activation(out=sq[:, 0:W], in_=qps[:, 0:W], func=AF.Square), reads=[qpk], writes=[sqk])
            yield
            P.add("pe", lambda e: e.matmul(sps[:, 0:W], lhsT=onesb[:, :], rhs=sq[:, 0:W], start=True, stop=True),
                  reads=[sqk, "onesb"], writes=[spk])
            yield
            P.add("act", lambda e: e.activation(out=lt[:, 0:W], in_=sps[:, 0:W], func=AF.Ln, bias=EPS, scale=1.0 / HD),
                  reads=[spk], writes=[ltk])
            yield
            P.add("act", lambda e: e.activation(out=lt[:, 0:W], in_=lt[:, 0:W], func=AF.Exp, scale=-0.5), reads=[ltk], writes=[ltk])
            yield
            if hb < 8:
                P.add("dve", lambda e: e.scalar_tensor_tensor(
                    out=qT[:, hb, 0:W], in0=qps[:, 0:W], scalar=dv[:, DV_GQS:DV_GQS + 1], in1=lt[:, 0:W],
                    op0=ALU.mult, op1=ALU.mult), reads=[qpk, ltk] + CONSTS, writes=[("qT", hb)])
            else:
                g = hb - 8
                P.add("dve", lambda e: e.scalar_tensor_tensor(
                    out=kT[:, g, 128:128 + W], in0=qps[:, 0:W], scalar=prmc(P_GK, 1), in1=lt[:, 0:W],
                    op0=ALU.mult, op1=ALU.mult), reads=[qpk, ltk] + CONSTS, writes=[("kT", 1)])
                outs = []
                if is_sample:
                    for (c0, L, sidx) in segs:
                        outs.append((c0, L, kvs[0:L, sidx - 1, 0, g * 128:(g + 1) * 128], ("kvs", sidx - 1, 0)))
                elif ti == NT - 1:
                    outs.append((W - 128, 128, kvo[:, 0, g * 128:(g + 1) * 128], ("kvo", 0)))
                for (c0, L, dst, dkey) in outs:
                    kf, kfk = tl[:, 18, ql * 256:(ql + 1) * 256], ("tl18", ql)
                    P.add("dve", lambda e, c0=c0, L=L: e.scalar_tensor_tensor(
                        out=kf[:, 0:L], in0=qps[:, c0:c0 + L], scalar=prmc(P_GK, 1), in1=lt[:, c0:c0 + L],
                        op0=ALU.mult, op1=ALU.mult), reads=[qpk, ltk] + CONSTS, writes=[kfk])
                    P.add("pe", lambda e, L=L: e.transpose(sps[0:L, 0:128], kf[:, 0:L], identf[:, :]),
                          reads=[kfk, "identf"], writes=[spk])
                    P.add("act", lambda e, dst=dst, L=L: e.activation(out=dst, in_=sps[0:L, 0:128], func=AF.Copy),
                          reads=[spk], writes=[dkey])
            yield

        def v_unit(job, ql):
            (j, c0, L, dst, dkey, fout) = job
            vps, vpk = banks8[2 + ql]
            wv, wkey = use_slot(SL_V)
            for kc in range(8):
                P.add("pe", lambda e, kc=kc: e.matmul(vps[0:L, 0:256], lhsT=hTa[:, kc, c0:c0 + L], rhs=wv[:, kc, 0:256],
                                                      start=(kc == 0), stop=(kc == 7)), reads=[wkey] + HT, writes=[vpk])
            done_slot(SL_V)
            yield
            P.add("act", lambda e: e.activation(out=dst, in_=vps[0:L, 0:256].rearrange("p (g d) -> p g d", g=2), func=AF.Copy),
                  reads=[vpk], writes=[dkey])
            if fout is not None:
                P.add("act", lambda e: e.activation(out=fout[0], in_=vps[0:L, 0:256], func=AF.Copy), reads=[vpk], writes=[fout[1]])
            yield

        if not is_sample:
            vjobs = [(j, j * 128, 128, vb[:, 1 + j, :, 0:128], ("vb", 1 + j),
                      (kvo[:, 1, :], ("kvo", 1)) if (ti == NT - 1 and j == nsub - 1) else None) for j in range(nsub)]
        else:
            vjobs = [(sidx - 1, c0, L, vb[0:L, sidx, :, 0:128], ("vb", sidx), (kvs[0:L, sidx - 1, 1, :], ("kvs", sidx - 1, 1)))
                     for (c0, L, sidx) in segs]


        if not is_sample:
            vjobs = [(j, j * 128, 128, vb[:, 1 + j, :, 0:128], ("vb", 1 + j),
                      (kvo[:, 1, :], ("kvo", 1)) if (ti == NT - 1 and j == nsub - 1) else None) for j in range(nsub)]
        else:
            vjobs = [(sidx - 1, c0, L, vb[0:L, sidx, :, 0:128], ("vb", sidx), (kvs[0:L, sidx - 1, 1, :], ("kvs", sidx - 1, 1)))
                     for (c0, L, sidx) in segs]
        qk_flags = {}

        def qk_done(ql):
            qk_flags[ql] = True
            yield

        yield from par([chain([rnn_head(n) for n in (0, 2, 4, 6)]),
                        chain([rnn_head(n) for n in (1, 3, 5, 7)]),
                        chain([rnn_tail(n) for n in (0, 4)]),
                        chain([rnn_tail(n) for n in (1, 5)]),
                        chain([rnn_tail(n) for n in (2, 6)]),
                        chain([rnn_tail(n) for n in (3, 7)]),
                        chain([qk_gate()] + [qk_unit(hb, 0) for hb in range(0, 10, 2)] + [v_unit(jb, 0) for jb in vjobs[0::2]]
                              + [qk_done(0)]),
                        chain([qk_gate()] + [qk_unit(hb, 1) for hb in range(1, 10, 2)] + [v_unit(jb, 1) for jb in vjobs[1::2]]
                              + [qk_done(1)]),
                        attn_gen(ti, xs_i, W, rows, nsub, segs, first_tile, is_sample, flags=qk_flags, gate=wd_gate)])

    def attn_gen(ti, xs_i, W, rows, nsub, segs, first_tile, is_sample, flags=None, gate=None):
        while (flags is not None and not (flags.get(0) and flags.get(1))) or (gate is not None and not gate["go"]):
            yield
        acnt = [0]
        if not is_sample:
            ajobs = []
            for j in range(nsub):
                kbs = []
                if not (first_tile and j == 0):
                    kbs.append((0, kT[:, :, j * 128:(j + 1) * 128], ("kT", 0 if j == 0 else 1), vb[:, j, :, :], ("vb", j), 128))
                kbs.append((1, kT[:, :, (j + 1) * 128:(j + 2) * 128], ("kT", 1), vb[:, j + 1, :, :], ("vb", j + 1), 128))
                ajobs.append((j * 128, 128, kbs))
        else:
            ajobs = []
            for (c0, L, sidx) in segs:
                kbs = [(0, kTc[:, sidx - 1, :, :], ("kTc", sidx - 1), vc[:, sidx - 1, :, :], ("vc", sidx - 1), 128),
                       (1, kT[:, :, 128 + c0:128 + c0 + L], ("kT", 1), vb[0:L, sidx, :, :], ("vb", sidx), L)]
                ajobs.append((c0, L, kbs))
        for (c0, nq, kbs) in ajobs:
            pts = {}
            for g in range(2):
                for (kb, kTv, kTk, vv, vk, nk) in kbs:
                    sps, spk = banks8[2 + acnt[0] % 2]
                    acnt[0] += 1
                    so2 = sps[0:nk, 0:4 * nq]
                    so = so2.rearrange("p (h q) -> p h q", h=4)
                    P.add("pe", lambda e, so2=so2, kTv=kTv, g=g, nk=nk, c0=c0, nq=nq: e.matmul(
                        so2, lhsT=kTv[:, g, 0:nk], rhs=qT[:, 4 * g:4 * g + 4, c0:c0 + nq], start=True, stop=False),
                        reads=[kTk] + [("qT", 4 * g + i) for i in range(4)], writes=[spk])
                    P.add("pe", lambda e, so2=so2, kb=kb, g=g, nk=nk, nq=nq: e.matmul(
                        so2, lhsT=identb[0:nk, 0:nk], rhs=btab[0:nk, kb, 4 * g:4 * g + 4, 0:nq], start=False, stop=True),
                        reads=["identb", "btab"], writes=[spk])
                    pi = cnt["pT"] % 4
                    cnt["pT"] += 1
                    po = pT[0:nk, pi, 0:4 * nq].rearrange("p (h q) -> p h q", h=4)
                    P.add("act", lambda e, po=po, so=so: e.activation(out=po, in_=so, func=AF.Exp), reads=[spk],
                          writes=[("pT", pi)])
                    pts[(g, kb)] = (po, ("pT", pi), vv, vk, nk)
                    yield
            dps, dpk = bankB()
            for h in range(8):
                g = h // 4
                lst = [pts[(g, kb)] for (kb, *_r) in kbs]
                for idx, (po, pk, vv, vk, nk) in enumerate(lst):
                    P.add("pe", lambda e, po=po, vv=vv, nk=nk, h=h, g=g, idx=idx, n=len(lst), nq=nq: e.matmul(
                        psC[0:nq, h * 128:(h + 1) * 128], lhsT=po[:, h % 4, :], rhs=vv[0:nk, g, 0:128],
                        start=(idx == 0), stop=(idx == n - 1)), reads=[pk, vk], writes=[("psC", h // 4)])
                    P.add("pe", lambda e, po=po, vv=vv, nk=nk, h=h, g=g, idx=idx, n=len(lst), nq=nq, dps=dps: e.matmul(
                        dps[0:nq, h:h + 1], lhsT=po[:, h % 4, :], rhs=onesb[0:nk, 0:1],
                        start=(idx == 0), stop=(idx == n - 1)), reads=[pk, "onesb"], writes=[dpk])
            yield
            P.add("dve", lambda e, dps=dps, nq=nq: e.tensor_tensor(out=smal[0:nq, 0:8], in0=dps[0:nq, 0:8],
                                                                   in1=dv[0:nq, DV_ESK:DV_ESK + 8], op=ALU.add),
                  reads=[dpk] + CONSTS, writes=["smal"])
            P.add("dve", lambda e, nq=nq: e.reciprocal(out=smal[0:nq, 8:16], in_=smal[0:nq, 0:8]), reads=["smal"],
                  writes=["smal2"])
            yi = cnt["yat"] % 2
            cnt["yat"] += 1
            for half in range(2):
                P.add("dve", lambda e, half=half, yi=yi, nq=nq: e.tensor_tensor(
                    out=yat[0:nq, yi, half * 512:(half + 1) * 512].rearrange("p (h d) -> p h d", h=4),
                    in0=psC[0:nq, half * 512:(half + 1) * 512].rearrange("p (h d) -> p h d", h=4),
                    in1=smal[0:nq, 8 + 4 * half:12 + 4 * half].unsqueeze(2).to_broadcast([nq, 4, 128]), op=ALU.mult),
                    reads=[("psC", half), "smal2"], writes=[("yat", yi)])
            yield
            bk, bkk = bankB()
            bkb = bk[:, :].bitcast(BF16)
            for h in range(8):
                P.add("pe", lambda e, h=h, yi=yi, bkb=bkb, nq=nq: e.transpose(bkb[:, h * 128:h * 128 + nq],
                                                                               yat[0:nq, yi, h * 128:(h + 1) * 128],
                                                                               identb[0:nq, 0:nq]),
                      reads=[("yat", yi), "identb"], writes=[bkk])
            P.add("act", lambda e, bkb=bkb, c0=c0, nq=nq: e.activation(
                out=ya[:, :, c0:c0 + nq], in_=bkb.rearrange("p (h t) -> p h t", h=8)[:, :, 0:nq], func=AF.Copy),
                reads=[bkk], writes=["ya"])
            yield
        if not is_sample:
            P.add("pool", lambda e: e.tensor_copy(out=kT[:, :, 0:128], in_=kT[:, :, 512:640]), reads=[("kT", 1)], writes=[("kT", 0)])
            P.add("pool", lambda e: e.tensor_copy(out=vb[:, 0, :, 0:128], in_=vb[:, 4, :, 0:128]), reads=[("vb", 4)],
                  writes=[("vb", 0)])
        yield

    def mid(ti, xs_i, W, rows, nsub, segs, first_tile, is_sample):
        YR = [("yr", n) for n in range(8)]
        for pr in range(4):
            s_rp = load_slot(SL_M + 4 * pr + 0)
            s_ap = load_slot(SL_M + 4 * pr + 1)
            s_gr = load_slot(SL_M + 4 * pr + 2)
            s_ga = load_slot(SL_M + 4 * pr + 3)
            for sub in range(2):
                m = 2 * pr + sub
                col0 = sub * 128
                p3, p3k = bankA()
                mm_fm(p3[:, 0:W], p3k, s_gr[0], s_gr[1], col0, hTa, HT, W)
                p4, p4k = bankA()
                mm_fm(p4[:, 0:W], p4k, s_ga[0], s_ga[1], col0, hTa, HT, W)
                p1, p1k = bankA()
                mm_fm(p1[:, 0:W], p1k, s_rp[0], s_rp[1], col0, yr, YR, W)
                p2, p2k = bankA()
                mm_fm(p2[:, 0:W], p2k, s_ap[0], s_ap[1], col0, ya, ["ya"], W)
                t1, t1k = tmpf()
                t2, t2k = tmpf()
                P.add("act", lambda e, m=m, t1=t1, p3=p3: e.activation(out=t1[:, 0:W], in_=p3[:, 0:W], func=AF.Exp,
                                                                        bias=dv[:, DV_NBG + m:DV_NBG + m + 1], scale=-1.0),
                      reads=[p3k] + CONSTS, writes=[t1k])
                P.add("act", lambda e, m=m, t2=t2, p4=p4: e.activation(out=t2[:, 0:W], in_=p4[:, 0:W], func=AF.Exp,
                                                                        bias=dv[:, DV_NBG + 8 + m:DV_NBG + 9 + m], scale=-1.0),
                      reads=[p4k] + CONSTS, writes=[t2k])
                for tt, ttk in ((t1, t1k), (t2, t2k)):
                    P.add("act", lambda e, tt=tt: e.activation(out=tt[:, 0:W], in_=tt[:, 0:W], func=AF.Ln, bias=1.0),
                          reads=[ttk], writes=[ttk])
                for tt, ttk in ((t1, t1k), (t2, t2k)):
                    P.add("act", lambda e, tt=tt: e.activation(out=tt[:, 0:W], in_=tt[:, 0:W], func=AF.Exp, scale=-1.0),
                          reads=[ttk], writes=[ttk])
                m1, m1k = tmpb()
                m2, m2k = tmpb()
                P.add("dve", lambda e, m1=m1, t1=t1, p1=p1: e.tensor_tensor(
                    out=m1[:, 0:W], in0=p1[:, 0:W], in1=t1[:, 0:W], op=ALU.mult), reads=[t1k, p1k], writes=[m1k])
                P.add("dve", lambda e, m2=m2, t2=t2, p2=p2: e.tensor_tensor(
                    out=m2[:, 0:W], in0=p2[:, 0:W], in1=t2[:, 0:W], op=ALU.mult), reads=[t2k, p2k], writes=[m2k])
                P.add("dve", lambda e, m=m, m1=m1, m2=m2: e.tensor_tensor(out=mixed[:, m, 0:W], in0=m1[:, 0:W], in1=m2[:, 0:W],
                                                                          op=ALU.add), reads=[m1k, m2k], writes=[("mixed", m)])
            for sl_ in (s_rp, s_ap, s_gr, s_ga):
                release(sl_)
        MX = [("mixed", m) for m in range(8)]
        run(tm_proj(xs_i, rows, nsub, lambda cg: [SL_O + cg], 8, mixed, MX))
        run(norm_transpose(xs_i, rows, nsub, 4, hTb, "hTb"))

    def ffn_up(ti, xs_i, W, rows, nsub, segs):
        for pr in range(11):
            s_a = load_slot(SL_UP + 2 * pr)
            s_b = load_slot(SL_UP + 2 * pr + 1)
            for sub in range(2):
                kc2 = 2 * pr + sub
                col0 = sub * 128
                aps, apk = bankA()
                mm_fm(aps[:, 0:W], apk, s_a[0], s_a[1], col0, hTb, ["hTb"], W)
                bps, bpk = bankA()
                mm_fm(bps[:, 0:W], bpk, s_b[0], s_b[1], col0, hTb, ["hTb"], W)
                acc, acck = tmpf()
                conv_taps(aps, apk, acc, acck, W, segs, 3, lambda j, kc2=kc2: prmc(P_FW + j * NFB + kc2, 1),
                          prmc(P_FB + kc2, 1),
                          lambda sidx, kc2=kc2: st[:, ST_FH + sidx * 44 + kc2 * 2:ST_FH + sidx * 44 + kc2 * 2 + 2],
                          lambda sidx, kc2=kc2: ("st_fh", sidx, kc2))
                gl, glk = tmpf()
                P.add("act", lambda e, gl=gl, acc=acc: e.activation(out=gl[:, 0:W], in_=acc[:, 0:W], func=AF.Gelu),
                      reads=[acck], writes=[glk])
                P.add("dve", lambda e, kc2=kc2, gl=gl, bps=bps: e.tensor_tensor(out=actb[:, kc2, 0:W], in0=bps[:, 0:W],
                                                                                in1=gl[:, 0:W], op=ALU.mult),
                      reads=[bpk, glk], writes=[("act", kc2)])
                yield
            release(s_a)
            release(s_b)

    def wdown_gen(ti, xs_i, W, rows, nsub, segs):
        yield from tm_proj(xs_i, rows, nsub, lambda cg: [SL_DN + 3 * cg + k for k in range(3)], NFB, actb,
                           [("act", k) for k in range(NFB)])

    def load_x(ti):
        b = ti % 2
        if ti < NT:
            P.dma(lambda e: e.dma_start(out=xt[:, b, :, :], in_=xp[ti * 512:(ti + 1) * 512, :].rearrange("(j p) d -> p j d", p=128)),
                  d_x[b], writes=[("xt", b, j) for j in range(4)])
        else:
            P.dma(lambda e: e.dma_start(out=xt[0:32, b, 0, :], in_=xs[:, :]), d_x[b], writes=[("xt", b, 0)])

    def targs(ti):
        b = ti % 2
        if ti < NT:
            return (ti, b, 512, 128, 4, [(0, 512, 0)], ti == 0, False)
        return (ti, b, 32, 32, 1, [(0, 16, 1), (16, 16, 2)], False, True)

    load_x(0)
    run(par([prepass_ffn_gen(), chain([stageA_gen(*targs(0)), front_gen(*targs(0))])]))
    for ti in range(NT + 1):
        b = ti % 2
        if ti + 1 <= NT:
            load_x(ti + 1)
        ta_ = targs(ti)
        mid(*ta_)
        if ti == 0:
            P.fence()
        gens = [ffn_up(*ta_[:6])]
        if ti + 1 <= NT:
            gens.append(stageA_gen(*targs(ti + 1)))
        run(par(gens))
        gate = {"go": False}

        def wd_then_open(g=gate, a=ta_[:6]):
            yield from wdown_gen(*a)
            g["go"] = True

        gens = [wd_then_open()]
        if ti + 1 <= NT:
            gens.append(front_gen(*targs(ti + 1), gate=None, wd_gate=gate))
        run(par(gens))
        if ti < NT:
            P.dma(lambda e, ti=ti, b=b: e.dma_start(out=yp[ti * 512:(ti + 1) * 512, :].rearrange("(j p) d -> p j d", p=128),
                                                    in_=xt[:, b, :, :]), d_y[b], reads=[("xt", b, j) for j in range(4)],
                  queue="pool")
        else:
            P.dma(lambda e, b=b: e.dma_start(out=ys[:, :], in_=xt[0:32, b, 0, :]), d_y[b], reads=[("xt", b, 0)], queue="pool")
    stkeys = (["st_hu", "st_h", "st_fh"] + [("st_hu", s, n) for s in range(3) for n in range(8)]
              + [("st_h", s, n) for s in range(3) for n in range(8)] + [("st_fh", s, k) for s in range(3) for k in range(NFB)])
    P.dma(lambda e: e.dma_start(out=st_o[:, :], in_=st[:, :]), d_fin, reads=stkeys, queue="pool")
    P.dma(lambda e: e.dma_start(out=kp_o[:, :], in_=kvo[:, 0, :]), d_fin, reads=[("kvo", 0)], queue="pool")
    P.dma(lambda e: e.dma_start(out=vp_o[:, :], in_=kvo[:, 1, :]), d_fin, reads=[("kvo", 1)], queue="pool")
    for s in range(2):
        P.dma(lambda e, s=s: e.dma_start(out=ks_o[s, 112:128, :], in_=kvs[0:16, s, 0, :]), d_fin, reads=[("kvs", s, 0)], queue="pool")
        P.dma(lambda e, s=s: e.dma_start(out=vs_o[s, 112:128, :], in_=kvs[0:16, s, 1, :]), d_fin, reads=[("kvs", s, 1)], queue="pool")
    P.finish()
    return nc, P


def _fm(v, nblk):
    v = np.asarray(v, np.float32)
    lead = v.shape[:-1]
    v = v.reshape(lead + (nblk, 128))
    return np.moveaxis(v, -1, 0)


_CACHE = {}


def kernel(x_prompt, x_sample, state_rnn_conv, state_rnn_h, cache_attn_k, cache_attn_v, state_ffn_conv,
           norm_mix_g, w_in, b_gate, rnn_conv_w, rnn_conv_b, rnn_gate_a_w, rnn_gate_a_b, rnn_gate_x_w,
           rnn_gate_x_b, rnn_lambda, q_norm_g, k_norm_g, attn_sinks, w_rnn_proj, w_attn_proj, w_out,
           norm_ffn_g, w_up, ffn_conv_w, ffn_conv_b, w_down):
    f32 = lambda a: np.ascontiguousarray(np.asarray(a, np.float32))
    x_prompt = f32(x_prompt)
    B, T, _ = x_prompt.shape
    NC = 8
    assert B == NC and x_sample.shape[0] == 2 * NC and x_sample.shape[1] == 16
    prm = np.zeros((128, NPRM), np.float32)
    prm[:, P_GM:P_GM + 8] = _fm(norm_mix_g[0], 8)
    prm[:, P_GF:P_GF + 8] = _fm(norm_ffn_g[0], 8)
    prm[:, P_CW:P_CW + 32] = _fm(rnn_conv_w[0], 8).reshape(128, 32)
    prm[:, P_CB:P_CB + 8] = _fm(rnn_conv_b[0], 8)
    prm[:, P_BA:P_BA + 8] = _fm(rnn_gate_a_b[0], 8)
    prm[:, P_BX:P_BX + 8] = _fm(rnn_gate_x_b[0], 8)
    prm[:, P_LAM:P_LAM + 8] = _fm(rnn_lambda[0], 8)
    prm[:, P_BG:P_BG + 16] = _fm(b_gate[0], 16)
    prm[:, P_FW:P_FW + 66] = _fm(ffn_conv_w[0], NFB).reshape(128, 66)
    prm[:, P_FB:P_FB + 22] = _fm(ffn_conv_b[0], NFB)
    prm[:, P_GQ] = np.asarray(q_norm_g[0], np.float32)
    prm[:, P_GK] = np.asarray(k_norm_g[0], np.float32)
    prm[:, P_SINK:P_SINK + 8] = np.broadcast_to(np.asarray(attn_sinks[0], np.float32)[None, :], (128, 8))
    key = T
    if key not in _CACHE:
        _CACHE[key] = build_program(T)[0]
    nc = _CACHE[key]
    shared = {
        "prm": prm, "w_in": f32(w_in[0]), "gate_a": f32(rnn_gate_a_w[0]), "gate_x": f32(rnn_gate_x_w[0]),
        "w_rnn_proj": f32(w_rnn_proj[0]), "w_attn_proj": f32(w_attn_proj[0]), "w_out": f32(w_out[0]),
        "w_up": f32(w_up[0]), "w_down": f32(w_down[0]),
    }
    src = np.asarray(state_rnn_conv[0], np.float32)
    sh = np.asarray(state_rnn_h[0], np.float32)
    sf = np.asarray(state_ffn_conv[0], np.float32)
    ck = np.asarray(cache_attn_k[0], np.float32).reshape(2 * NC, 128, 256)
    cv = np.asarray(cache_attn_v[0], np.float32).reshape(2 * NC, 128, 256)
    xs_all = f32(x_sample)
    in_maps = []
    for c in range(NC):
        sst = np.zeros((128, 152), np.float32)
        for s in range(2):
            q = 2 * c + s
            sst[:, s * 24:(s + 1) * 24] = np.transpose(_fm(src[q], 8), (0, 2, 1)).reshape(128, 24)
            sst[:, 48 + s * 8:48 + (s + 1) * 8] = _fm(sh[q], 8)
            sst[:, 64 + s * 44:64 + (s + 1) * 44] = np.transpose(_fm(sf[q], NFB), (0, 2, 1)).reshape(128, 44)
        m = dict(shared)
        m["xp"] = x_prompt[c]
        m["xs"] = np.ascontiguousarray(xs_all[2 * c:2 * c + 2].reshape(32, D))
        m["sst"] = sst
        m["ck"] = np.ascontiguousarray(ck[2 * c:2 * c + 2])
        m["cv"] = np.ascontiguousarray(cv[2 * c:2 * c + 2])
        in_maps.append(m)
    res = run_bass_kernel_spmd(nc, in_maps, core_ids=list(range(NC)))
    R = res.results
    y_p = np.stack([R[c]["yp"] for c in range(NC)])[:, :, :]
    y_s = np.concatenate([R[c]["ys"].reshape(2, 16, D) for c in range(NC)], axis=0)
    st = np.stack([R[c]["st_o"] for c in range(NC)])

    def unfm(a):
        return np.transpose(a, (2, 1, 0)).reshape(a.shape[2], -1)

    rc_p = np.stack([unfm(st[c][:, ST_HU:ST_HU + 24].reshape(128, 8, 3)) for c in range(NC)])[None]
    rc_s = np.stack([unfm(st[c][:, ST_HU + 24 * (1 + s):ST_HU + 24 * (2 + s)].reshape(128, 8, 3))
                     for c in range(NC) for s in range(2)])[None]
    h_p = np.stack([st[c][:, ST_H:ST_H + 8].T.reshape(-1) for c in range(NC)])[None]
    h_s = np.stack([st[c][:, ST_H + 8 * (1 + s):ST_H + 8 * (2 + s)].T.reshape(-1) for c in range(NC) for s in range(2)])[None]
    f_p = np.stack([unfm(st[c][:, ST_FH:ST_FH + 44].reshape(128, NFB, 2)) for c in range(NC)])[None]
    f_s = np.stack([unfm(st[c][:, ST_FH + 44 * (1 + s):ST_FH + 44 * (2 + s)].reshape(128, NFB, 2))
                    for c in range(NC) for s in range(2)])[None]
    k_p = np.stack([R[c]["kp"].reshape(128, 2, 128) for c in range(NC)])[None]
    v_p = np.stack([R[c]["vp"].reshape(128, 2, 128) for c in range(NC)])[None]
    k_s = np.concatenate([R[c]["ks"].reshape(2, 128, 2, 128) for c in range(NC)], axis=0)[None]
    v_s = np.concatenate([R[c]["vs"].reshape(2, 128, 2, 128) for c in range(NC)], axis=0)[None]
    outs = (y_p, y_s, rc_p, rc_s, h_p, h_s, k_p, k_s, v_p, v_s, f_p, f_s)
    return tuple(np.ascontiguousarray(o, dtype=np.float32) for o in outs)
```
